# Optimizing a Trainium2 kernel written in Bass

```python
import functools
import math
import jax
import jax.numpy as jnp
from jax import lax
import numpy as np


D_MODEL = 1024
BATCH = 1
SEQ = 16384
DEPTH = 1
DEC_BATCH = 128
DEC_SEQ = 8
PAST_LEN = 8192
PAGE_SIZE = 128

D_MIX = D_MODEL
D_ATT = D_MIX // 2
D_SSM = D_MIX - D_ATT
HEAD_DIM = 64
N_HEADS = D_ATT // HEAD_DIM
ROT_DIM = HEAD_DIM // 4
ROPE_THETA = 500000.0
DILATED_GROUPS = ((128, 1), (512, 4), (2048, 16))
WIN_MAX = 2048
Q_BLOCK = 128
SSM_GROUP = 16
N_SSM_GROUPS = D_SSM // SSM_GROUP
SSM_STATE = 64
D_FF = 2816
N_ADA = 9
EPS = 1e-6
DT_MIN = 1e-3
DT_MAX = 1e-1

kernel_name = "hymba_dilated_s5_macaron_step"


def rmsnorm(x, g):
    xf = x.astype(jnp.float32)
    y = xf * lax.rsqrt(jnp.mean(xf * xf, axis=-1, keepdims=True) + EPS)
    return (y * g.astype(jnp.float32)).astype(x.dtype)


def ada_terms(c, w_ada, b_ada):
    h = jax.nn.silu(c) @ w_ada + b_ada
    return jnp.split(h[:, None, :], N_ADA, axis=-1)


def modulate(x, shift, scale):
    return x * (1.0 + scale) + shift


def swiglu(x, w_gate, w_up, w_down):
    return (jax.nn.silu(x @ w_gate) * (x @ w_up)) @ w_down


def rope_partial(x, pos):
    half = ROT_DIM // 2
    inv = ROPE_THETA ** (-jnp.arange(half, dtype=jnp.float32) / half)
    ang = pos.astype(jnp.float32)[:, None] * inv[None, :]
    cos = jnp.cos(ang)[None, :, None, :]
    sin = jnp.sin(ang)[None, :, None, :]
    xr = x[..., :ROT_DIM].astype(jnp.float32)
    x1, x2 = xr[..., :half], xr[..., half:]
    rot = jnp.concatenate([x1 * cos - x2 * sin, x2 * cos + x1 * sin], axis=-1)
    return jnp.concatenate([rot.astype(x.dtype), x[..., ROT_DIM:]], axis=-1)


def dilated_attention(q, k_ext, v_ext, q_idx):
    qf = q.astype(jnp.float32) * (HEAD_DIM ** -0.5)
    outs, lses = [], []
    for window, dil in DILATED_GROUPS:
        offs = jnp.arange(window // dil + 1) * dil
        idx = q_idx[:, None] - offs[None, :]
        valid = idx >= 0
        idx = jnp.maximum(idx, 0)
        kg = k_ext[:, idx].astype(jnp.float32)
        vg = v_ext[:, idx].astype(jnp.float32)
        s = jnp.einsum("bqhd,bqjhd->bqhj", qf, kg)
        s = jnp.where(valid[None, :, None, :], s, -jnp.inf)
        lse = jax.nn.logsumexp(s, axis=-1)
        p = jnp.exp(s - lse[..., None])
        outs.append(jnp.einsum("bqhj,bqjhd->bqhd", p, vg))
        lses.append(lse)
    w = jax.nn.softmax(jnp.stack(lses, axis=0), axis=0)
    o = jnp.sum(w[..., None] * jnp.stack(outs, axis=0), axis=0)
    return o.astype(q.dtype)


def prompt_attend(q, k, v):
    b, s, h, hd = q.shape
    nb = s // Q_BLOCK
    qb = q.reshape(b, nb, Q_BLOCK, h, hd).transpose(1, 0, 2, 3, 4)
    starts = jnp.arange(nb) * Q_BLOCK

    def blk(args):
        qi, st = args
        return dilated_attention(qi, k, v, st + jnp.arange(Q_BLOCK))

    o = lax.map(blk, (qb, starts)).transpose(1, 0, 2, 3, 4).reshape(b, s, h, hd)
    keep = min(WIN_MAX, s)
    return o, k[:, s - keep:], v[:, s - keep:]


def buffered_attend(q, k, v, k_buf, v_buf):
    w_buf = k_buf.shape[1]
    k_ext = jnp.concatenate([k_buf.astype(k.dtype), k], axis=1)
    v_ext = jnp.concatenate([v_buf.astype(v.dtype), v], axis=1)
    o = dilated_attention(q, k_ext, v_ext, w_buf + jnp.arange(q.shape[1]))
    return o, k_ext[:, -w_buf:], v_ext[:, -w_buf:]


def s5_mixer(u, h0_re, h0_im, a_re, a_im, log_dt, b_re, b_im, c_re, c_im, d_skip):
    b, s, _ = u.shape
    uf = u.astype(jnp.float32).reshape(b, s, N_SSM_GROUPS, SSM_GROUP)
    lam = lax.complex(a_re.astype(jnp.float32), a_im.astype(jnp.float32))
    dt = jnp.exp(log_dt.astype(jnp.float32))[:, None]
    lam_bar = jnp.exp(lam * dt)
    bmat = lax.complex(b_re.astype(jnp.float32), b_im.astype(jnp.float32))
    b_bar = ((lam_bar - 1.0) / lam)[..., None] * bmat
    cmat = lax.complex(c_re.astype(jnp.float32), c_im.astype(jnp.float32))
    bu = jnp.einsum("gpc,bsgc->bsgp", b_bar, uf.astype(jnp.complex64))
    h0 = lax.complex(h0_re.astype(jnp.float32), h0_im.astype(jnp.float32))
    bu = bu.at[:, 0].add(lam_bar[None] * h0)
    a = jnp.broadcast_to(lam_bar, bu.shape)

    def combine(left, right):
        return (left[0] * right[0], right[0] * left[1] + right[1])

    _, h = lax.associative_scan(combine, (a, bu), axis=1)
    y = jnp.einsum("gcp,bsgp->bsgc", cmat, h).real \
        + d_skip.astype(jnp.float32).reshape(N_SSM_GROUPS, SSM_GROUP) * uf
    h_last = h[:, -1]
    return y.reshape(b, s, D_SSM).astype(u.dtype), jnp.real(h_last), jnp.imag(h_last)


def decoder_layer(x, c, pos, attend, h0_re, h0_im, p):
    (w_ada, b_ada, g_ffn1, w1_gate, w1_up, w1_down, g_mix, w_in, g_q, g_k,
     a_re, a_im, log_dt, b_re, b_im, c_re, c_im, d_skip, w_glu, w_out,
     g_ffn2, w2_gate, w2_up, w2_down) = p
    sh1, sc1, gt1, sh2, sc2, gt2, sh3, sc3, gt3 = ada_terms(c, w_ada, b_ada)
    b, s, _ = x.shape
    x = x + 0.5 * gt1 * swiglu(modulate(rmsnorm(x, g_ffn1), sh1, sc1), w1_gate, w1_up, w1_down)
    h = modulate(rmsnorm(x, g_mix), sh2, sc2)
    proj = h @ w_in
    q, k, v, u = jnp.split(proj, [D_ATT, 2 * D_ATT, 3 * D_ATT], axis=-1)
    q = rope_partial(rmsnorm(q.reshape(b, s, N_HEADS, HEAD_DIM), g_q), pos)
    k = rope_partial(rmsnorm(k.reshape(b, s, N_HEADS, HEAD_DIM), g_k), pos)
    v = v.reshape(b, s, N_HEADS, HEAD_DIM)
    o_att, k_state, v_state = attend(q, k, v)
    y_ssm, h_re, h_im = s5_mixer(u, h0_re, h0_im, a_re, a_im, log_dt, b_re, b_im, c_re, c_im, d_skip)
    g = jax.nn.gelu(y_ssm)
    ga, gb = jnp.split(g @ w_glu, 2, axis=-1)
    o_ssm = ga * jax.nn.sigmoid(gb)
    mix = jnp.concatenate([o_att.reshape(b, s, D_ATT), o_ssm], axis=-1) @ w_out
    x = x + gt2 * mix
    x = x + 0.5 * gt3 * swiglu(modulate(rmsnorm(x, g_ffn2), sh3, sc3), w2_gate, w2_up, w2_down)
    return x, k_state, v_state, h_re, h_im


def setup_inputs(seed: int = 0) -> dict:
    key = jax.random.key(seed)
    ks = jax.random.split(key, 40)
    f32 = jnp.float32
    nrm = lambda k, shape, sc: jax.random.normal(k, shape, f32) * sc
    w_buf = min(WIN_MAX, PAST_LEN)
    L = DEPTH
    n_idx = jnp.arange(SSM_STATE, dtype=f32)
    a_im = jnp.broadcast_to(math.pi * n_idx, (L, N_SSM_GROUPS, SSM_STATE)) \
        + nrm(ks[20], (L, N_SSM_GROUPS, SSM_STATE), 0.01)
    return {
        "x_prompt": nrm(ks[0], (BATCH, SEQ, D_MODEL), 1.0),
        "x_sample": nrm(ks[1], (DEC_BATCH, DEC_SEQ, D_MODEL), 1.0),
        "c_prompt": nrm(ks[2], (BATCH, D_MODEL), 1.0),
        "c_sample": nrm(ks[3], (DEC_BATCH, D_MODEL), 1.0),
        "cache_k_win": nrm(ks[4], (L, DEC_BATCH, w_buf, N_HEADS, HEAD_DIM), 1.0),
        "cache_v_win": nrm(ks[5], (L, DEC_BATCH, w_buf, N_HEADS, HEAD_DIM), 1.0),
        "state_ssm_re": nrm(ks[6], (L, DEC_BATCH, N_SSM_GROUPS, SSM_STATE), 0.1),
        "state_ssm_im": nrm(ks[7], (L, DEC_BATCH, N_SSM_GROUPS, SSM_STATE), 0.1),
        "w_ada": nrm(ks[8], (L, D_MODEL, N_ADA * D_MODEL), 0.5 * D_MODEL ** -0.5),
        "b_ada": nrm(ks[9], (L, N_ADA * D_MODEL), 0.02),
        "g_ffn1": 1.0 + nrm(ks[10], (L, D_MODEL), 0.02),
        "w1_gate": nrm(ks[11], (L, D_MODEL, D_FF), D_MODEL ** -0.5),
        "w1_up": nrm(ks[12], (L, D_MODEL, D_FF), D_MODEL ** -0.5),
        "w1_down": nrm(ks[13], (L, D_FF, D_MODEL), D_FF ** -0.5),
        "g_mix": 1.0 + nrm(ks[14], (L, D_MODEL), 0.02),
        "w_in": nrm(ks[15], (L, D_MODEL, 3 * D_ATT + D_SSM), D_MODEL ** -0.5),
        "g_q": 1.0 + nrm(ks[16], (L, HEAD_DIM), 0.02),
        "g_k": 1.0 + nrm(ks[17], (L, HEAD_DIM), 0.02),
        "ssm_a_re": -0.5 + nrm(ks[18], (L, N_SSM_GROUPS, SSM_STATE), 0.01),
        "ssm_a_im": a_im,
        "ssm_log_dt": jax.random.uniform(ks[19], (L, N_SSM_GROUPS), f32,
                                         math.log(DT_MIN), math.log(DT_MAX)),
        "ssm_b_re": nrm(ks[21], (L, N_SSM_GROUPS, SSM_STATE, SSM_GROUP), (2.0 * SSM_GROUP) ** -0.5),
        "ssm_b_im": nrm(ks[22], (L, N_SSM_GROUPS, SSM_STATE, SSM_GROUP), (2.0 * SSM_GROUP) ** -0.5),
        "ssm_c_re": nrm(ks[23], (L, N_SSM_GROUPS, SSM_GROUP, SSM_STATE), (2.0 * SSM_STATE) ** -0.5),
        "ssm_c_im": nrm(ks[24], (L, N_SSM_GROUPS, SSM_GROUP, SSM_STATE), (2.0 * SSM_STATE) ** -0.5),
        "ssm_d": nrm(ks[25], (L, D_SSM), 1.0),
        "w_glu": nrm(ks[26], (L, D_SSM, 2 * D_SSM), D_SSM ** -0.5),
        "w_out": nrm(ks[27], (L, D_MIX, D_MODEL), D_MIX ** -0.5),
        "g_ffn2": 1.0 + nrm(ks[28], (L, D_MODEL), 0.02),
        "w2_gate": nrm(ks[29], (L, D_MODEL, D_FF), D_MODEL ** -0.5),
        "w2_up": nrm(ks[30], (L, D_MODEL, D_FF), D_MODEL ** -0.5),
        "w2_down": nrm(ks[31], (L, D_FF, D_MODEL), D_FF ** -0.5),
    }


def reference(x_prompt, x_sample, c_prompt, c_sample, cache_k_win, cache_v_win,
              state_ssm_re, state_ssm_im, w_ada, b_ada, g_ffn1, w1_gate, w1_up, w1_down,
              g_mix, w_in, g_q, g_k, ssm_a_re, ssm_a_im, ssm_log_dt, ssm_b_re, ssm_b_im,
              ssm_c_re, ssm_c_im, ssm_d, w_glu, w_out, g_ffn2, w2_gate, w2_up, w2_down):
    pos_p = jnp.arange(x_prompt.shape[1])
    pos_s = PAST_LEN + jnp.arange(x_sample.shape[1])
    h0_p = jnp.zeros((x_prompt.shape[0], N_SSM_GROUPS, SSM_STATE), jnp.float32)
    y_p, y_s = x_prompt, x_sample
    kp_l, vp_l, hrp_l, hip_l, ks_l, vs_l, hrs_l, his_l = [], [], [], [], [], [], [], []
    for l in range(DEPTH):
        p = (w_ada[l], b_ada[l], g_ffn1[l], w1_gate[l], w1_up[l], w1_down[l], g_mix[l], w_in[l],
             g_q[l], g_k[l], ssm_a_re[l], ssm_a_im[l], ssm_log_dt[l], ssm_b_re[l], ssm_b_im[l],
             ssm_c_re[l], ssm_c_im[l], ssm_d[l], w_glu[l], w_out[l], g_ffn2[l], w2_gate[l],
             w2_up[l], w2_down[l])
        y_p, kp, vp, hrp, hip = decoder_layer(y_p, c_prompt, pos_p, prompt_attend, h0_p, h0_p, p)
        sample_attend = functools.partial(buffered_attend, k_buf=cache_k_win[l], v_buf=cache_v_win[l])
        y_s, ksm, vsm, hrs, his = decoder_layer(y_s, c_sample, pos_s, sample_attend,
                                                state_ssm_re[l], state_ssm_im[l], p)
        kp_l.append(kp); vp_l.append(vp); hrp_l.append(hrp); hip_l.append(hip)
        ks_l.append(ksm); vs_l.append(vsm); hrs_l.append(hrs); his_l.append(his)
    return (y_p, y_s,
            jnp.stack(kp_l, 0), jnp.stack(vp_l, 0), jnp.stack(hrp_l, 0), jnp.stack(hip_l, 0),
            jnp.stack(ks_l, 0), jnp.stack(vs_l, 0), jnp.stack(hrs_l, 0), jnp.stack(his_l, 0))
```

```python
import math
from contextlib import ExitStack

import numpy as np
import concourse.bass as bass
import concourse.mybir as mybir
from concourse.bass_utils import run_bass_kernel_spmd

F32 = mybir.dt.float32
BF16 = mybir.dt.bfloat16
AF = mybir.ActivationFunctionType
ALU = mybir.AluOpType

NCORES = 8
D = 1024
DFF = 2816
NFF = DFF // 128
SEQ = 16384
TOK_CORE = SEQ // NCORES
NT_P = TOK_CORE // 128
NT = NT_P + 1
SB = 16
WBUF = 2048
NH = 8
HD = 64
EPS = 1e-6
PAST = 8192
ENG = ("pe", "act", "dve", "pool", "sp")


class T:
    registry = []

    def __init__(self, name):
        T.registry.append(self)
        self.name = name
        self.w = None
        self.r = []
        self.dkey = None
        self.dcnt = 0


class Sched:
    def __init__(self):
        self.prog = {e: [] for e in ENG}
        self.cnt = {e: 0 for e in ENG}
        self.seen = {e: {} for e in ENG}
        self.sems = {}
        self.dsem_names = []
        self.final = []

    def _wait(self, eng, dep):
        if dep is None:
            return
        key, val = dep
        if self.seen[eng].get(key, 0) >= val:
            return
        self.seen[eng][key] = val
        self.prog[eng].append(lambda E, key=key, val=val: E.wait_ge(self.sems[key], val))

    def _deps(self, eng, reads, writes):
        for t in reads:
            self._wait(eng, t.w)
        for t in writes:
            self._wait(eng, t.w)
            for d in t.r:
                self._wait(eng, d)

    def _commit(self, dep, reads, writes):
        for t in reads:
            t.r.append(dep)
        for t in writes:
            t.w = dep
            t.r = []

    def op(self, eng, fn, reads=(), writes=()):
        self._deps(eng, reads, writes)
        self.cnt[eng] += 1
        dep = (eng, self.cnt[eng])
        self.prog[eng].append(lambda E, fn=fn, eng=eng: fn(E).then_inc(self.sems[eng], 1))
        self._commit(dep, reads, writes)

    def mm(self, fns, reads=(), writes=()):
        self._deps("pe", reads, writes)
        for fn in fns[:-1]:
            self.prog["pe"].append(lambda E, fn=fn: fn(E))
        self.cnt["pe"] += 1
        dep = ("pe", self.cnt["pe"])
        last = fns[-1]
        self.prog["pe"].append(lambda E, fn=last: fn(E).then_inc(self.sems["pe"], 1))
        self._commit(dep, reads, writes)

    def I(self, eng, name, reads, writes, *args, **kw):
        self.op(eng, lambda E: getattr(E, name)(*args, **kw), reads, writes)

    def MM(self, items, reads, writes):
        self.mm([lambda E, n=n, a=a, k=k: getattr(E, n)(*a, **k) for (n, a, k) in items], reads, writes)

    def dma(self, out_ap, in_ap, reads=(), writes=(), sem_t=None, slow=False):
        eng = "sp"
        self._deps(eng, reads, writes)
        st = sem_t if sem_t is not None else (writes[0] if writes else reads[0])
        if st.dkey is None:
            st.dkey = "d_" + st.name
            self.dsem_names.append(st.dkey)
        st.dcnt += 1
        dep = (st.dkey, 16 * st.dcnt)
        key = st.dkey
        if slow:
            self.prog[eng].append(lambda E, o=out_ap, i=in_ap, key=key: E.dma_start(
                out=o, in_=i, allow_slow_non_contiguous=True).then_inc(self.sems[key], 16))
        else:
            self.prog[eng].append(lambda E, o=out_ap, i=in_ap, key=key: E.dma_start(
                out=o, in_=i).then_inc(self.sems[key], 16))
        self._commit(dep, reads, writes)
        return dep


BT = 256
NTB = BT // 128
NSC = BT // 8
NBLK = 64
OWN0 = 56
HALO0 = 48
NSP = 33
PI = math.pi


import os
STOP = int(os.environ.get('KSTOP', '99'))


def build_nc():
    nc = bass.Bass("TRN2", target_bir_lowering=False)
    S = Sched()
    es = ExitStack()

    def din(name, shape, dt=F32):
        return nc.dram_tensor(name, list(shape), dt, kind="ExternalInput").ap()

    def dout(name, shape, dt=F32):
        return nc.dram_tensor(name, list(shape), dt, kind="ExternalOutput").ap()

    def sb(name, shape, dt=F32):
        return es.enter_context(nc.sbuf_tensor(name, list(shape), dt))

    def ps(name, shape, dt=F32):
        return es.enter_context(nc.psum_tensor(name, list(shape), dt))

    x_seq = din("x_seq", [NBLK * BT, D])
    x_smp = din("x_smp", [128, D])
    blkvalid_in = din("blkvalid", [128, NBLK])
    cT_in = din("cT", [128, 8, 17])
    bada_row = din("bada_row", [1, 9 * D])
    badaT = din("badaT", [128, 72])
    gT_in = din("gT", [128, 3, 8])
    ident_in = din("ident", [128, 128])
    esel_in = din("esel", [17, 2, 128])
    w_ada = din("w_ada", [D, 9 * D])
    w1g = din("w1_gate", [D, DFF])
    w1u = din("w1_up", [D, DFF])
    w1d = din("w1_down", [DFF, D])
    w_in = din("w_in", [D, 2048])
    ck = din("cache_k", [SB, WBUF, 512])
    cv = din("cache_v", [SB, WBUF, 512])
    ssm_ar = din("ssm_ar", [128, 16])
    ssm_ai = din("ssm_ai", [128, 16])
    ssm_ldt = din("ssm_ldt", [128, 16])
    ssm_bre = din("ssm_bre", [128, 16, 32])
    ssm_bim = din("ssm_bim", [128, 16, 32])
    ssm_dT = din("ssm_dT", [128, 4])
    h0re_in = din("h0re", [128, 16, SB])
    h0im_in = din("h0im", [128, 16, SB])

    o_ks = dout("o_ks", [SB, WBUF, 512])
    o_vs = dout("o_vs", [SB, WBUF, 512])
    o_hp = dout("o_hp", [128, 2, 16])
    o_hs = dout("o_hs", [128, 2, 16, SB])
    x1s = nc.dram_tensor("x1s", [NSP * 128, D], F32).ap()
    uTs = nc.dram_tensor("uTs", [128, 4, TOK_CORE + 128], BF16).ap()
    xss = nc.dram_tensor("xss", [128, 2, 16, TOK_CORE // 8], F32).ap()

    ident_f = sb("ident_f", [128, 128])
    ident_b = sb("ident_b", [128, 128], BF16)
    esel_f = sb("esel_f", [17, 2, 128])
    cT = sb("cT_sb", [128, 8, 17])
    sT = sb("sT_sb", [128, 8, 17], BF16)
    sig = sb("sig_sb", [128, 8, 17])
    badaT_sb = sb("badaT_sb", [128, 72])
    gT = sb("gT_sb", [128, 3, 8])
    blkvalid = sb("blkvalid_sb", [128, NBLK])
    gtab = sb("gtab", [128, 1, 2, D])
    AB = sb("AB", [128, 3, 2, 8, 17])
    eps_t = sb("eps_t", [128, 1])
    ARN = 3 * 8 * DFF
    arena = sb("arena", [128, ARN], BF16)
    wg = arena[:, 0:8 * DFF].rearrange("p (k n) -> p k n", k=8)
    wu = arena[:, 8 * DFF:16 * DFF].rearrange("p (k n) -> p k n", k=8)
    wd = arena[:, 16 * DFF:24 * DFF].rearrange("p (c n) -> p c n", c=NFF)
    wuin = sb("wuin", [128, 8, 512], BF16)
    wada_b = wd[:, 0:4, :].rearrange("p c (a n) -> p (c a) n", a=2)
    xt = sb("xt", [128, NTB, D])
    tmp = sb("tmp", [128, D])
    bada_blk = tmp[0:17, 0:512]
    grow = tmp[0:17, 512:1024]
    xn = sb("xn", [128, D], BF16)
    ssq = sb("ssq", [128, 4])
    rstd = sb("rstd", [128, 8])
    hT = sb("hT", [128, 8, BT], BF16)
    actT = sb("actT", [128, NFF, BT], BF16)
    silu_t = sb("silu_t", [128, BT], BF16)
    uT = sb("uT", [128, 4, BT], BF16)
    PR = sb("PR", [128, 18, 16])
    PRI = sb("PRI", [128, 16], mybir.dt.int32)
    LP = sb("LP", [128, 16, 9, 3])
    bb = xt[:, 0, :].rearrange("p (r q n) -> p r q n", r=2, q=16)
    bex = xt[:, 1, :].rearrange("p (r q n) -> p r q n", r=2, q=16)
    wexp = actT[:, 0:16, :].rearrange("p a (b n) -> p (a b) n", n=128).rearrange("p (r q) n -> p r q n", r=2)
    lhsB = sb("lhsB", [128, 16, 2, 128], BF16)
    dT = sb("dT_sb", [128, 4])
    BU = [sb(f"BU{i}", [128, 2, BT]) for i in range(2)]
    wst = [BU[i][:].rearrange("p r n -> p (r n)") for i in range(2)]
    WCH = 2 * BT
    XS = sb("XS", [128, 2, 16, NSC + 1])
    t4 = sb("t4", [128, 4, 16])
    hend = tmp[:, 0:512].rearrange("p (r q b) -> p r q b", r=2, q=16)
    h0 = tmp[:, 512:1024].rearrange("p (r q b) -> p r q b", r=2, q=16)

    p_a = ps("p_a", [128, 512])
    p_b = ps("p_b", [128, 512])
    p_c = ps("p_c", [128, 512])
    p_d = ps("p_d", [128, 512])
    p_t = ps("p_t", [128, 1024], BF16)
    p_s1 = ps("p_s1", [128, 512])
    p_s2 = ps("p_s2", [128, 512])

    t_const = T("const")
    t_wg, t_wu, t_wd, t_wuin = T("wg"), T("wu"), T("wd"), T("wuin")
    t_wada = t_wd
    t_gtab, t_AB = T("gtab"), T("AB")
    t_pa, t_pb, t_pc, t_pd, t_pt, t_ps1, t_ps2 = T("pa"), T("pb"), T("pc"), T("pd"), T("pt"), T("ps1"), T("ps2")
    t_xt, t_xn, t_ssq, t_rstd, t_tmp = T("xt"), T("xn"), T("ssq"), T("rstd"), T("tmp")
    t_grow = t_bblk = t_tmp
    t_hT, t_act, t_silu, t_uT = T("hT"), T("act"), T("silu"), T("uT")
    t_cache = T("cache")
    t_x1s = T("x1s")
    t_PR, t_LP, t_lhsB = T("PR"), T("LP"), T("lhsB")
    t_bb = t_xt
    t_wexp = t_act
    t_BU = [T("BU0"), T("BU1")]
    t_wst = t_BU
    t_XS, t_t4 = T("XS"), T("t4")
    t_h0 = t_hend = t_tmp

    for b in range(SB):
        for (src, dst) in ((ck, o_ks), (cv, o_vs)):
            S.dma(dst[b, 0:WBUF - 8, :], src[b, 8:WBUF, :], sem_t=t_cache)

    for (dst, srcap) in ((ident_f, ident_in), (esel_f, esel_in), (cT, cT_in), (badaT_sb, badaT),
                         (gT, gT_in), (blkvalid, blkvalid_in), (dT, ssm_dT)):
        S.dma(dst[:], srcap, writes=(t_const,))
    S.dma(PR[:, 0, :], ssm_ar, writes=(t_PR,))
    S.dma(PR[:, 1, :], ssm_ai, writes=(t_PR,))
    S.dma(PR[:, 2, :], ssm_ldt, writes=(t_PR,))
    S.dma(bex[:, 0], ssm_bre, writes=(t_bb,))
    S.dma(bex[:, 1], ssm_bim, writes=(t_bb,))
    S.I("dve", "memset", (), (t_const,), eps_t[:], EPS)
    S.I("dve", "tensor_copy", (t_const,), (t_const,), out=ident_b[:], in_=ident_f[:])
    S.I("act", "activation", (t_const,), (t_const,), out=sig[:], in_=cT[:], func=AF.Sigmoid)
    S.I("dve", "tensor_tensor", (t_const,), (t_const,), out=sT[:], in0=sig[:], in1=cT[:], op=ALU.mult)

    def P(i):
        return PR[:, i, :]

    def pr(name, *a, **k):
        S.I("dve", name, (t_PR,), (t_PR,), *a, **k)

    AR, AI, LDT, DT_, MAG, ANG, R1, R2, SN, CS, LR, LI = range(12)
    S.I("act", "activation", (t_PR,), (t_PR,), out=P(DT_), in_=P(LDT), func=AF.Exp)
    pr("tensor_tensor", out=P(MAG), in0=P(AR), in1=P(DT_), op=ALU.mult)
    S.I("act", "activation", (t_PR,), (t_PR,), out=P(MAG), in_=P(MAG), func=AF.Exp)
    pr("tensor_tensor", out=P(ANG), in0=P(AI), in1=P(DT_), op=ALU.mult)
    T1, T2 = 12, 13
    for (dst_, off_) in ((R1, 10.0 * PI), (R2, 10.5 * PI)):
        pr("tensor_scalar", out=P(T1), in0=P(ANG), scalar1=off_, scalar2=1.0 / (2.0 * PI), op0=ALU.add, op1=ALU.mult)
        S.I("dve", "tensor_copy", (t_PR,), (t_PR,), out=PRI[:], in_=P(T1))
        S.I("dve", "tensor_copy", (t_PR,), (t_PR,), out=P(T2), in_=PRI[:])
        pr("tensor_scalar", out=P(dst_), in0=P(ANG), scalar1=off_, scalar2=None, op0=ALU.add)
        pr("scalar_tensor_tensor", out=P(dst_), in0=P(T2), scalar=-2.0 * PI, in1=P(dst_), op0=ALU.mult, op1=ALU.add)
        pr("tensor_scalar", out=P(T1), in0=P(dst_), scalar1=PI, scalar2=-2.0 * PI, op0=ALU.is_gt, op1=ALU.mult)
        pr("tensor_tensor", out=P(dst_), in0=P(dst_), in1=P(T1), op=ALU.add)
        pr("tensor_scalar", out=P(dst_), in0=P(dst_), scalar1=-PI, scalar2=PI, op0=ALU.max, op1=ALU.min)
    S.I("act", "activation", (t_PR,), (t_PR,), out=P(SN), in_=P(R1), func=AF.Sin)
    S.I("act", "activation", (t_PR,), (t_PR,), out=P(CS), in_=P(R2), func=AF.Sin)
    pr("tensor_tensor", out=P(LR), in0=P(MAG), in1=P(CS), op=ALU.mult)
    pr("tensor_tensor", out=P(LI), in0=P(MAG), in1=P(SN), op=ALU.mult)
    S.I("dve", "memset", (), (t_LP,), LP[:, :, 0, 0], 1.0)
    S.I("dve", "memset", (t_LP,), (t_LP,), LP[:, :, 0, 1:3], 0.0)
    S.I("dve", "tensor_copy", (t_PR, t_LP), (t_LP,), out=LP[:, :, 1, 0], in_=P(LR))
    S.I("dve", "tensor_copy", (t_PR, t_LP), (t_LP,), out=LP[:, :, 1, 1], in_=P(LI))
    T1, T2 = 12, 13
    for k in range(2, 9):
        S.I("dve", "tensor_tensor", (t_LP, t_PR), (t_PR,), out=P(T1), in0=LP[:, :, k - 1, 0], in1=P(LR), op=ALU.mult)
        S.I("dve", "tensor_tensor", (t_LP, t_PR), (t_PR,), out=P(T2), in0=LP[:, :, k - 1, 1], in1=P(LI), op=ALU.mult)
        S.I("dve", "tensor_tensor", (t_PR, t_LP), (t_LP,), out=LP[:, :, k, 0], in0=P(T1), in1=P(T2), op=ALU.subtract)
        S.I("dve", "tensor_tensor", (t_LP, t_PR), (t_PR,), out=P(T1), in0=LP[:, :, k - 1, 0], in1=P(LI), op=ALU.mult)
        S.I("dve", "tensor_tensor", (t_LP, t_PR), (t_PR,), out=P(T2), in0=LP[:, :, k - 1, 1], in1=P(LR), op=ALU.mult)
        S.I("dve", "tensor_tensor", (t_PR, t_LP), (t_LP,), out=LP[:, :, k, 1], in0=P(T1), in1=P(T2), op=ALU.add)
    S.I("dve", "tensor_scalar", (t_LP,), (t_LP,), out=LP[:, :, 1:9, 2], in0=LP[:, :, 1:9, 1], scalar1=-1.0,
        scalar2=None, op0=ALU.mult)
    NR, DEN, CR, CI = 14, 15, 16, 17
    pr("tensor_scalar", out=P(NR), in0=P(LR), scalar1=-1.0, scalar2=None, op0=ALU.add)
    pr("tensor_tensor", out=P(T1), in0=P(AR), in1=P(AR), op=ALU.mult)
    pr("tensor_tensor", out=P(T2), in0=P(AI), in1=P(AI), op=ALU.mult)
    pr("tensor_tensor", out=P(DEN), in0=P(T1), in1=P(T2), op=ALU.add)
    pr("reciprocal", out=P(DEN), in_=P(DEN))
    pr("tensor_tensor", out=P(T1), in0=P(NR), in1=P(AR), op=ALU.mult)
    pr("tensor_tensor", out=P(T2), in0=P(LI), in1=P(AI), op=ALU.mult)
    pr("tensor_tensor", out=P(CR), in0=P(T1), in1=P(T2), op=ALU.add)
    pr("tensor_tensor", out=P(CR), in0=P(CR), in1=P(DEN), op=ALU.mult)
    pr("tensor_tensor", out=P(T1), in0=P(LI), in1=P(AR), op=ALU.mult)
    pr("tensor_tensor", out=P(T2), in0=P(NR), in1=P(AI), op=ALU.mult)
    pr("tensor_tensor", out=P(CI), in0=P(T1), in1=P(T2), op=ALU.subtract)
    pr("tensor_tensor", out=P(CI), in0=P(CI), in1=P(DEN), op=ALU.mult)
    crb = P(CR).unsqueeze(2).to_broadcast((128, 16, 32))
    cib = P(CI).unsqueeze(2).to_broadcast((128, 16, 32))
    S.I("dve", "tensor_tensor", (t_PR, t_bb), (t_bb,), out=bb[:, 0], in0=bex[:, 0], in1=crb, op=ALU.mult)
    S.I("dve", "tensor_tensor", (t_PR, t_bb), (t_bb,), out=bb[:, 1], in0=bex[:, 1], in1=crb, op=ALU.mult)
    S.I("dve", "tensor_tensor", (t_PR, t_bb), (t_bb,), out=bex[:, 0], in0=bex[:, 0], in1=cib, op=ALU.mult)
    S.I("dve", "tensor_tensor", (t_PR, t_bb), (t_bb,), out=bex[:, 1], in0=bex[:, 1], in1=cib, op=ALU.mult)
    S.I("dve", "tensor_tensor", (t_bb,), (t_bb,), out=bb[:, 0], in0=bb[:, 0], in1=bex[:, 1], op=ALU.subtract)
    S.I("dve", "tensor_tensor", (t_bb,), (t_bb,), out=bb[:, 1], in0=bb[:, 1], in1=bex[:, 0], op=ALU.add)
    LPB = PR[:, 0:12, :].rearrange("p (m r) n -> p m r n", r=2)
    S.I("dve", "tensor_copy", (t_LP,), (t_LP,), out=LPB[:, 0, 0, :], in_=LP[:, :, 8, 0])
    S.I("dve", "tensor_copy", (t_LP,), (t_LP,), out=LPB[:, 0, 1, :], in_=LP[:, :, 8, 1])
    for m in range(1, 6):
        a_, b_ = LPB[:, m - 1, 0, :], LPB[:, m - 1, 1, :]
        S.I("dve", "tensor_tensor", (t_LP, t_PR), (t_PR,), out=P(T1), in0=a_, in1=a_, op=ALU.mult)
        S.I("dve", "tensor_tensor", (t_LP, t_PR), (t_PR,), out=P(T2), in0=b_, in1=b_, op=ALU.mult)
        S.I("dve", "tensor_tensor", (t_PR, t_LP), (t_LP,), out=LPB[:, m, 0, :], in0=P(T1), in1=P(T2), op=ALU.subtract)
        S.I("dve", "tensor_tensor", (t_LP, t_PR), (t_PR,), out=P(T1), in0=a_, in1=b_, op=ALU.mult)
        S.I("dve", "tensor_scalar", (t_PR, t_LP), (t_LP,), out=LPB[:, m, 1, :], in0=P(T1), scalar1=2.0, scalar2=None,
            op0=ALU.mult)
    S.I("dve", "memset", (), (t_wexp,), wexp[:], 0.0)
    for ri in range(2):
        for s in range(4):
            S.I("dve", "tensor_copy", (t_bb, t_wexp), (t_wexp,), out=wexp[:, ri, s::4, 32 * s:32 * s + 32],
                in_=bb[:, ri, s::4, :])
    for q in range(16):
        S.MM([("transpose", (p_t[:, ri * 128:(ri + 1) * 128], wexp[:, ri, q, :], ident_b[:]), {}) for ri in range(2)],
             (t_wexp, t_const), (t_pt,))
        S.I("act", "activation", (t_pt,), (t_lhsB,), out=lhsB[:, q, :, :],
            in_=p_t[:, 0:256].rearrange("p (r n) -> p r n", r=2), func=AF.Copy)
    S.I("dve", "memset", (), (t_XS,), XS[:], 0.0)

    wcount = [0]

    def load_cast(dst_ap, src_ap, ncols, t_dst):
        for c0 in range(0, ncols, WCH):
            c1 = min(ncols, c0 + WCH)
            i = wcount[0] % 2
            wcount[0] += 1
            S.dma(wst[i][:, 0:c1 - c0], src_ap[:, c0:c1], writes=(t_wst[i],))
            eng3 = (wcount[0] - 1) % 3
            if eng3 == 0:
                S.I("pool", "tensor_copy", (t_wst[i],), (t_dst,), out=dst_ap[:, c0:c1], in_=wst[i][:, 0:c1 - c0])
            elif eng3 == 1:
                S.I("act", "activation", (t_wst[i],), (t_dst,), out=dst_ap[:, c0:c1], in_=wst[i][:, 0:c1 - c0],
                    func=AF.Copy)
            else:
                S.I("dve", "tensor_copy", (t_wst[i],), (t_dst,), out=dst_ap[:, c0:c1], in_=wst[i][:, 0:c1 - c0])

    gate_blocks = {4: (0, 0), 5: (0, 1)}
    for blk in range(18):
        for k in range(8):
            load_cast(wada_b[:, k, :], w_ada[k * 128:(k + 1) * 128, blk * 512:(blk + 1) * 512], 512, t_wada)
        for cc in range(4):
            ch = blk * 4 + cc
            S.MM([("matmul", (p_a[:, cc * 32:cc * 32 + 17],),
                   dict(lhsT=wada_b[:, k, cc * 128:(cc + 1) * 128], rhs=sT[:, k, :], start=(k == 0), stop=(k == 7)))
                  for k in range(8)], (t_wada, t_const), (t_pa,))
            term, kk = ch // 8, ch % 8
            li_, kind = term // 3, term % 3
            if kind == 0:
                S.I("dve", "tensor_scalar", (t_pa, t_const), (t_AB,), out=AB[:, li_, 1, kk, :],
                    in0=p_a[:, cc * 32:cc * 32 + 17], scalar1=badaT_sb[:, ch:ch + 1], scalar2=None, op0=ALU.add)
            elif kind == 1:
                S.I("dve", "tensor_scalar", (t_pa, t_const), (t_AB,), out=AB[:, li_, 0, kk, :],
                    in0=p_a[:, cc * 32:cc * 32 + 17], scalar1=badaT_sb[:, ch:ch + 1], scalar2=1.0,
                    op0=ALU.add, op1=ALU.add)
                S.I("dve", "tensor_scalar", (t_AB, t_const), (t_AB,), out=AB[:, li_, 0, kk, :],
                    in0=AB[:, li_, 0, kk, :], scalar1=gT[:, li_, kk:kk + 1], scalar2=None, op0=ALU.mult)
        if blk in gate_blocks:
            li, half = gate_blocks[blk]
            S.dma(bada_blk[:], bada_row[0:1, blk * 512:(blk + 1) * 512].to_broadcast((17, 512)), writes=(t_bblk,))
            S.MM([("matmul", (p_b[0:17, :],), dict(lhsT=sT[:, k, :], rhs=wada_b[:, k, :], start=(k == 0), stop=(k == 7)))
                  for k in range(8)], (t_wada, t_const), (t_pb,))
            S.I("dve", "tensor_tensor", (t_pb, t_bblk), (t_grow,), out=grow[:], in0=p_b[0:17, :], in1=bada_blk[:],
                op=ALU.add)
            for which in range(2):
                S.MM([("matmul", (p_c[:, :],), dict(lhsT=esel_f[:, which, :], rhs=grow[:], start=True, stop=True))],
                     (t_grow, t_const), (t_pc,))
                S.I("act", "activation", (t_pc,), (t_gtab,), out=gtab[:, li, which, half * 512:(half + 1) * 512],
                    in_=p_c[:, :], func=AF.Copy)
    for k in range(8):
        load_cast(wg[:, k, :], w1g[k * 128:(k + 1) * 128, :], DFF, t_wg)
        load_cast(wu[:, k, :], w1u[k * 128:(k + 1) * 128, :], DFF, t_wu)
        load_cast(wuin[:, k, :], w_in[k * 128:(k + 1) * 128, 1536:2048], 512, t_wuin)
    for c in range(NFF):
        load_cast(wd[:, c, :], w1d[c * 128:(c + 1) * 128, :], D, t_wd)

    def norm_mod_T(nt, li, is_sample):
        for t in range(nt):
            S.I("act", "activation", (t_xt,), (t_xn, t_ssq), out=xn[:], in_=xt[:, t, :], func=AF.Square,
                accum_out=ssq[:, t:t + 1])
            S.I("act", "activation", (t_ssq, t_const), (t_rstd,), out=rstd[:, t:t + 1], in_=ssq[:, t:t + 1],
                func=AF.Sqrt, scale=1.0 / D, bias=eps_t[:, 0:1])
            S.I("dve", "reciprocal", (t_rstd,), (t_rstd,), out=rstd[:, 4 + t:5 + t], in_=rstd[:, t:t + 1])
            S.I("dve", "tensor_scalar", (t_xt, t_rstd), (t_xn,), out=xn[:], in0=xt[:, t, :],
                scalar1=rstd[:, 4 + t:5 + t], scalar2=None, op0=ALU.mult)
            S.MM([("transpose", (p_t[:, k * 128:(k + 1) * 128], xn[:, k * 128:(k + 1) * 128], ident_b[:]), {})
                  for k in range(8)], (t_xn, t_const), (t_pt,))
            for k in range(8):
                if not is_sample:
                    if k % 2 == 0:
                        S.I("dve", "tensor_scalar", (t_pt, t_AB), (t_hT,), out=hT[:, k, t * 128:(t + 1) * 128],
                            in0=p_t[:, k * 128:(k + 1) * 128], scalar1=AB[:, li, 0, k, 0:1],
                            scalar2=AB[:, li, 1, k, 0:1], op0=ALU.mult, op1=ALU.add)
                    else:
                        S.I("act", "activation", (t_pt, t_AB), (t_hT,), out=hT[:, k, t * 128:(t + 1) * 128],
                            in_=p_t[:, k * 128:(k + 1) * 128], func=AF.Identity, scale=AB[:, li, 0, k, 0:1],
                            bias=AB[:, li, 1, k, 0:1])
                else:
                    for b in range(SB):
                        S.I("dve", "tensor_scalar", (t_pt, t_AB), (t_hT,), out=hT[:, k, b * 8:(b + 1) * 8],
                            in0=p_t[:, k * 128 + b * 8:k * 128 + (b + 1) * 8], scalar1=AB[:, li, 0, k, 1 + b:2 + b],
                            scalar2=AB[:, li, 1, k, 1 + b:2 + b], op0=ALU.mult, op1=ALU.add)

    def ffn(nt, li, is_sample, filler=None):
        n = nt * 128
        norm_mod_T(nt, li, is_sample)
        for c in range(NFF):
            pg, tg = (p_a, t_pa) if c % 2 == 0 else (p_c, t_pc)
            pu, tu = (p_b, t_pb) if c % 2 == 0 else (p_d, t_pd)
            S.MM([("matmul", (pg[:, 0:n],), dict(lhsT=wg[:, k, c * 128:(c + 1) * 128], rhs=hT[:, k, 0:n],
                                                 start=(k == 0), stop=(k == 7))) for k in range(8)],
                 (t_wg, t_hT), (tg,))
            S.MM([("matmul", (pu[:, 0:n],), dict(lhsT=wu[:, k, c * 128:(c + 1) * 128], rhs=hT[:, k, 0:n],
                                                 start=(k == 0), stop=(k == 7))) for k in range(8)],
                 (t_wu, t_hT), (tu,))
            S.I("act", "activation", (tg,), (t_silu,), out=silu_t[:, 0:n], in_=pg[:, 0:n], func=AF.Silu)
            S.I("dve", "tensor_tensor", (t_silu, tu), (t_act,), out=actT[:, c, 0:n], in0=silu_t[:, 0:n],
                in1=pu[:, 0:n], op=ALU.mult)
            if filler is not None:
                filler(c)
        if filler is not None:
            filler(-1)
        which = 1 if is_sample else 0
        for t in range(nt):
            for half in range(2):
                pp, tp = (p_a, t_pa) if half == 0 else (p_c, t_pc)
                S.MM([("matmul", (pp[:, :],), dict(lhsT=actT[:, c, t * 128:(t + 1) * 128],
                                                   rhs=wd[:, c, half * 512:(half + 1) * 512],
                                                   start=(c == 0), stop=(c == NFF - 1))) for c in range(NFF)],
                     (t_act, t_wd), (tp,))
                S.I("dve", "tensor_tensor", (tp, t_gtab), (t_tmp,), out=tmp[:, half * 512:(half + 1) * 512],
                    in0=pp[:, :], in1=gtab[:, 0, which, half * 512:(half + 1) * 512], op=ALU.mult)
            S.I("dve", "scalar_tensor_tensor", (t_tmp, t_xt), (t_xt,), out=xt[:, t, :], in0=tmp[:], scalar=0.5,
                in1=xt[:, t, :], op0=ALU.mult, op1=ALU.add)

    def lp(q, k, j):
        return LP[:, q, k, j:j + 1]

    def ssm_bu_and_scan(q, n, bufi):
        j = q // 4
        S.MM([("matmul", (p_s1[:, 0:n],), dict(lhsT=lhsB[:, q, 0, :], rhs=uT[:, j, 0:n], start=True, stop=True))],
             (t_lhsB, t_uT), (t_ps1,))
        S.MM([("matmul", (p_s2[:, 0:n],), dict(lhsT=lhsB[:, q, 1, :], rhs=uT[:, j, 0:n], start=True, stop=True))],
             (t_lhsB, t_uT), (t_ps2,))
        bu = BU[bufi]
        tb = t_BU[bufi]
        S.I("act", "activation", (t_ps1,), (tb,), out=bu[:, 0, 0:n], in_=p_s1[:, 0:n], func=AF.Copy)
        S.I("act", "activation", (t_ps2,), (tb,), out=bu[:, 1, 0:n], in_=p_s2[:, 0:n], func=AF.Copy)
        for s in range(1, 8):
            cur_re, cur_im = bu[:, 0, s:n:8], bu[:, 1, s:n:8]
            prv_re, prv_im = bu[:, 0, s - 1:n:8], bu[:, 1, s - 1:n:8]
            S.I("dve", "scalar_tensor_tensor", (tb, t_LP), (tb,), out=cur_re, in0=prv_re, scalar=lp(q, 1, 0),
                in1=cur_re, op0=ALU.mult, op1=ALU.add)
            S.I("dve", "scalar_tensor_tensor", (tb, t_LP), (tb,), out=cur_re, in0=prv_im, scalar=lp(q, 1, 2),
                in1=cur_re, op0=ALU.mult, op1=ALU.add)
            S.I("dve", "scalar_tensor_tensor", (tb, t_LP), (tb,), out=cur_im, in0=prv_re, scalar=lp(q, 1, 1),
                in1=cur_im, op0=ALU.mult, op1=ALU.add)
            S.I("dve", "scalar_tensor_tensor", (tb, t_LP), (tb,), out=cur_im, in0=prv_im, scalar=lp(q, 1, 0),
                in1=cur_im, op0=ALU.mult, op1=ALU.add)

    def ssm_tree_two(q0, n):
        banks = ((p_s1, t_ps1), (p_s2, t_ps2))
        for i in range(2):
            q = q0 + i
            j = q // 4
            pp, tp = banks[i]
            for ri in range(2):
                S.MM([("matmul", (pp[:, ri * n:(ri + 1) * n],), dict(lhsT=lhsB[:, q, ri, :], rhs=uT[:, j, 0:n],
                                                                      start=True, stop=True))], (t_lhsB, t_uT), (tp,))
            S.I("act", "activation", (tp,), (t_BU[i],), out=BU[i][:, :, 0:n],
                in_=pp[:, 0:2 * n].rearrange("p (r n) -> p r n", r=2), func=AF.Copy)
        for (st, k) in ((2, 1), (4, 2), (8, 4)):
            views = []
            for i in range(2):
                bu = BU[i]
                views.append((bu[:, 0, st - 1:n:st], bu[:, 1, st - 1:n:st], bu[:, 0, st // 2 - 1:n:st], bu[:, 1, st // 2 - 1:n:st]))
            for part in range(2):
                for i in range(2):
                    q = q0 + i
                    cre, cim, pre, pim = views[i]
                    if part == 0:
                        S.I("dve", "scalar_tensor_tensor", (t_BU[i], t_LP), (t_BU[i],), out=cre, in0=pre, scalar=lp(q, k, 0),
                            in1=cre, op0=ALU.mult, op1=ALU.add)
                        S.I("dve", "scalar_tensor_tensor", (t_BU[i], t_LP), (t_BU[i],), out=cim, in0=pre, scalar=lp(q, k, 1),
                            in1=cim, op0=ALU.mult, op1=ALU.add)
                    else:
                        S.I("dve", "scalar_tensor_tensor", (t_BU[i], t_LP), (t_BU[i],), out=cre, in0=pim, scalar=lp(q, k, 2),
                            in1=cre, op0=ALU.mult, op1=ALU.add)
                        S.I("dve", "scalar_tensor_tensor", (t_BU[i], t_LP), (t_BU[i],), out=cim, in0=pim, scalar=lp(q, k, 0),
                            in1=cim, op0=ALU.mult, op1=ALU.add)

    def ssm_full_two(q0, n, nsc):
        banks = ((p_s1, t_ps1), (p_s2, t_ps2))
        for i in range(2):
            q = q0 + i
            j = q // 4
            pp, tp = banks[i]
            for ri in range(2):
                S.MM([("matmul", (pp[:, ri * n:(ri + 1) * n],), dict(lhsT=lhsB[:, q, ri, :], rhs=uT[:, j, 0:n],
                                                                      start=True, stop=True))], (t_lhsB, t_uT), (tp,))
            S.I("act", "activation", (tp,), (t_BU[i],), out=BU[i][:, :, 0:n],
                in_=pp[:, 0:2 * n].rearrange("p (r n) -> p r n", r=2), func=AF.Copy)
        for tau in range(8):
            for part in range(2):
                for i in range(2):
                    q = q0 + i
                    bu = BU[i]
                    cre, cim = bu[:, 0, tau:n:8], bu[:, 1, tau:n:8]
                    if tau == 0:
                        pre, pim = XS[:, 0, q, 0:nsc], XS[:, 1, q, 0:nsc]
                        rd = (t_BU[i], t_LP, t_XS)
                    else:
                        pre, pim = bu[:, 0, tau - 1:n:8], bu[:, 1, tau - 1:n:8]
                        rd = (t_BU[i], t_LP)
                    if part == 0:
                        S.I("dve", "scalar_tensor_tensor", rd, (t_BU[i],), out=cre, in0=pre, scalar=lp(q, 1, 0),
                            in1=cre, op0=ALU.mult, op1=ALU.add)
                        S.I("dve", "scalar_tensor_tensor", rd, (t_BU[i],), out=cim, in0=pre, scalar=lp(q, 1, 1),
                            in1=cim, op0=ALU.mult, op1=ALU.add)
                    else:
                        S.I("dve", "scalar_tensor_tensor", rd, (t_BU[i],), out=cre, in0=pim, scalar=lp(q, 1, 2),
                            in1=cre, op0=ALU.mult, op1=ALU.add)
                        S.I("dve", "scalar_tensor_tensor", rd, (t_BU[i],), out=cim, in0=pim, scalar=lp(q, 1, 0),
                            in1=cim, op0=ALU.mult, op1=ALU.add)
        for i in range(2):
            S.I("pool", "tensor_copy", (t_BU[i], t_Hbf), (t_Hbf,), out=Hbf[:, (q0 + i) % 4, :, 0:n], in_=BU[i][:, :, 0:n])

    a8re = LP[:, :, 8, 0]
    a8im = LP[:, :, 8, 1]

    bufc = [0]

    XS2 = tmp[:].rearrange("p (r q j) -> p r q j", r=2, q=16)
    tA = BU[0][:].rearrange("p r n -> p (r n)")

    def level_b(nsc):
        assert nsc == 32
        c_re, c_im = XS[:, 0, :, 0], XS[:, 1, :, 0]
        S.I("dve", "tensor_tensor", (t_XS, t_LP), (t_t4,), out=t4[:, 0], in0=c_re, in1=a8re, op=ALU.mult)
        S.I("dve", "tensor_tensor", (t_XS, t_LP), (t_t4,), out=t4[:, 1], in0=c_im, in1=a8im, op=ALU.mult)
        S.I("dve", "tensor_tensor", (t_XS, t_LP), (t_t4,), out=t4[:, 2], in0=c_im, in1=a8re, op=ALU.mult)
        S.I("dve", "tensor_tensor", (t_XS, t_LP), (t_t4,), out=t4[:, 3], in0=c_re, in1=a8im, op=ALU.mult)
        S.I("dve", "tensor_tensor", (t_t4,), (t_t4,), out=t4[:, 0], in0=t4[:, 0], in1=t4[:, 1], op=ALU.subtract)
        S.I("dve", "tensor_tensor", (t_t4,), (t_t4,), out=t4[:, 2], in0=t4[:, 2], in1=t4[:, 3], op=ALU.add)
        S.I("dve", "tensor_tensor", (t_t4, t_XS), (t_XS,), out=XS[:, 0, :, 1], in0=t4[:, 0], in1=XS[:, 0, :, 1], op=ALU.add)
        S.I("dve", "tensor_tensor", (t_t4, t_XS), (t_XS,), out=XS[:, 1, :, 1], in0=t4[:, 2], in1=XS[:, 1, :, 1], op=ALU.add)
        src, t_src = XS[:, :, :, 1:33], t_XS
        dst, t_dst = XS2, t_tmp
        for m, d in enumerate((1, 2, 4, 8, 16)):
            w = 32 - d
            are = LPB[:, m, 0, :].unsqueeze(2).to_broadcast((128, 16, w))
            aim = LPB[:, m, 1, :].unsqueeze(2).to_broadcast((128, 16, w))
            tAv = tA[:, 0:16 * w].rearrange("p (q j) -> p q j", q=16)
            s_re_lo, s_im_lo = src[:, 0, :, 0:w], src[:, 1, :, 0:w]
            s_re_hi, s_im_hi = src[:, 0, :, d:32], src[:, 1, :, d:32]
            d_re_hi, d_im_hi = dst[:, 0, :, d:32], dst[:, 1, :, d:32]
            S.I("dve", "tensor_copy", (t_src, t_dst), (t_dst,), out=dst[:, :, :, 0:d], in_=src[:, :, :, 0:d])
            S.I("dve", "tensor_tensor", (t_src, t_LP, t_dst), (t_dst,), out=d_re_hi, in0=s_re_lo, in1=are, op=ALU.mult)
            S.I("dve", "tensor_tensor", (t_src, t_LP, t_dst), (t_dst,), out=d_im_hi, in0=s_im_lo, in1=are, op=ALU.mult)
            S.I("dve", "tensor_tensor", (t_src, t_LP, t_BU[0]), (t_BU[0],), out=tAv, in0=s_im_lo, in1=aim, op=ALU.mult)
            S.I("dve", "tensor_tensor", (t_src, t_dst), (t_dst,), out=d_re_hi, in0=d_re_hi, in1=s_re_hi, op=ALU.add)
            S.I("dve", "tensor_tensor", (t_src, t_dst), (t_dst,), out=d_im_hi, in0=d_im_hi, in1=s_im_hi, op=ALU.add)
            S.I("dve", "tensor_tensor", (t_BU[0], t_dst), (t_dst,), out=d_re_hi, in0=d_re_hi, in1=tAv, op=ALU.subtract)
            S.I("dve", "tensor_tensor", (t_src, t_LP, t_BU[0]), (t_BU[0],), out=tAv, in0=s_re_lo, in1=aim, op=ALU.mult)
            S.I("dve", "tensor_tensor", (t_BU[0], t_dst), (t_dst,), out=d_im_hi, in0=d_im_hi, in1=tAv, op=ALU.add)
            src, t_src, dst, t_dst = dst, t_dst, src, t_src
        S.I("dve", "tensor_copy", (t_tmp, t_XS), (t_XS,), out=XS[:, :, :, 1:33], in_=XS2)

    pending = []

    def filler(c):
        if c == -1:
            while pending:
                pending.pop(0)()
        elif c % 2 == 1 and pending:
            pending.pop(0)()

    def ssm_items(bi):
        items = []
        for q0 in range(0, 16, 2):
            def pp_item(q0=q0):
                ssm_tree_two(q0, BT)
                for i in range(2):
                    S.I("pool", "tensor_copy", (t_BU[i], t_XS), (t_XS,), out=XS[:, :, q0 + i, 1:NSC + 1],
                        in_=BU[i][:, :, 7:BT:8])
            items.append(pp_item)

        def lb_item(bi=bi):
            level_b(NSC)
            if bi >= OWN0:
                S.dma(xss[:, :, :, (bi - OWN0) * NSC:(bi - OWN0 + 1) * NSC], XS[:, :, :, 0:NSC], reads=(t_XS,),
                      sem_t=t_XS, slow=True)
            if bi == NBLK - 1:
                S.dma(o_hp, XS[:, :, :, NSC], reads=(t_XS,), sem_t=t_XS, slow=True)
            S.I("dve", "tensor_copy", (t_XS,), (t_XS,), out=XS[:, :, :, 0], in_=XS[:, :, :, NSC])
        items.append(lb_item)
        return items

    for bi in range(NBLK):
        S.dma(xt[:], x_seq[bi * BT:(bi + 1) * BT, :].rearrange("(t p) d -> p t d", p=128), writes=(t_xt,))
        ffn(NTB, 0, False, filler)
        if bi >= HALO0:
            S.dma(x1s[(bi - HALO0) * BT:(bi - HALO0 + 1) * BT, :].rearrange("(t p) d -> p t d", p=128), xt[:],
                  reads=(t_xt,), writes=(t_x1s,), sem_t=t_xt)
        norm_mod_T(NTB, 1, False)
        for j in range(4):
            pp, tp = (p_a, t_pa) if j % 2 == 0 else (p_c, t_pc)
            S.MM([("matmul", (pp[:, 0:BT],), dict(lhsT=wuin[:, k, j * 128:(j + 1) * 128], rhs=hT[:, k, :],
                                                  start=(k == 0), stop=(k == 7))) for k in range(8)],
                 (t_wuin, t_hT), (tp,))
            S.I("dve", "tensor_scalar", (tp, t_const), (t_uT,), out=uT[:, j, :], in0=pp[:, 0:BT],
                scalar1=blkvalid[:, bi:bi + 1], scalar2=None, op0=ALU.mult)
        if bi >= OWN0:
            S.dma(uTs[:, :, (bi - OWN0) * BT:(bi - OWN0 + 1) * BT], uT[:], reads=(t_uT,), sem_t=t_uT)
        pending.extend(ssm_items(bi))
    filler(-1)

    S.dma(xt[:, 0, :], x_smp, writes=(t_xt,))
    ffn(1, 0, True)
    S.dma(x1s[32 * 128:33 * 128, :], xt[:, 0, :], reads=(t_xt,), writes=(t_x1s,), sem_t=t_xt)
    norm_mod_T(1, 1, True)
    for j in range(4):
        pp, tp = (p_a, t_pa) if j % 2 == 0 else (p_c, t_pc)
        S.MM([("matmul", (pp[:, 0:128],), dict(lhsT=wuin[:, k, j * 128:(j + 1) * 128], rhs=hT[:, k, 0:128],
                                               start=(k == 0), stop=(k == 7))) for k in range(8)],
             (t_wuin, t_hT), (tp,))
        S.I("dve", "tensor_copy", (tp,), (t_uT,), out=uT[:, j, 0:128], in_=pp[:, 0:128])
    S.dma(uTs[:, :, TOK_CORE:TOK_CORE + 128], uT[:, :, 0:128], reads=(t_uT,), sem_t=t_uT)
    for q in range(16):
        bufi = bufc[0] % 2
        bufc[0] += 1
        ssm_bu_and_scan(q, 128, bufi)
        S.I("pool", "tensor_copy", (t_BU[bufi], t_XS), (t_XS,), out=XS[:, :, q, 16:32], in_=BU[bufi][:, :, 7:128:8])
    S.dma(h0[:, 0], h0re_in, writes=(t_h0,))
    S.dma(h0[:, 1], h0im_in, writes=(t_h0,))
    a8re_b = LP[:, :, 8, 0].unsqueeze(2).to_broadcast((128, 16, SB))
    a8im_b = LP[:, :, 8, 1].unsqueeze(2).to_broadcast((128, 16, SB))
    S.I("dve", "tensor_tensor", (t_h0, t_LP), (t_hend,), out=hend[:, 0], in0=h0[:, 0], in1=a8re_b, op=ALU.mult)
    S.I("dve", "tensor_tensor", (t_h0, t_LP), (t_hend,), out=hend[:, 1], in0=h0[:, 1], in1=a8re_b, op=ALU.mult)
    S.I("dve", "tensor_tensor", (t_h0, t_LP, t_XS), (t_XS,), out=XS[:, 0, :, 0:16], in0=h0[:, 1], in1=a8im_b, op=ALU.mult)
    S.I("dve", "tensor_tensor", (t_h0, t_LP, t_XS), (t_XS,), out=XS[:, 1, :, 0:16], in0=h0[:, 0], in1=a8im_b, op=ALU.mult)
    S.I("dve", "tensor_tensor", (t_hend, t_XS), (t_hend,), out=hend[:, 0], in0=hend[:, 0], in1=XS[:, 0, :, 0:16],
        op=ALU.subtract)
    S.I("dve", "tensor_tensor", (t_hend, t_XS), (t_hend,), out=hend[:, 1], in0=hend[:, 1], in1=XS[:, 1, :, 0:16],
        op=ALU.add)
    S.I("dve", "tensor_tensor", (t_hend, t_XS), (t_hend,), out=hend[:, 0], in0=hend[:, 0], in1=XS[:, 0, :, 16:32],
        op=ALU.add)
    S.I("dve", "tensor_tensor", (t_hend, t_XS), (t_hend,), out=hend[:, 1], in0=hend[:, 1], in1=XS[:, 1, :, 16:32],
        op=ALU.add)
    S.dma(o_hs, hend, reads=(t_hend,), sem_t=t_hend)

    t_arena = T("arena")

    def AV(off, n):
        return arena[:, off:off + n]

    NTOK = TOK_CORE + 128
    VE = 80
    QTz = AV(0, 8 * NTOK).rearrange("p (h n) -> p h n", h=NH)
    catA = QTz[:, 0:4, :]
    KT = AV(17408, 4 * NSP * 128).rearrange("p (c n) -> p c n", c=4)
    VX = AV(34304, NSP * NH * VE).rearrange("p (t h e) -> p t h e", t=NSP, h=NH)
    WQKV = AV(55424, 8 * 1024).rearrange("p (k n) -> p k n", k=8)
    MS = AV(63616, 17 * 128).rearrange("p (t b i) -> p t b i", t=17, b=16)
    Es = AV(65792, 17 * 64).rearrange("p (t h i) -> p t h i", t=17, h=NH)
    catS = AV(8704, 4 * NTOK).rearrange("p (c n) -> p c n", c=4)
    lhsC = AV(17408, 4096).rearrange("p (q r n) -> p q r n", q=16, r=2)
    wglu = AV(21504, 4096).rearrange("p (k n) -> p k n", k=4)
    gTs = AV(25600, 4 * NTOK).rearrange("p (c n) -> p c n", c=4)
    Hbf = AV(34304, 2048).rearrange("p (q r n) -> p q r n", q=4, r=2)
    wout = AV(36352, 8192).rearrange("p (k n) -> p k n", k=8)
    KTs = AV(55424, 8192).rearrange("p (c n) -> p c n", c=4)
    VXs = AV(34304, 16 * NH * VE).rearrange("p (t h e) -> p t h e", t=16, h=NH)
    Ppad = [AV(44544 + i * 2176, 2176).rearrange("p (t m) -> p t m", t=17) for i in range(2)]
    actflat = actT[:].rearrange("p c n -> p (c n)")
    Mk = actflat[:, 0:17 * 128].rearrange("p (d q) -> p d q", d=17)
    Pb = [actflat[:, 2176 + i * 512:2176 + (i + 1) * 512] for i in range(2)]

    mask_in = din("maskp", [128, 17, 128])
    masks_in = din("masks", [128, 17, 16, 8])
    halo_valid_in = din("halo_valid", [128, 1])
    gqk_in = din("gqk", [1, 2 * HD])
    rope_in = din("rope", [128, NSP, 2, 8])
    cexre_in = din("ssm_cre", [128, 16, 128])
    cexim_in = din("ssm_cim", [128, 16, 128])
    w_glu_in = din("w_glu", [512, 1024])
    w_out_in = din("w_out", [D, D])
    w2g = din("w2_gate", [D, DFF])
    w2u = din("w2_up", [D, DFF])
    w2d = din("w2_down", [DFF, D])
    o_kp = dout("o_kp", [TOK_CORE, 512])
    o_vp = dout("o_vp", [TOK_CORE, 512])
    o_yp = dout("o_yp", [TOK_CORE, D])
    o_ys = dout("o_ys", [128, D])

    rope = sb("rope_sb", [128, NSP, 2, 8])
    gqk = sig[:].rearrange("p k n -> p (k n)")[:, 0:2 * HD]
    cTf = cT[:].rearrange("p k n -> p (k n)")
    hv = cTf[:, 0:1]
    kss = cTf[:, 8:16]
    krs = cTf[:, 16:24]
    rt = PR[:].rearrange("p s n -> p (s n)")[:, 0:256].rearrange("p (a h d) -> p a h d", a=4, h=8)
    t_rope, t_kss, t_krs, t_rt, t_MS, t_Mk = T("rope"), T("kss"), T("krs"), T("rt"), T("MS"), T("Mk")
    t_catA, t_QT, t_KT, t_VX, t_WQKV = T("catA"), T("QT"), T("KT"), T("VX"), T("WQKV")
    t_Pb = [T("Pb0"), T("Pb1")]
    t_Ppad = [T("Pp0"), T("Pp1")]
    t_KTs, t_Es, t_VXs = T("KTs"), T("Es"), T("VXs")

    def barrier_on(trackers, engines=ENG):
        for tt in trackers:
            for e in engines:
                S._wait(e, tt.w)
                for d in tt.r:
                    S._wait(e, d)
    t_catS, t_lhsC, t_wglu, t_gTs, t_Hbf, t_wout = T("catS"), T("lhsC"), T("wglu"), T("gTs"), T("Hbf"), T("wout")
    t_out = T("outst")

    for tt in (t_wg, t_wu, t_wd, t_act):
        for e in ("pe", "act", "dve", "pool", "sp"):
            S._wait(e, tt.w)
            for d in tt.r:
                S._wait(e, d)

    S.dma(rope[:], rope_in, writes=(t_rope,))
    S.dma(gqk, gqk_in.to_broadcast((128, 2 * HD)), writes=(t_rope,))
    S.dma(hv, halo_valid_in, writes=(t_rope,))
    for d0 in range(0, 17, 8):
        d1 = min(17, d0 + 8)
        S.dma(tmp[:, 0:(d1 - d0) * 128], mask_in[:, d0:d1, :].rearrange("p d q -> p (d q)"), writes=(t_tmp,))
        S.I("dve", "tensor_copy", (t_tmp,), (t_Mk,), out=Mk[:, d0:d1, :].rearrange("p d q -> p (d q)"),
            in_=tmp[:, 0:(d1 - d0) * 128])
    for t0 in range(0, 17, 8):
        t1 = min(17, t0 + 8)
        S.dma(tmp[:, 0:(t1 - t0) * 128], masks_in[:, t0:t1].rearrange("p t b i -> p (t b i)"), writes=(t_tmp,))
        S.I("dve", "tensor_copy", (t_tmp,), (t_MS,), out=MS[:, t0:t1].rearrange("p t b i -> p (t b i)"),
            in_=tmp[:, 0:(t1 - t0) * 128])

    def ada_gate_table(li, stage_ap, t_stage):
        for half in range(2):
            blk = (li * 3 + 2) * 2 + half
            for k in range(8):
                load_cast(stage_ap[:, k, :], w_ada[k * 128:(k + 1) * 128, blk * 512:(blk + 1) * 512], 512, t_stage)
            S.dma(bada_blk, bada_row[0:1, blk * 512:(blk + 1) * 512].to_broadcast((17, 512)), writes=(t_bblk,))
            S.MM([("matmul", (p_b[0:17, :],), dict(lhsT=sT[:, k, :], rhs=stage_ap[:, k, :], start=(k == 0), stop=(k == 7)))
                  for k in range(8)], (t_stage, t_const), (t_pb,))
            S.I("dve", "tensor_tensor", (t_pb, t_bblk), (t_grow,), out=grow, in0=p_b[0:17, :], in1=bada_blk, op=ALU.add)
            for which in range(2):
                S.MM([("matmul", (p_c[:, :],), dict(lhsT=esel_f[:, which, :], rhs=grow, start=True, stop=True))],
                     (t_grow, t_const), (t_pc,))
                S.I("act", "activation", (t_pc,), (t_gtab,), out=gtab[:, 0, which, half * 512:(half + 1) * 512],
                    in_=p_c[:, :], func=AF.Copy)

    stage_wq = WQKV[:, :, 0:512]
    ada_gate_table(1, stage_wq, t_WQKV)
    for k in range(8):
        load_cast(WQKV[:, k, :], w_in[k * 128:(k + 1) * 128, 512:1536], 1024, t_WQKV)
    S.I("dve", "memset", (t_VX,), (t_VX,), VX[:, :, :, 64:65], 1.0)
    S.I("pool", "memset", (t_QT,), (t_QT,), QTz, 0.0)

    def qk_norm_rope(buf, gsl, ti, tb):
        b3 = buf.rearrange("p (h d) -> p h d", h=NH)
        S.I("act", "activation", (tb,), (t_tmp,), out=tmp[:, 0:512], in_=buf, func=AF.Square)
        S.I("dve", "tensor_reduce", (t_tmp,), (t_kss,), out=kss, in_=tmp[:, 0:512].rearrange("p (h d) -> p h d", h=NH),
            axis=mybir.AxisListType.X, op=ALU.add)
        S.I("act", "activation", (t_kss, t_const), (t_krs,), out=krs, in_=kss, func=AF.Sqrt, scale=1.0 / HD,
            bias=eps_t[:, 0:1])
        S.I("dve", "reciprocal", (t_krs,), (t_krs,), out=krs, in_=krs)
        S.I("dve", "tensor_tensor", (tb, t_krs), (tb,), out=b3, in0=b3,
            in1=krs.unsqueeze(2).to_broadcast((128, NH, HD)), op=ALU.mult)
        S.I("dve", "tensor_tensor", (tb, t_rope), (tb,), out=b3, in0=b3,
            in1=gqk[:, gsl].unsqueeze(1).to_broadcast((128, NH, HD)), op=ALU.mult)
        cosb = rope[:, ti, 0, :].unsqueeze(1).to_broadcast((128, NH, 8))
        sinb = rope[:, ti, 1, :].unsqueeze(1).to_broadcast((128, NH, 8))
        xa, xb = b3[:, :, 0:8], b3[:, :, 8:16]
        S.I("dve", "tensor_tensor", (tb, t_rope), (t_rt,), out=rt[:, 0], in0=xa, in1=cosb, op=ALU.mult)
        S.I("dve", "tensor_tensor", (tb, t_rope), (t_rt,), out=rt[:, 1], in0=xb, in1=sinb, op=ALU.mult)
        S.I("dve", "tensor_tensor", (tb, t_rope), (t_rt,), out=rt[:, 2], in0=xb, in1=cosb, op=ALU.mult)
        S.I("dve", "tensor_tensor", (tb, t_rope), (t_rt,), out=rt[:, 3], in0=xa, in1=sinb, op=ALU.mult)
        S.I("dve", "tensor_tensor", (t_rt, tb), (tb,), out=xa, in0=rt[:, 0], in1=rt[:, 1], op=ALU.subtract)
        S.I("dve", "tensor_tensor", (t_rt, tb), (tb,), out=xb, in0=rt[:, 2], in1=rt[:, 3], op=ALU.add)

    def to_featmajor(buf, tb, scale):
        S.I("dve", "tensor_scalar", (tb,), (t_xn,), out=xn[:, 0:512], in0=buf, scalar1=scale, scalar2=None, op0=ALU.mult)
        S.MM([("transpose", (p_t[:, c * 128:(c + 1) * 128], xn[:, c * 128:(c + 1) * 128], ident_b[:]), {})
              for c in range(4)], (t_xn, t_const), (t_pt,))
        return p_t[:, 0:512].rearrange("p (c n) -> p c n", c=4)

    kbuf = BU[0][:].rearrange("p r n -> p (r n)")
    vbuf = BU[1][:].rearrange("p r n -> p (r n)")
    for ti in range(NSP if STOP >= 1 else 0):
        is_sample = ti == 32
        is_own = 16 <= ti < 32
        S.dma(xt[:, 0, :], x1s[ti * 128:(ti + 1) * 128, :], reads=(t_x1s,), writes=(t_xt,))
        norm_mod_T(1, 1, is_sample)
        S.MM([("matmul", (p_a[:, :],), dict(lhsT=hT[:, k, 0:128], rhs=WQKV[:, k, 512:1024], start=(k == 0), stop=(k == 7)))
              for k in range(8)], (t_hT, t_WQKV), (t_pa,))
        S.I("act", "activation", (t_pa,), (t_BU[1],), out=vbuf, in_=p_a[:, :], func=AF.Copy)
        if ti < 16:
            S.I("dve", "tensor_scalar", (t_BU[1], t_rope, t_VX), (t_VX,), out=VX[:, ti, :, 0:64],
                in0=vbuf.rearrange("p (h d) -> p h d", h=NH), scalar1=hv, scalar2=None, op0=ALU.mult)
            S.I("dve", "tensor_scalar", (t_rope, t_VX), (t_VX,), out=VX[:, ti, :, 64:65], in0=VX[:, ti, :, 64:65],
                scalar1=hv, scalar2=None, op0=ALU.mult)
        else:
            S.I("dve", "tensor_copy", (t_BU[1], t_VX), (t_VX,), out=VX[:, ti, :, 0:64],
                in_=vbuf.rearrange("p (h d) -> p h d", h=NH))
        if is_own:
            S.dma(o_vp[(ti - 16) * 128:(ti - 15) * 128, :], vbuf, reads=(t_BU[1],), sem_t=t_BU[1])
        if is_sample:
            for b in range(SB):
                S.dma(o_vs[b, WBUF - 8:WBUF, :], BU[1][b * 8:(b + 1) * 8].rearrange("p r n -> p (r n)"),
                      reads=(t_BU[1],), sem_t=t_BU[1])
        S.MM([("matmul", (p_c[:, :],), dict(lhsT=hT[:, k, 0:128], rhs=WQKV[:, k, 0:512], start=(k == 0), stop=(k == 7)))
              for k in range(8)], (t_hT, t_WQKV), (t_pc,))
        S.I("act", "activation", (t_pc,), (t_BU[0],), out=kbuf, in_=p_c[:, :], func=AF.Copy)
        qk_norm_rope(kbuf, slice(HD, 2 * HD), ti, t_BU[0])
        if is_own:
            S.dma(o_kp[(ti - 16) * 128:(ti - 15) * 128, :], kbuf, reads=(t_BU[0],), sem_t=t_BU[0])
        if is_sample:
            for b in range(SB):
                S.dma(o_ks[b, WBUF - 8:WBUF, :], BU[0][b * 8:(b + 1) * 8].rearrange("p r n -> p (r n)"),
                      reads=(t_BU[0],), sem_t=t_BU[0])
        src = to_featmajor(kbuf, t_BU[0], 1.0)
        S.I("act", "activation", (t_pt, t_KT), (t_KT,), out=KT[:, :, ti * 128:(ti + 1) * 128], in_=src, func=AF.Copy)
    if STOP >= 1:
        for k in range(8):
            load_cast(WQKV[:, k, 0:512], w_in[k * 128:(k + 1) * 128, 0:512], 512, t_WQKV)
    for ti in range(16, NSP if STOP >= 1 else 16):
        is_sample = ti == 32
        S.dma(xt[:, 0, :], x1s[ti * 128:(ti + 1) * 128, :], reads=(t_x1s,), writes=(t_xt,))
        norm_mod_T(1, 1, is_sample)
        S.MM([("matmul", (p_b[:, :],), dict(lhsT=hT[:, k, 0:128], rhs=WQKV[:, k, 0:512], start=(k == 0), stop=(k == 7)))
              for k in range(8)], (t_hT, t_WQKV), (t_pb,))
        S.I("act", "activation", (t_pb,), (t_BU[0],), out=kbuf, in_=p_b[:, :], func=AF.Copy)
        qk_norm_rope(kbuf, slice(0, HD), ti, t_BU[0])
        src = to_featmajor(kbuf, t_BU[0], HD ** -0.5)
        cs_ = slice((ti - 16) * 128, (ti - 15) * 128)
        S.I("act", "activation", (t_pt, t_QT), (t_QT,), out=QTz[0:64, 0:NH:2, cs_], in_=src[0:64], func=AF.Copy)
        S.I("act", "activation", (t_pt, t_QT), (t_QT,), out=QTz[64:128, 1:NH:2, cs_], in_=src[64:128], func=AF.Copy)

    def attn_finish(a, num_den, t_nd, obuf, t_ob):
        for g in range(2):
            acc, tacc = num_den(g), t_nd[g]
            S.I("dve", "reciprocal", (tacc,), (t_kss,), out=kss[:, 4 * g:4 * g + 4], in_=acc[:, :, 64])
            S.I("dve", "tensor_tensor", (tacc, t_kss), (t_ob,),
                out=obuf[:, g * 256:(g + 1) * 256].rearrange("p (h d) -> p h d", h=4), in0=acc[:, :, 0:64],
                in1=kss[:, 4 * g:4 * g + 4].unsqueeze(2).to_broadcast((128, 4, HD)), op=ALU.mult)
        src = to_featmajor(obuf, t_ob, 1.0)
        S.I("act", "activation", (t_pt, t_QT), (t_catA, t_QT), out=catA[:, :, a * 128:(a + 1) * 128], in_=src, func=AF.Copy)

    def acc_view(g):
        return (p_s1 if g == 0 else p_s2)[:, 0:4 * VE].rearrange("p (h e) -> p h e", h=4)

    sbanks = ((p_a, t_pa), (p_b, t_pb), (p_c, t_pc), (p_d, t_pd))
    it = [0]
    for a in range(16 if STOP >= 2 else 0):
        for h in range(NH):
            c = h // 2
            pacc, tacc = (p_s1, t_ps1) if h < 4 else (p_s2, t_ps2)
            hh = h % 4
            for d0 in range(0, 17, 4):
                dls = list(range(d0, min(17, d0 + 4)))
                nd = len(dls)
                pbk, tbk = sbanks[it[0] % 4]
                i2 = it[0] % 2
                it[0] += 1
                S.MM([("matmul", (pbk[:, i * 128:(i + 1) * 128],),
                       dict(lhsT=KT[:, c, (16 + a - dl) * 128:(17 + a - dl) * 128], rhs=QTz[:, h, a * 128:(a + 1) * 128],
                            start=True, stop=True)) for i, dl in enumerate(dls)], (t_KT, t_QT), (tbk,))
                S.I("act", "activation", (tbk,), (t_Pb[i2],), out=Pb[i2][:, 0:nd * 128], in_=pbk[:, 0:nd * 128], func=AF.Exp)
                S.I("dve", "tensor_tensor", (t_Pb[i2], t_Mk), (t_Pb[i2],), out=Pb[i2][:, 0:nd * 128], in0=Pb[i2][:, 0:nd * 128],
                    in1=Mk[:, d0:d0 + nd, :].rearrange("p d q -> p (d q)"), op=ALU.mult)
                S.MM([("matmul", (pacc[:, hh * VE:hh * VE + 65],),
                       dict(lhsT=Pb[i2][:, i * 128:(i + 1) * 128], rhs=VX[:, 16 + a - dl, h, 0:65],
                            start=(dl == 0), stop=(dl == 16))) for i, dl in enumerate(dls)], (t_Pb[i2], t_VX), (tacc,))
        attn_finish(a, acc_view, (t_ps1, t_ps2), tmp[:, 0:512], t_tmp)

    barrier_on((t_WQKV, t_VX, t_Pb[0], t_Pb[1]))
    Os = tmp[:, 0:NH * VE].rearrange("p (h e) -> p h e", h=NH)
    S.I("dve", "memset", (t_VXs,), (t_VXs,), VXs[:, :, :, 64:65], 1.0)
    stg = [xt[:, 0, 0:512], xt[:, 0, 512:1024], xt[:, 1, 0:512], xt[:, 1, 512:1024]]
    t_stg = [T("stg0"), T("stg1"), T("stg2"), T("stg3")]
    for tsx in t_stg:
        tsx.w = t_xt.w
        tsx.r = list(t_xt.r)
    sc = [0]
    pp_i = [0]
    for b in range(SB if STOP >= 3 else 0):
        for t in range(16):
            si = sc[0] % 4
            sc[0] += 1
            S.dma(stg[si], ck[b, t * 128:(t + 1) * 128, :], writes=(t_stg[si],))
            S.MM([("transpose", (p_a[:, c * 128:(c + 1) * 128], stg[si][:, c * 128:(c + 1) * 128], ident_f[:]), {})
                  for c in range(4)], (t_stg[si], t_const), (t_pa,))
            S.I("act", "activation", (t_pa, t_KTs), (t_KTs,), out=KTs[:, :, t * 128:(t + 1) * 128],
                in_=p_a[:, :].rearrange("p (c n) -> p c n", c=4), func=AF.Copy)
            si = sc[0] % 4
            sc[0] += 1
            S.dma(stg[si], cv[b, t * 128:(t + 1) * 128, :], writes=(t_stg[si],))
            S.I("pool", "tensor_copy", (t_stg[si], t_VXs), (t_VXs,), out=VXs[:, t, :, 0:64],
                in_=stg[si].rearrange("p (h d) -> p h d", h=NH))
        qs = slice(TOK_CORE + b * 8, TOK_CORE + b * 8 + 8)
        for (pbk, tbk, t_lo, t_hi) in ((p_b, t_pb, 0, 8), (p_c, t_pc, 8, 16), (p_d, t_pd, 16, 17)):
            items = []
            for t in range(t_lo, t_hi):
                for h in range(NH):
                    c = h // 2
                    lhs = KTs[:, c, t * 128:(t + 1) * 128] if t < 16 else KT[:, c, 32 * 128:33 * 128]
                    col = ((t - t_lo) * NH + h) * 8
                    items.append(("matmul", (pbk[:, col:col + 8],), dict(lhsT=lhs, rhs=QTz[:, h, qs], start=True, stop=True)))
            S.MM(items, (t_KTs, t_KT, t_QT), (tbk,))
            ncol = (t_hi - t_lo) * 64
            S.I("act", "activation", (tbk, t_Es), (t_Es,), out=Es[:, t_lo:t_hi].rearrange("p t h i -> p (t h i)"),
                in_=pbk[:, 0:ncol], func=AF.Exp)
        for h in range(NH):
            pi = pp_i[0] % 2
            pp_i[0] += 1
            pacc, tacc = (p_s1, t_ps1) if h < 4 else (p_s2, t_ps2)
            hh = h % 4
            S.I("dve", "memset", (t_Ppad[pi],), (t_Ppad[pi],), Ppad[pi], 0.0)
            S.I("dve", "tensor_tensor", (t_Es, t_MS, t_Ppad[pi]), (t_Ppad[pi],), out=Ppad[pi][:, :, b * 8:(b + 1) * 8],
                in0=Es[:, :, h, :], in1=MS[:, :, b, :], op=ALU.mult)
            S.MM([("matmul", (pacc[:, hh * VE:hh * VE + 65],),
                   dict(lhsT=Ppad[pi][:, t, :], rhs=(VXs[:, t, h, 0:65] if t < 16 else VX[:, 32, h, 0:65]),
                        start=(t == 0), stop=(t == 16))) for t in range(17)], (t_Ppad[pi], t_VXs, t_VX), (tacc,))
        for g in range(2):
            tacc = t_ps1 if g == 0 else t_ps2
            if b == 0:
                S.I("dve", "tensor_copy", (tacc, t_tmp), (t_tmp,), out=Os[:, 4 * g:4 * g + 4, 0:65], in_=acc_view(g)[:, :, 0:65])
            else:
                S.I("dve", "tensor_tensor", (tacc, t_tmp), (t_tmp,), out=Os[:, 4 * g:4 * g + 4, 0:65],
                    in0=acc_view(g)[:, :, 0:65], in1=Os[:, 4 * g:4 * g + 4, 0:65], op=ALU.add)
    if STOP >= 3:
        attn_finish(16, lambda g: Os[:, 4 * g:4 * g + 4, :], (t_tmp, t_tmp), kbuf, t_BU[0])
    t_xt.w = None
    t_xt.r = [d for tsx in t_stg for d in ([tsx.w] + tsx.r) if d is not None]

    barrier_on((t_KT, t_VX, t_QT, t_KTs, t_Es, t_VXs, t_WQKV, t_Ppad[0], t_Ppad[1]))
    S.dma(tmp[:, :], cexre_in[:, 0:8, :].rearrange("p q n -> p (q n)"), writes=(t_tmp,))
    S.I("dve", "tensor_copy", (t_tmp, t_lhsC), (t_lhsC,), out=lhsC[:, 0:8, 0, :], in_=tmp[:].rearrange("p (q n) -> p q n", q=8))
    S.dma(tmp[:, :], cexre_in[:, 8:16, :].rearrange("p q n -> p (q n)"), writes=(t_tmp,))
    S.I("dve", "tensor_copy", (t_tmp, t_lhsC), (t_lhsC,), out=lhsC[:, 8:16, 0, :], in_=tmp[:].rearrange("p (q n) -> p q n", q=8))
    for q0 in (0, 8):
        S.dma(tmp[:, :], cexim_in[:, q0:q0 + 8, :].rearrange("p q n -> p (q n)"), writes=(t_tmp,))
        S.I("dve", "tensor_scalar", (t_tmp, t_lhsC), (t_lhsC,), out=lhsC[:, q0:q0 + 8, 1, :],
            in0=tmp[:].rearrange("p (q n) -> p q n", q=8), scalar1=-1.0, scalar2=None, op0=ALU.mult)
    for k in range(4):
        load_cast(wglu[:, k, :], w_glu_in[k * 128:(k + 1) * 128, :], 1024, t_wglu)
    for k in range(8):
        load_cast(wout[:, k, :], w_out_in[k * 128:(k + 1) * 128, :], 1024, t_wout)
    for c in range(NFF):
        load_cast(wd[:, c, :], w2d[c * 128:(c + 1) * 128, :], D, t_wd)

    for ob in range(9 if STOP >= 4 else 0):
        is_sample = ob == 8
        n = 128 if is_sample else BT
        nsc = n // 8
        tok0 = TOK_CORE if is_sample else ob * BT
        S.dma(uT[:, :, 0:n], uTs[:, :, tok0:tok0 + n], writes=(t_uT,))
        if is_sample:
            S.dma(XS[:, 0, :, 0:16], h0re_in, writes=(t_XS,))
            S.dma(XS[:, 1, :, 0:16], h0im_in, writes=(t_XS,))
        else:
            S.dma(XS[:, :, :, 0:NSC], xss[:, :, :, ob * NSC:(ob + 1) * NSC], writes=(t_XS,), slow=True)
        for j in range(4):
            for qq in (0, 2):
                ssm_full_two(4 * j + qq, n, nsc)
            S.MM([("matmul", (p_a[:, 0:n],), dict(lhsT=lhsC[:, 4 * j + qq, ri, :], rhs=Hbf[:, qq, ri, 0:n],
                                                  start=(qq == 0 and ri == 0), stop=(qq == 3 and ri == 1)))
                  for qq in range(4) for ri in range(2)], (t_lhsC, t_Hbf), (t_pa,))
            yv = tmp[:, 0:n]
            y2 = tmp[:, 256:256 + n]
            S.I("dve", "scalar_tensor_tensor", (t_uT, t_const, t_pa), (t_tmp,), out=yv, in0=uT[:, j, 0:n],
                scalar=dT[:, j:j + 1], in1=p_a[:, 0:n], op0=ALU.mult, op1=ALU.add)
            S.I("dve", "tensor_tensor", (t_tmp,), (t_tmp,), out=y2, in0=yv, in1=yv, op=ALU.mult)
            S.I("dve", "tensor_scalar", (t_tmp,), (t_tmp,), out=y2, in0=y2, scalar1=0.044715, scalar2=1.0,
                op0=ALU.mult, op1=ALU.add)
            S.I("dve", "tensor_tensor", (t_tmp,), (t_tmp,), out=y2, in0=y2, in1=yv, op=ALU.mult)
            S.I("act", "activation", (t_tmp,), (t_tmp,), out=y2, in_=y2, func=AF.Sigmoid, scale=2.0 * math.sqrt(2.0 / PI))
            S.I("dve", "tensor_tensor", (t_tmp, t_gTs), (t_gTs,), out=gTs[:, j, tok0:tok0 + n], in0=yv, in1=y2, op=ALU.mult)
    for t0 in range(0, NTOK if STOP >= 4 else 0, 512):
        n = min(512, NTOK - t0)
        for c in range(4):
            S.MM([("matmul", (p_a[:, 0:n],), dict(lhsT=wglu[:, k, c * 128:(c + 1) * 128], rhs=gTs[:, k, t0:t0 + n],
                                                  start=(k == 0), stop=(k == 3))) for k in range(4)],
                 (t_wglu, t_gTs), (t_pa,))
            S.MM([("matmul", (p_b[:, 0:n],), dict(lhsT=wglu[:, k, 512 + c * 128:512 + (c + 1) * 128], rhs=gTs[:, k, t0:t0 + n],
                                                  start=(k == 0), stop=(k == 3))) for k in range(4)],
                 (t_wglu, t_gTs), (t_pb,))
            S.I("act", "activation", (t_pb,), (t_tmp,), out=tmp[:, 0:n], in_=p_b[:, 0:n], func=AF.Sigmoid)
            S.I("dve", "tensor_tensor", (t_tmp, t_pa, t_catS), (t_catS,), out=catS[:, c, t0:t0 + n], in0=tmp[:, 0:n],
                in1=p_a[:, 0:n], op=ALU.mult)

    for ti in range(17 if STOP >= 5 else 0):
        which = 1 if ti == 16 else 0
        S.dma(xt[:, 0, :], x1s[(16 + ti) * 128:(17 + ti) * 128, :], reads=(t_x1s,), writes=(t_xt,))
        for half in range(2):
            pp, tp = (p_a, t_pa) if half == 0 else (p_c, t_pc)
            S.MM([("matmul", (pp[:, :],),
                   dict(lhsT=(catA[:, k, ti * 128:(ti + 1) * 128] if k < 4 else catS[:, k - 4, ti * 128:(ti + 1) * 128]),
                        rhs=wout[:, k, half * 512:(half + 1) * 512], start=(k == 0), stop=(k == 7))) for k in range(8)],
                 (t_catA, t_catS, t_wout), (tp,))
            S.I("dve", "tensor_tensor", (tp, t_gtab), (t_tmp,), out=tmp[:, half * 512:(half + 1) * 512], in0=pp[:, :],
                in1=gtab[:, 0, which, half * 512:(half + 1) * 512], op=ALU.mult)
        S.I("dve", "tensor_tensor", (t_tmp, t_xt), (t_xt,), out=xt[:, 0, :], in0=tmp[:], in1=xt[:, 0, :], op=ALU.add)
        S.dma(x1s[(16 + ti) * 128:(17 + ti) * 128, :], xt[:, 0, :], reads=(t_xt,), writes=(t_x1s,), sem_t=t_xt)

    for tt in (t_catA, t_catS, t_wout, t_wglu, t_gTs, t_Hbf, t_lhsC, t_KT, t_VX, t_QT, t_WQKV, t_KTs, t_Es, t_VXs):
        for e in ("pool", "sp", "pe"):
            S._wait(e, tt.w)
            for d in tt.r:
                S._wait(e, d)
    ada_gate_table(2, AV(0, 4096).rearrange("p (k n) -> p k n", k=8), t_wg)
    for k in range(8):
        load_cast(wg[:, k, :], w2g[k * 128:(k + 1) * 128, :], DFF, t_wg)
        load_cast(wu[:, k, :], w2u[k * 128:(k + 1) * 128, :], DFF, t_wu)
    for ob in range(8 if STOP >= 6 else 0):
        S.dma(xt[:], x1s[(16 + 2 * ob) * 128:(18 + 2 * ob) * 128, :].rearrange("(t p) d -> p t d", p=128),
              reads=(t_x1s,), writes=(t_xt,))
        ffn(NTB, 2, False)
        S.dma(o_yp[ob * BT:(ob + 1) * BT, :].rearrange("(t p) d -> p t d", p=128), xt[:], reads=(t_xt,), sem_t=t_xt)
    if STOP >= 6:
        S.dma(xt[:, 0, :], x1s[32 * 128:33 * 128, :], reads=(t_x1s,), writes=(t_xt,))
        ffn(1, 2, True)
        S.dma(o_ys, xt[:, 0, :], reads=(t_xt,), sem_t=t_xt)

    for t in T.registry:
        if t.dkey is not None:
            S._wait("sp", (t.dkey, 16 * t.dcnt))

    for e in ENG:
        S.sems[e] = es.enter_context(nc.semaphore("s_" + e))
    for n in S.dsem_names:
        S.sems[n] = es.enter_context(nc.semaphore(n))
    with es:
        with nc.Block() as block:
            @block.tensor
            def _(E):
                for f in S.prog["pe"]:
                    f(E)

            @block.scalar
            def _(E):
                for f in S.prog["act"]:
                    f(E)

            @block.vector
            def _(E):
                for f in S.prog["dve"]:
                    f(E)

            @block.gpsimd
            def _(E):
                for f in S.prog["pool"]:
                    f(E)

            @block.sync
            def _(E):
                for f in S.prog["sp"]:
                    f(E)
    return nc


_NC = None


def _rope_tables(pos):
    half = 8
    inv = (500000.0 ** (-np.arange(half, dtype=np.float32) / half)).astype(np.float32)
    ang = pos.astype(np.float32)[:, None] * inv[None, :]
    return np.cos(ang).astype(np.float32), np.sin(ang).astype(np.float32)


def kernel(x_prompt, x_sample, c_prompt, c_sample, cache_k_win, cache_v_win,
           state_ssm_re, state_ssm_im, w_ada, b_ada, g_ffn1, w1_gate, w1_up, w1_down,
           g_mix, w_in, g_q, g_k, ssm_a_re, ssm_a_im, ssm_log_dt, ssm_b_re, ssm_b_im,
           ssm_c_re, ssm_c_im, ssm_d, w_glu, w_out, g_ffn2, w2_gate, w2_up, w2_down):
    global _NC
    if _NC is None:
        _NC = build_nc()
    nc = _NC
    f = np.float32
    xp = np.asarray(x_prompt, f)[0]
    xs = np.asarray(x_sample, f)
    ident = np.eye(128, dtype=f)
    gT = np.stack([np.asarray(g, f)[0].reshape(8, 128).T for g in (g_ffn1, g_mix, g_ffn2)], axis=1)
    gqk = np.concatenate([np.asarray(g_q, f)[0], np.asarray(g_k, f)[0]])[None, :]
    badaT = np.asarray(b_ada, f)[0].reshape(72, 128).T
    esel = np.zeros((17, 2, 128), f)
    esel[0, 0, :] = 1.0
    for b in range(SB):
        esel[1 + b, 1, b * 8:(b + 1) * 8] = 1.0

    def pairrow(a):
        return np.ascontiguousarray(np.asarray(a, f).reshape(16, 2, 64).transpose(1, 2, 0).reshape(128, 16))
    ar = pairrow(ssm_a_re[0])
    ai = pairrow(ssm_a_im[0])
    ldt = pairrow(np.repeat(np.asarray(ssm_log_dt, f)[0][:, None], 64, axis=1))

    def bexp(b):
        b = np.asarray(b, f).reshape(16, 2, 64, 16)
        o = np.zeros((2, 64, 16, 2, 16), f)
        for g2 in range(2):
            o[g2, :, :, g2, :] = b[:, g2].transpose(1, 0, 2)
        return np.ascontiguousarray(o.reshape(128, 16, 32))

    def cexp(c):
        c = np.asarray(c, f).reshape(16, 2, 16, 64)
        o = np.zeros((2, 64, 16, 128), f)
        for q in range(16):
            for g2 in range(2):
                c0 = 32 * (q % 4) + 16 * g2
                o[g2, :, q, c0:c0 + 16] = c[q, g2].T
        return np.ascontiguousarray(o.reshape(128, 16, 128))
    bre, bim = bexp(ssm_b_re[0]), bexp(ssm_b_im[0])
    cre, cim = cexp(ssm_c_re[0]), cexp(ssm_c_im[0])
    dTt = np.ascontiguousarray(np.asarray(ssm_d, f)[0].reshape(4, 128).T)

    def h0lay(s, i):
        s = np.asarray(s, f)[0, i * SB:(i + 1) * SB].reshape(SB, 16, 2, 64)
        return np.ascontiguousarray(s.transpose(2, 3, 1, 0).reshape(128, 16, SB))

    def mult(dist):
        dist = np.asarray(dist)
        m = ((dist >= 0) & (dist <= 128)).astype(f)
        m += ((dist >= 0) & (dist <= 512) & (dist % 4 == 0)).astype(f)
        m += ((dist >= 0) & (dist <= 2048) & (dist % 16 == 0)).astype(f)
        return m
    kk = np.arange(128)[:, None, None]
    dd = np.arange(17)[None, :, None]
    qq_ = np.arange(128)[None, None, :]
    maskp = mult(128 * dd + qq_ - kk).astype(f)
    masks = np.zeros((128, 17, SB, 8), f)
    ii = np.arange(8)[None, :]
    for t in range(16):
        masks[:, t, :, :] = mult(WBUF + ii - (128 * t + np.arange(128)[:, None]))[:, None, :]
    for b_ in range(SB):
        for i2 in range(8):
            for i1_ in range(i2 + 1):
                masks[b_ * 8 + i1_, 16, b_, i2] = mult(i2 - i1_)
    in_maps = []
    for i in range(NCORES):
        nreal = (i + 1) * TOK_CORE
        x_seq = np.zeros((NBLK * BT, D), f)
        x_seq[NBLK * BT - nreal:] = xp[:nreal]
        blkvalid = np.zeros((128, NBLK), f)
        blkvalid[:, NBLK - nreal // BT:] = 1.0
        c17 = np.concatenate([np.asarray(c_prompt, f), np.asarray(c_sample, f)[i * SB:(i + 1) * SB]], 0)
        cT = c17.T.reshape(8, 128, 17).transpose(1, 0, 2)
        pos = np.concatenate([(i - 1) * TOK_CORE + np.arange(2 * TOK_CORE), np.tile(PAST + np.arange(8), SB)])
        cs, sn = _rope_tables(pos)
        rope = np.stack([cs.reshape(NSP, 128, 8), sn.reshape(NSP, 128, 8)], axis=2).transpose(1, 0, 2, 3)
        in_maps.append({
            "x_seq": x_seq, "x_smp": np.ascontiguousarray(xs[i * SB:(i + 1) * SB].reshape(128, D)),
            "blkvalid": blkvalid, "cT": np.ascontiguousarray(cT),
            "bada_row": np.asarray(b_ada, f), "badaT": np.ascontiguousarray(badaT),
            "gT": np.ascontiguousarray(gT), "ident": ident, "esel": esel,
            "w_ada": np.asarray(w_ada, f)[0], "w1_gate": np.asarray(w1_gate, f)[0],
            "w1_up": np.asarray(w1_up, f)[0], "w1_down": np.asarray(w1_down, f)[0],
            "w_in": np.asarray(w_in, f)[0],
            "cache_k": np.asarray(cache_k_win, f)[0, i * SB:(i + 1) * SB].reshape(SB, WBUF, 512),
            "cache_v": np.asarray(cache_v_win, f)[0, i * SB:(i + 1) * SB].reshape(SB, WBUF, 512),
            "ssm_ar": ar, "ssm_ai": ai, "ssm_ldt": ldt, "ssm_bre": bre, "ssm_bim": bim,
            "ssm_dT": dTt, "ssm_cre": cre, "ssm_cim": cim, "maskp": maskp, "masks": masks,
            "halo_valid": np.full((128, 1), 1.0 if i > 0 else 0.0, f), "gqk": np.ascontiguousarray(gqk),
            "rope": np.ascontiguousarray(rope), "w_glu": np.asarray(w_glu, f)[0], "w_out": np.asarray(w_out, f)[0],
            "w2_gate": np.asarray(w2_gate, f)[0], "w2_up": np.asarray(w2_up, f)[0], "w2_down": np.asarray(w2_down, f)[0],
            "h0re": h0lay(state_ssm_re, i), "h0im": h0lay(state_ssm_im, i),
        })
    res = run_bass_kernel_spmd(nc, in_maps, core_ids=list(range(NCORES)))
    R = res.results
    y_p = np.concatenate([R[i]["o_yp"] for i in range(NCORES)], 0).reshape(1, SEQ, D)
    y_s = np.concatenate([R[i]["o_ys"] for i in range(NCORES)], 0).reshape(128, 8, D)
    kp = R[7]["o_kp"].reshape(1, 1, WBUF, NH, HD)
    vp = R[7]["o_vp"].reshape(1, 1, WBUF, NH, HD)
    hp = R[7]["o_hp"]
    hp = hp.reshape(2, 64, 2, 16).transpose(2, 3, 0, 1).reshape(2, 32, 64)
    hpr, hpi = hp[0][None, None], hp[1][None, None]
    ks = np.concatenate([R[i]["o_ks"] for i in range(NCORES)], 0).reshape(1, 128, WBUF, NH, HD)
    vs = np.concatenate([R[i]["o_vs"] for i in range(NCORES)], 0).reshape(1, 128, WBUF, NH, HD)
    hs = np.stack([R[i]["o_hs"] for i in range(NCORES)], 0)
    hs = hs.reshape(NCORES, 2, 64, 2, 16, SB).transpose(3, 0, 5, 4, 1, 2).reshape(2, 128, 32, 64)
    return (y_p, y_s, kp, vp, np.ascontiguousarray(hpr), np.ascontiguousarray(hpi), ks, vs,
            np.ascontiguousarray(hs[0][None]), np.ascontiguousarray(hs[1][None]))
```

```python
import math
from contextlib import ExitStack

import numpy as np
import concourse.bass as bass
import concourse.mybir as mybir
from concourse.bass_utils import run_bass_kernel_spmd

F32 = mybir.dt.float32
BF16 = mybir.dt.bfloat16
AF = mybir.ActivationFunctionType
ALU = mybir.AluOpType

NCORES = 8
D = 1024
DFF = 2816
NFF = DFF // 128
SEQ = 16384
TOK_CORE = SEQ // NCORES
NT_P = TOK_CORE // 128
NT = NT_P + 1
SB = 16
WBUF = 2048
NH = 8
HD = 64
EPS = 1e-6
PAST = 8192
ENG = ("pe", "act", "dve", "pool", "sp")


class T:
    registry = []

    def __init__(self, name):
        T.registry.append(self)
        self.name = name
        self.w = None
        self.r = []
        self.dkey = None
        self.dcnt = 0


class Sched:
    def __init__(self):
        self.prog = {e: [] for e in ENG}
        self.cnt = {e: 0 for e in ENG}
        self.seen = {e: {} for e in ENG}
        self.sems = {}
        self.dsem_names = []
        self.final = []

    def _wait(self, eng, dep):
        if dep is None:
            return
        key, val = dep
        if self.seen[eng].get(key, 0) >= val:
            return
        self.seen[eng][key] = val
        self.prog[eng].append(lambda E, key=key, val=val: E.wait_ge(self.sems[key], val))

    def _deps(self, eng, reads, writes):
        for t in reads:
            self._wait(eng, t.w)
        for t in writes:
            self._wait(eng, t.w)
            for d in t.r:
                self._wait(eng, d)

    def _commit(self, dep, reads, writes):
        for t in reads:
            t.r.append(dep)
        for t in writes:
            t.w = dep
            t.r = []

    def op(self, eng, fn, reads=(), writes=()):
        self._deps(eng, reads, writes)
        self.cnt[eng] += 1
        dep = (eng, self.cnt[eng])
        self.prog[eng].append(lambda E, fn=fn, eng=eng: fn(E).then_inc(self.sems[eng], 1))
        self._commit(dep, reads, writes)

    def mm(self, fns, reads=(), writes=()):
        self._deps("pe", reads, writes)
        for fn in fns[:-1]:
            self.prog["pe"].append(lambda E, fn=fn: fn(E))
        self.cnt["pe"] += 1
        dep = ("pe", self.cnt["pe"])
        last = fns[-1]
        self.prog["pe"].append(lambda E, fn=last: fn(E).then_inc(self.sems["pe"], 1))
        self._commit(dep, reads, writes)

    def I(self, eng, name, reads, writes, *args, **kw):
        self.op(eng, lambda E: getattr(E, name)(*args, **kw), reads, writes)

    def MM(self, items, reads, writes):
        self.mm([lambda E, n=n, a=a, k=k: getattr(E, n)(*a, **k) for (n, a, k) in items], reads, writes)

    def dma(self, out_ap, in_ap, reads=(), writes=(), sem_t=None, slow=False):
        eng = "sp"
        self._deps(eng, reads, writes)
        st = sem_t if sem_t is not None else (writes[0] if writes else reads[0])
        if st.dkey is None:
            st.dkey = "d_" + st.name
            self.dsem_names.append(st.dkey)
        st.dcnt += 1
        dep = (st.dkey, 16 * st.dcnt)
        key = st.dkey
        if slow:
            self.prog[eng].append(lambda E, o=out_ap, i=in_ap, key=key: E.dma_start(
                out=o, in_=i, allow_slow_non_contiguous=True).then_inc(self.sems[key], 16))
        else:
            self.prog[eng].append(lambda E, o=out_ap, i=in_ap, key=key: E.dma_start(
                out=o, in_=i).then_inc(self.sems[key], 16))
        self._commit(dep, reads, writes)
        return dep


BT = 256
NTB = BT // 128
NSC = BT // 8
NBLK = 64
OWN0 = 56
HALO0 = 48
NSP = 33
PI = math.pi


import os
STOP = int(os.environ.get('KSTOP', '99'))


def build_nc():
    nc = bass.Bass("TRN2", target_bir_lowering=False)
    S = Sched()
    es = ExitStack()

    def din(name, shape, dt=F32):
        return nc.dram_tensor(name, list(shape), dt, kind="ExternalInput").ap()

    def dout(name, shape, dt=F32):
        return nc.dram_tensor(name, list(shape), dt, kind="ExternalOutput").ap()

    def sb(name, shape, dt=F32):
        return es.enter_context(nc.sbuf_tensor(name, list(shape), dt))

    def ps(name, shape, dt=F32):
        return es.enter_context(nc.psum_tensor(name, list(shape), dt))

    x_seq = din("x_seq", [NBLK * BT, D])
    x_smp = din("x_smp", [128, D])
    blkvalid_in = din("blkvalid", [128, NBLK])
    cT_in = din("cT", [128, 8, 17])
    bada_row = din("bada_row", [1, 9 * D])
    badaT = din("badaT", [128, 72])
    gT_in = din("gT", [128, 3, 8])
    ident_in = din("ident", [128, 128])
    esel_in = din("esel", [17, 2, 128])
    w_ada = din("w_ada", [D, 9 * D])
    w1g = din("w1_gate", [D, DFF])
    w1u = din("w1_up", [D, DFF])
    w1d = din("w1_down", [DFF, D])
    w_in = din("w_in", [D, 2048])
    ck = din("cache_k", [SB, WBUF, 512])
    cv = din("cache_v", [SB, WBUF, 512])
    ssm_ar = din("ssm_ar", [128, 16])
    ssm_ai = din("ssm_ai", [128, 16])
    ssm_ldt = din("ssm_ldt", [128, 16])
    ssm_bre = din("ssm_bre", [128, 16, 32])
    ssm_bim = din("ssm_bim", [128, 16, 32])
    ssm_dT = din("ssm_dT", [128, 4])
    h0re_in = din("h0re", [128, 16, SB])
    h0im_in = din("h0im", [128, 16, SB])

    o_ks = dout("o_ks", [SB, WBUF, 512])
    o_vs = dout("o_vs", [SB, WBUF, 512])
    o_hp = dout("o_hp", [128, 2, 16])
    o_hs = dout("o_hs", [128, 2, 16, SB])
    x1s = nc.dram_tensor("x1s", [NSP * 128, D], F32).ap()
    uTs = nc.dram_tensor("uTs", [128, 4, TOK_CORE + 128], BF16).ap()
    xss = nc.dram_tensor("xss", [128, 2, 16, TOK_CORE // 8], F32).ap()

    ident_f = sb("ident_f", [128, 128])
    ident_b = sb("ident_b", [128, 128], BF16)
    esel_f = sb("esel_f", [17, 2, 128])
    cT = sb("cT_sb", [128, 8, 17])
    sT = sb("sT_sb", [128, 8, 17], BF16)
    sig = sb("sig_sb", [128, 8, 17])
    badaT_sb = sb("badaT_sb", [128, 72])
    gT = sb("gT_sb", [128, 3, 8])
    blkvalid = sb("blkvalid_sb", [128, NBLK])
    gtab = sb("gtab", [128, 1, 2, D])
    AB = sb("AB", [128, 3, 2, 8, 17])
    eps_t = sb("eps_t", [128, 1])
    ARN = 3 * 8 * DFF
    arena = sb("arena", [128, ARN], BF16)
    wg = arena[:, 0:8 * DFF].rearrange("p (k n) -> p k n", k=8)
    wu = arena[:, 8 * DFF:16 * DFF].rearrange("p (k n) -> p k n", k=8)
    wd = arena[:, 16 * DFF:24 * DFF].rearrange("p (c n) -> p c n", c=NFF)
    wuin = sb("wuin", [128, 8, 512], BF16)
    wada_b = wd[:, 0:4, :].rearrange("p c (a n) -> p (c a) n", a=2)
    xt = sb("xt", [128, NTB, D])
    tmp = sb("tmp", [128, D])
    bada_blk = tmp[0:17, 0:512]
    grow = tmp[0:17, 512:1024]
    xn = sb("xn", [128, D], BF16)
    ssq = sb("ssq", [128, 4])
    rstd = sb("rstd", [128, 8])
    hT = sb("hT", [128, 8, BT], BF16)
    actT = sb("actT", [128, NFF, BT], BF16)
    silu_t = sb("silu_t", [128, BT], BF16)
    uT = sb("uT", [128, 4, BT], BF16)
    PR = sb("PR", [128, 18, 16])
    PRI = sb("PRI", [128, 16], mybir.dt.int32)
    LP = sb("LP", [128, 16, 9, 3])
    bb = xt[:, 0, :].rearrange("p (r q n) -> p r q n", r=2, q=16)
    bex = xt[:, 1, :].rearrange("p (r q n) -> p r q n", r=2, q=16)
    wexp = actT[:, 0:16, :].rearrange("p a (b n) -> p (a b) n", n=128).rearrange("p (r q) n -> p r q n", r=2)
    lhsB = sb("lhsB", [128, 16, 2, 128], BF16)
    dT = sb("dT_sb", [128, 4])
    BU = [sb(f"BU{i}", [128, 2, BT]) for i in range(2)]
    wst = [BU[i][:].rearrange("p r n -> p (r n)") for i in range(2)]
    WCH = 2 * BT
    XS = sb("XS", [128, 2, 16, NSC + 1])
    t4 = sb("t4", [128, 4, 16])
    hend = tmp[:, 0:512].rearrange("p (r q b) -> p r q b", r=2, q=16)
    h0 = tmp[:, 512:1024].rearrange("p (r q b) -> p r q b", r=2, q=16)

    p_a = ps("p_a", [128, 512])
    p_b = ps("p_b", [128, 512])
    p_c = ps("p_c", [128, 512])
    p_d = ps("p_d", [128, 512])
    p_t = ps("p_t", [128, 1024], BF16)
    p_s1 = ps("p_s1", [128, 512])
    p_s2 = ps("p_s2", [128, 512])

    t_const = T("const")
    t_wg, t_wu, t_wd, t_wuin = T("wg"), T("wu"), T("wd"), T("wuin")
    t_wada = t_wd
    t_gtab, t_AB = T("gtab"), T("AB")
    t_pa, t_pb, t_pc, t_pd, t_pt, t_ps1, t_ps2 = T("pa"), T("pb"), T("pc"), T("pd"), T("pt"), T("ps1"), T("ps2")
    t_xt, t_xn, t_ssq, t_rstd, t_tmp = T("xt"), T("xn"), T("ssq"), T("rstd"), T("tmp")
    t_grow = t_bblk = t_tmp
    t_hT, t_act, t_silu, t_uT = T("hT"), T("act"), T("silu"), T("uT")
    t_cache = T("cache")
    t_x1s = T("x1s")
    t_PR, t_LP, t_lhsB = T("PR"), T("LP"), T("lhsB")
    t_bb = t_xt
    t_wexp = t_act
    t_BU = [T("BU0"), T("BU1")]
    t_wst = t_BU
    t_XS, t_t4 = T("XS"), T("t4")
    t_h0 = t_hend = t_tmp

    for b in range(SB):
        for (src, dst) in ((ck, o_ks), (cv, o_vs)):
            S.dma(dst[b, 0:WBUF - 8, :], src[b, 8:WBUF, :], sem_t=t_cache)

    for (dst, srcap) in ((ident_f, ident_in), (esel_f, esel_in), (cT, cT_in), (badaT_sb, badaT),
                         (gT, gT_in), (blkvalid, blkvalid_in), (dT, ssm_dT)):
        S.dma(dst[:], srcap, writes=(t_const,))
    S.dma(PR[:, 0, :], ssm_ar, writes=(t_PR,))
    S.dma(PR[:, 1, :], ssm_ai, writes=(t_PR,))
    S.dma(PR[:, 2, :], ssm_ldt, writes=(t_PR,))
    S.dma(bex[:, 0], ssm_bre, writes=(t_bb,))
    S.dma(bex[:, 1], ssm_bim, writes=(t_bb,))
    S.I("dve", "memset", (), (t_const,), eps_t[:], EPS)
    S.I("dve", "tensor_copy", (t_const,), (t_const,), out=ident_b[:], in_=ident_f[:])
    S.I("act", "activation", (t_const,), (t_const,), out=sig[:], in_=cT[:], func=AF.Sigmoid)
    S.I("dve", "tensor_tensor", (t_const,), (t_const,), out=sT[:], in0=sig[:], in1=cT[:], op=ALU.mult)

    def P(i):
        return PR[:, i, :]

    def pr(name, *a, **k):
        S.I("dve", name, (t_PR,), (t_PR,), *a, **k)

    AR, AI, LDT, DT_, MAG, ANG, R1, R2, SN, CS, LR, LI = range(12)
    S.I("act", "activation", (t_PR,), (t_PR,), out=P(DT_), in_=P(LDT), func=AF.Exp)
    pr("tensor_tensor", out=P(MAG), in0=P(AR), in1=P(DT_), op=ALU.mult)
    S.I("act", "activation", (t_PR,), (t_PR,), out=P(MAG), in_=P(MAG), func=AF.Exp)
    pr("tensor_tensor", out=P(ANG), in0=P(AI), in1=P(DT_), op=ALU.mult)
    T1, T2 = 12, 13
    for (dst_, off_) in ((R1, 10.0 * PI), (R2, 10.5 * PI)):
        pr("tensor_scalar", out=P(T1), in0=P(ANG), scalar1=off_, scalar2=1.0 / (2.0 * PI), op0=ALU.add, op1=ALU.mult)
        S.I("dve", "tensor_copy", (t_PR,), (t_PR,), out=PRI[:], in_=P(T1))
        S.I("dve", "tensor_copy", (t_PR,), (t_PR,), out=P(T2), in_=PRI[:])
        pr("tensor_scalar", out=P(dst_), in0=P(ANG), scalar1=off_, scalar2=None, op0=ALU.add)
        pr("scalar_tensor_tensor", out=P(dst_), in0=P(T2), scalar=-2.0 * PI, in1=P(dst_), op0=ALU.mult, op1=ALU.add)
        pr("tensor_scalar", out=P(T1), in0=P(dst_), scalar1=PI, scalar2=-2.0 * PI, op0=ALU.is_gt, op1=ALU.mult)
        pr("tensor_tensor", out=P(dst_), in0=P(dst_), in1=P(T1), op=ALU.add)
        pr("tensor_scalar", out=P(dst_), in0=P(dst_), scalar1=-PI, scalar2=PI, op0=ALU.max, op1=ALU.min)
    S.I("act", "activation", (t_PR,), (t_PR,), out=P(SN), in_=P(R1), func=AF.Sin)
    S.I("act", "activation", (t_PR,), (t_PR,), out=P(CS), in_=P(R2), func=AF.Sin)
    pr("tensor_tensor", out=P(LR), in0=P(MAG), in1=P(CS), op=ALU.mult)
    pr("tensor_tensor", out=P(LI), in0=P(MAG), in1=P(SN), op=ALU.mult)
    S.I("dve", "memset", (), (t_LP,), LP[:, :, 0, 0], 1.0)
    S.I("dve", "memset", (t_LP,), (t_LP,), LP[:, :, 0, 1:3], 0.0)
    S.I("dve", "tensor_copy", (t_PR, t_LP), (t_LP,), out=LP[:, :, 1, 0], in_=P(LR))
    S.I("dve", "tensor_copy", (t_PR, t_LP), (t_LP,), out=LP[:, :, 1, 1], in_=P(LI))
    T1, T2 = 12, 13
    for k in range(2, 9):
        S.I("dve", "tensor_tensor", (t_LP, t_PR), (t_PR,), out=P(T1), in0=LP[:, :, k - 1, 0], in1=P(LR), op=ALU.mult)
        S.I("dve", "tensor_tensor", (t_LP, t_PR), (t_PR,), out=P(T2), in0=LP[:, :, k - 1, 1], in1=P(LI), op=ALU.mult)
        S.I("dve", "tensor_tensor", (t_PR, t_LP), (t_LP,), out=LP[:, :, k, 0], in0=P(T1), in1=P(T2), op=ALU.subtract)
        S.I("dve", "tensor_tensor", (t_LP, t_PR), (t_PR,), out=P(T1), in0=LP[:, :, k - 1, 0], in1=P(LI), op=ALU.mult)
        S.I("dve", "tensor_tensor", (t_LP, t_PR), (t_PR,), out=P(T2), in0=LP[:, :, k - 1, 1], in1=P(LR), op=ALU.mult)
        S.I("dve", "tensor_tensor", (t_PR, t_LP), (t_LP,), out=LP[:, :, k, 1], in0=P(T1), in1=P(T2), op=ALU.add)
    S.I("dve", "tensor_scalar", (t_LP,), (t_LP,), out=LP[:, :, 1:9, 2], in0=LP[:, :, 1:9, 1], scalar1=-1.0,
        scalar2=None, op0=ALU.mult)
    NR, DEN, CR, CI = 14, 15, 16, 17
    pr("tensor_scalar", out=P(NR), in0=P(LR), scalar1=-1.0, scalar2=None, op0=ALU.add)
    pr("tensor_tensor", out=P(T1), in0=P(AR), in1=P(AR), op=ALU.mult)
    pr("tensor_tensor", out=P(T2), in0=P(AI), in1=P(AI), op=ALU.mult)
    pr("tensor_tensor", out=P(DEN), in0=P(T1), in1=P(T2), op=ALU.add)
    pr("reciprocal", out=P(DEN), in_=P(DEN))
    pr("tensor_tensor", out=P(T1), in0=P(NR), in1=P(AR), op=ALU.mult)
    pr("tensor_tensor", out=P(T2), in0=P(LI), in1=P(AI), op=ALU.mult)
    pr("tensor_tensor", out=P(CR), in0=P(T1), in1=P(T2), op=ALU.add)
    pr("tensor_tensor", out=P(CR), in0=P(CR), in1=P(DEN), op=ALU.mult)
    pr("tensor_tensor", out=P(T1), in0=P(LI), in1=P(AR), op=ALU.mult)
    pr("tensor_tensor", out=P(T2), in0=P(NR), in1=P(AI), op=ALU.mult)
    pr("tensor_tensor", out=P(CI), in0=P(T1), in1=P(T2), op=ALU.subtract)
    pr("tensor_tensor", out=P(CI), in0=P(CI), in1=P(DEN), op=ALU.mult)
    crb = P(CR).unsqueeze(2).to_broadcast((128, 16, 32))
    cib = P(CI).unsqueeze(2).to_broadcast((128, 16, 32))
    S.I("dve", "tensor_tensor", (t_PR, t_bb), (t_bb,), out=bb[:, 0], in0=bex[:, 0], in1=crb, op=ALU.mult)
    S.I("dve", "tensor_tensor", (t_PR, t_bb), (t_bb,), out=bb[:, 1], in0=bex[:, 1], in1=crb, op=ALU.mult)
    S.I("dve", "tensor_tensor", (t_PR, t_bb), (t_bb,), out=bex[:, 0], in0=bex[:, 0], in1=cib, op=ALU.mult)
    S.I("dve", "tensor_tensor", (t_PR, t_bb), (t_bb,), out=bex[:, 1], in0=bex[:, 1], in1=cib, op=ALU.mult)
    S.I("dve", "tensor_tensor", (t_bb,), (t_bb,), out=bb[:, 0], in0=bb[:, 0], in1=bex[:, 1], op=ALU.subtract)
    S.I("dve", "tensor_tensor", (t_bb,), (t_bb,), out=bb[:, 1], in0=bb[:, 1], in1=bex[:, 0], op=ALU.add)
    LPB = PR[:, 0:12, :].rearrange("p (m r) n -> p m r n", r=2)
    S.I("dve", "tensor_copy", (t_LP,), (t_LP,), out=LPB[:, 0, 0, :], in_=LP[:, :, 8, 0])
    S.I("dve", "tensor_copy", (t_LP,), (t_LP,), out=LPB[:, 0, 1, :], in_=LP[:, :, 8, 1])
    for m in range(1, 6):
        a_, b_ = LPB[:, m - 1, 0, :], LPB[:, m - 1, 1, :]
        S.I("dve", "tensor_tensor", (t_LP, t_PR), (t_PR,), out=P(T1), in0=a_, in1=a_, op=ALU.mult)
        S.I("dve", "tensor_tensor", (t_LP, t_PR), (t_PR,), out=P(T2), in0=b_, in1=b_, op=ALU.mult)
        S.I("dve", "tensor_tensor", (t_PR, t_LP), (t_LP,), out=LPB[:, m, 0, :], in0=P(T1), in1=P(T2), op=ALU.subtract)
        S.I("dve", "tensor_tensor", (t_LP, t_PR), (t_PR,), out=P(T1), in0=a_, in1=b_, op=ALU.mult)
        S.I("dve", "tensor_scalar", (t_PR, t_LP), (t_LP,), out=LPB[:, m, 1, :], in0=P(T1), scalar1=2.0, scalar2=None,
            op0=ALU.mult)
    S.I("dve", "memset", (), (t_wexp,), wexp[:], 0.0)
    for ri in range(2):
        for s in range(4):
            S.I("dve", "tensor_copy", (t_bb, t_wexp), (t_wexp,), out=wexp[:, ri, s::4, 32 * s:32 * s + 32],
                in_=bb[:, ri, s::4, :])
    for q in range(16):
        S.MM([("transpose", (p_t[:, ri * 128:(ri + 1) * 128], wexp[:, ri, q, :], ident_b[:]), {}) for ri in range(2)],
             (t_wexp, t_const), (t_pt,))
        S.I("act", "activation", (t_pt,), (t_lhsB,), out=lhsB[:, q, :, :],
            in_=p_t[:, 0:256].rearrange("p (r n) -> p r n", r=2), func=AF.Copy)
    S.I("dve", "memset", (), (t_XS,), XS[:], 0.0)

    wcount = [0]

    def load_cast(dst_ap, src_ap, ncols, t_dst):
        for c0 in range(0, ncols, WCH):
            c1 = min(ncols, c0 + WCH)
            i = wcount[0] % 2
            wcount[0] += 1
            S.dma(wst[i][:, 0:c1 - c0], src_ap[:, c0:c1], writes=(t_wst[i],))
            eng3 = (wcount[0] - 1) % 3
            if eng3 == 0:
                S.I("pool", "tensor_copy", (t_wst[i],), (t_dst,), out=dst_ap[:, c0:c1], in_=wst[i][:, 0:c1 - c0])
            elif eng3 == 1:
                S.I("act", "activation", (t_wst[i],), (t_dst,), out=dst_ap[:, c0:c1], in_=wst[i][:, 0:c1 - c0],
                    func=AF.Copy)
            else:
                S.I("dve", "tensor_copy", (t_wst[i],), (t_dst,), out=dst_ap[:, c0:c1], in_=wst[i][:, 0:c1 - c0])

    gate_blocks = {4: (0, 0), 5: (0, 1)}
    for blk in range(18):
        for k in range(8):
            load_cast(wada_b[:, k, :], w_ada[k * 128:(k + 1) * 128, blk * 512:(blk + 1) * 512], 512, t_wada)
        for cc in range(4):
            ch = blk * 4 + cc
            S.MM([("matmul", (p_a[:, cc * 32:cc * 32 + 17],),
                   dict(lhsT=wada_b[:, k, cc * 128:(cc + 1) * 128], rhs=sT[:, k, :], start=(k == 0), stop=(k == 7)))
                  for k in range(8)], (t_wada, t_const), (t_pa,))
            term, kk = ch // 8, ch % 8
            li_, kind = term // 3, term % 3
            if kind == 0:
                S.I("dve", "tensor_scalar", (t_pa, t_const), (t_AB,), out=AB[:, li_, 1, kk, :],
                    in0=p_a[:, cc * 32:cc * 32 + 17], scalar1=badaT_sb[:, ch:ch + 1], scalar2=None, op0=ALU.add)
            elif kind == 1:
                S.I("dve", "tensor_scalar", (t_pa, t_const), (t_AB,), out=AB[:, li_, 0, kk, :],
                    in0=p_a[:, cc * 32:cc * 32 + 17], scalar1=badaT_sb[:, ch:ch + 1], scalar2=1.0,
                    op0=ALU.add, op1=ALU.add)
                S.I("dve", "tensor_scalar", (t_AB, t_const), (t_AB,), out=AB[:, li_, 0, kk, :],
                    in0=AB[:, li_, 0, kk, :], scalar1=gT[:, li_, kk:kk + 1], scalar2=None, op0=ALU.mult)
        if blk in gate_blocks:
            li, half = gate_blocks[blk]
            S.dma(bada_blk[:], bada_row[0:1, blk * 512:(blk + 1) * 512].to_broadcast((17, 512)), writes=(t_bblk,))
            S.MM([("matmul", (p_b[0:17, :],), dict(lhsT=sT[:, k, :], rhs=wada_b[:, k, :], start=(k == 0), stop=(k == 7)))
                  for k in range(8)], (t_wada, t_const), (t_pb,))
            S.I("dve", "tensor_tensor", (t_pb, t_bblk), (t_grow,), out=grow[:], in0=p_b[0:17, :], in1=bada_blk[:],
                op=ALU.add)
            for which in range(2):
                S.MM([("matmul", (p_c[:, :],), dict(lhsT=esel_f[:, which, :], rhs=grow[:], start=True, stop=True))],
                     (t_grow, t_const), (t_pc,))
                S.I("act", "activation", (t_pc,), (t_gtab,), out=gtab[:, li, which, half * 512:(half + 1) * 512],
                    in_=p_c[:, :], func=AF.Copy)
    for k in range(8):
        load_cast(wg[:, k, :], w1g[k * 128:(k + 1) * 128, :], DFF, t_wg)
        load_cast(wu[:, k, :], w1u[k * 128:(k + 1) * 128, :], DFF, t_wu)
        load_cast(wuin[:, k, :], w_in[k * 128:(k + 1) * 128, 1536:2048], 512, t_wuin)
    for c in range(NFF):
        load_cast(wd[:, c, :], w1d[c * 128:(c + 1) * 128, :], D, t_wd)

    def norm_mod_T(nt, li, is_sample):
        for t in range(nt):
            S.I("act", "activation", (t_xt,), (t_xn, t_ssq), out=xn[:], in_=xt[:, t, :], func=AF.Square,
                accum_out=ssq[:, t:t + 1])
            S.I("act", "activation", (t_ssq, t_const), (t_rstd,), out=rstd[:, t:t + 1], in_=ssq[:, t:t + 1],
                func=AF.Sqrt, scale=1.0 / D, bias=eps_t[:, 0:1])
            S.I("dve", "reciprocal", (t_rstd,), (t_rstd,), out=rstd[:, 4 + t:5 + t], in_=rstd[:, t:t + 1])
        for t in range(nt):
            S.I("dve", "tensor_scalar", (t_xt, t_rstd), (t_xn,), out=xn[:], in0=xt[:, t, :],
                scalar1=rstd[:, 4 + t:5 + t], scalar2=None, op0=ALU.mult)
            S.MM([("transpose", (p_t[:, k * 128:(k + 1) * 128], xn[:, k * 128:(k + 1) * 128], ident_b[:]), {})
                  for k in range(8)], (t_xn, t_const), (t_pt,))
            for k in range(8):
                if not is_sample:
                    if k % 2 == 0:
                        S.I("dve", "tensor_scalar", (t_pt, t_AB), (t_hT,), out=hT[:, k, t * 128:(t + 1) * 128],
                            in0=p_t[:, k * 128:(k + 1) * 128], scalar1=AB[:, li, 0, k, 0:1],
                            scalar2=AB[:, li, 1, k, 0:1], op0=ALU.mult, op1=ALU.add)
                    else:
                        S.I("act", "activation", (t_pt, t_AB), (t_hT,), out=hT[:, k, t * 128:(t + 1) * 128],
                            in_=p_t[:, k * 128:(k + 1) * 128], func=AF.Identity, scale=AB[:, li, 0, k, 0:1],
                            bias=AB[:, li, 1, k, 0:1])
                else:
                    for b in range(SB):
                        S.I("dve", "tensor_scalar", (t_pt, t_AB), (t_hT,), out=hT[:, k, b * 8:(b + 1) * 8],
                            in0=p_t[:, k * 128 + b * 8:k * 128 + (b + 1) * 8], scalar1=AB[:, li, 0, k, 1 + b:2 + b],
                            scalar2=AB[:, li, 1, k, 1 + b:2 + b], op0=ALU.mult, op1=ALU.add)

    def ffn(nt, li, is_sample, filler=None):
        n = nt * 128
        norm_mod_T(nt, li, is_sample)
        for c in range(NFF):
            pg, tg = (p_a, t_pa) if c % 2 == 0 else (p_c, t_pc)
            pu, tu = (p_b, t_pb) if c % 2 == 0 else (p_d, t_pd)
            S.MM([("matmul", (pg[:, 0:n],), dict(lhsT=wg[:, k, c * 128:(c + 1) * 128], rhs=hT[:, k, 0:n],
                                                 start=(k == 0), stop=(k == 7))) for k in range(8)],
                 (t_wg, t_hT), (tg,))
            S.MM([("matmul", (pu[:, 0:n],), dict(lhsT=wu[:, k, c * 128:(c + 1) * 128], rhs=hT[:, k, 0:n],
                                                 start=(k == 0), stop=(k == 7))) for k in range(8)],
                 (t_wu, t_hT), (tu,))
            S.I("act", "activation", (tg,), (t_silu,), out=silu_t[:, 0:n], in_=pg[:, 0:n], func=AF.Silu)
            S.I("dve", "tensor_tensor", (t_silu, tu), (t_act,), out=actT[:, c, 0:n], in0=silu_t[:, 0:n],
                in1=pu[:, 0:n], op=ALU.mult)
            if filler is not None:
                filler(c)
        if filler is not None:
            filler(-1)
        which = 1 if is_sample else 0
        for t in range(nt):
            for half in range(2):
                pp, tp = (p_a, t_pa) if half == 0 else (p_c, t_pc)
                S.MM([("matmul", (pp[:, :],), dict(lhsT=actT[:, c, t * 128:(t + 1) * 128],
                                                   rhs=wd[:, c, half * 512:(half + 1) * 512],
                                                   start=(c == 0), stop=(c == NFF - 1))) for c in range(NFF)],
                     (t_act, t_wd), (tp,))
                S.I("dve", "tensor_tensor", (tp, t_gtab), (t_tmp,), out=tmp[:, half * 512:(half + 1) * 512],
                    in0=pp[:, :], in1=gtab[:, 0, which, half * 512:(half + 1) * 512], op=ALU.mult)
            S.I("dve", "scalar_tensor_tensor", (t_tmp, t_xt), (t_xt,), out=xt[:, t, :], in0=tmp[:], scalar=0.5,
                in1=xt[:, t, :], op0=ALU.mult, op1=ALU.add)

    def lp(q, k, j):
        return LP[:, q, k, j:j + 1]

    def ssm_bu_and_scan(q, n, bufi):
        j = q // 4
        S.MM([("matmul", (p_s1[:, 0:n],), dict(lhsT=lhsB[:, q, 0, :], rhs=uT[:, j, 0:n], start=True, stop=True))],
             (t_lhsB, t_uT), (t_ps1,))
        S.MM([("matmul", (p_s2[:, 0:n],), dict(lhsT=lhsB[:, q, 1, :], rhs=uT[:, j, 0:n], start=True, stop=True))],
             (t_lhsB, t_uT), (t_ps2,))
        bu = BU[bufi]
        tb = t_BU[bufi]
        S.I("act", "activation", (t_ps1,), (tb,), out=bu[:, 0, 0:n], in_=p_s1[:, 0:n], func=AF.Copy)
        S.I("act", "activation", (t_ps2,), (tb,), out=bu[:, 1, 0:n], in_=p_s2[:, 0:n], func=AF.Copy)
        for s in range(1, 8):
            cur_re, cur_im = bu[:, 0, s:n:8], bu[:, 1, s:n:8]
            prv_re, prv_im = bu[:, 0, s - 1:n:8], bu[:, 1, s - 1:n:8]
            S.I("dve", "scalar_tensor_tensor", (tb, t_LP), (tb,), out=cur_re, in0=prv_re, scalar=lp(q, 1, 0),
                in1=cur_re, op0=ALU.mult, op1=ALU.add)
            S.I("dve", "scalar_tensor_tensor", (tb, t_LP), (tb,), out=cur_re, in0=prv_im, scalar=lp(q, 1, 2),
                in1=cur_re, op0=ALU.mult, op1=ALU.add)
            S.I("dve", "scalar_tensor_tensor", (tb, t_LP), (tb,), out=cur_im, in0=prv_re, scalar=lp(q, 1, 1),
                in1=cur_im, op0=ALU.mult, op1=ALU.add)
            S.I("dve", "scalar_tensor_tensor", (tb, t_LP), (tb,), out=cur_im, in0=prv_im, scalar=lp(q, 1, 0),
                in1=cur_im, op0=ALU.mult, op1=ALU.add)

    def ssm_tree_two(q0, n):
        banks = ((p_s1, t_ps1), (p_s2, t_ps2))
        for i in range(2):
            q = q0 + i
            j = q // 4
            pp, tp = banks[i]
            for ri in range(2):
                S.MM([("matmul", (pp[:, ri * n:(ri + 1) * n],), dict(lhsT=lhsB[:, q, ri, :], rhs=uT[:, j, 0:n],
                                                                      start=True, stop=True))], (t_lhsB, t_uT), (tp,))
            S.I("act", "activation", (tp,), (t_BU[i],), out=BU[i][:, :, 0:n],
                in_=pp[:, 0:2 * n].rearrange("p (r n) -> p r n", r=2), func=AF.Copy)
        for (st, k) in ((2, 1), (4, 2), (8, 4)):
            views = []
            for i in range(2):
                bu = BU[i]
                views.append((bu[:, 0, st - 1:n:st], bu[:, 1, st - 1:n:st], bu[:, 0, st // 2 - 1:n:st], bu[:, 1, st // 2 - 1:n:st]))
            for part in range(2):
                for i in range(2):
                    q = q0 + i
                    cre, cim, pre, pim = views[i]
                    if part == 0:
                        S.I("dve", "scalar_tensor_tensor", (t_BU[i], t_LP), (t_BU[i],), out=cre, in0=pre, scalar=lp(q, k, 0),
                            in1=cre, op0=ALU.mult, op1=ALU.add)
                        S.I("dve", "scalar_tensor_tensor", (t_BU[i], t_LP), (t_BU[i],), out=cim, in0=pre, scalar=lp(q, k, 1),
                            in1=cim, op0=ALU.mult, op1=ALU.add)
                    else:
                        S.I("dve", "scalar_tensor_tensor", (t_BU[i], t_LP), (t_BU[i],), out=cre, in0=pim, scalar=lp(q, k, 2),
                            in1=cre, op0=ALU.mult, op1=ALU.add)
                        S.I("dve", "scalar_tensor_tensor", (t_BU[i], t_LP), (t_BU[i],), out=cim, in0=pim, scalar=lp(q, k, 0),
                            in1=cim, op0=ALU.mult, op1=ALU.add)

    def ssm_full_two(q0, n, nsc):
        banks = ((p_s1, t_ps1), (p_s2, t_ps2))
        for i in range(2):
            q = q0 + i
            j = q // 4
            pp, tp = banks[i]
            for ri in range(2):
                S.MM([("matmul", (pp[:, ri * n:(ri + 1) * n],), dict(lhsT=lhsB[:, q, ri, :], rhs=uT[:, j, 0:n],
                                                                      start=True, stop=True))], (t_lhsB, t_uT), (tp,))
            S.I("act", "activation", (tp,), (t_BU[i],), out=BU[i][:, :, 0:n],
                in_=pp[:, 0:2 * n].rearrange("p (r n) -> p r n", r=2), func=AF.Copy)
        for tau in range(8):
            for part in range(2):
                for i in range(2):
                    q = q0 + i
                    bu = BU[i]
                    cre, cim = bu[:, 0, tau:n:8], bu[:, 1, tau:n:8]
                    if tau == 0:
                        pre, pim = XS[:, 0, q, 0:nsc], XS[:, 1, q, 0:nsc]
                        rd = (t_BU[i], t_LP, t_XS)
                    else:
                        pre, pim = bu[:, 0, tau - 1:n:8], bu[:, 1, tau - 1:n:8]
                        rd = (t_BU[i], t_LP)
                    if part == 0:
                        S.I("dve", "scalar_tensor_tensor", rd, (t_BU[i],), out=cre, in0=pre, scalar=lp(q, 1, 0),
                            in1=cre, op0=ALU.mult, op1=ALU.add)
                        S.I("dve", "scalar_tensor_tensor", rd, (t_BU[i],), out=cim, in0=pre, scalar=lp(q, 1, 1),
                            in1=cim, op0=ALU.mult, op1=ALU.add)
                    else:
                        S.I("dve", "scalar_tensor_tensor", rd, (t_BU[i],), out=cre, in0=pim, scalar=lp(q, 1, 2),
                            in1=cre, op0=ALU.mult, op1=ALU.add)
                        S.I("dve", "scalar_tensor_tensor", rd, (t_BU[i],), out=cim, in0=pim, scalar=lp(q, 1, 0),
                            in1=cim, op0=ALU.mult, op1=ALU.add)
        for i in range(2):
            S.I("pool", "tensor_copy", (t_BU[i], t_Hbf), (t_Hbf,), out=Hbf[:, (q0 + i) % 4, :, 0:n], in_=BU[i][:, :, 0:n])

    a8re = LP[:, :, 8, 0]
    a8im = LP[:, :, 8, 1]

    bufc = [0]

    XS2 = tmp[:].rearrange("p (r q j) -> p r q j", r=2, q=16)
    tA = BU[0][:].rearrange("p r n -> p (r n)")

    def level_b(nsc):
        assert nsc == 32
        c_re, c_im = XS[:, 0, :, 0], XS[:, 1, :, 0]
        S.I("dve", "tensor_tensor", (t_XS, t_LP), (t_t4,), out=t4[:, 0], in0=c_re, in1=a8re, op=ALU.mult)
        S.I("dve", "tensor_tensor", (t_XS, t_LP), (t_t4,), out=t4[:, 1], in0=c_im, in1=a8im, op=ALU.mult)
        S.I("dve", "tensor_tensor", (t_XS, t_LP), (t_t4,), out=t4[:, 2], in0=c_im, in1=a8re, op=ALU.mult)
        S.I("dve", "tensor_tensor", (t_XS, t_LP), (t_t4,), out=t4[:, 3], in0=c_re, in1=a8im, op=ALU.mult)
        S.I("dve", "tensor_tensor", (t_t4,), (t_t4,), out=t4[:, 0], in0=t4[:, 0], in1=t4[:, 1], op=ALU.subtract)
        S.I("dve", "tensor_tensor", (t_t4,), (t_t4,), out=t4[:, 2], in0=t4[:, 2], in1=t4[:, 3], op=ALU.add)
        S.I("dve", "tensor_tensor", (t_t4, t_XS), (t_XS,), out=XS[:, 0, :, 1], in0=t4[:, 0], in1=XS[:, 0, :, 1], op=ALU.add)
        S.I("dve", "tensor_tensor", (t_t4, t_XS), (t_XS,), out=XS[:, 1, :, 1], in0=t4[:, 2], in1=XS[:, 1, :, 1], op=ALU.add)
        src, t_src = XS[:, :, :, 1:33], t_XS
        dst, t_dst = XS2, t_tmp
        for m, d in enumerate((1, 2, 4, 8, 16)):
            w = 32 - d
            are = LPB[:, m, 0, :].unsqueeze(2).to_broadcast((128, 16, w))
            aim = LPB[:, m, 1, :].unsqueeze(2).to_broadcast((128, 16, w))
            tAv = tA[:, 0:16 * w].rearrange("p (q j) -> p q j", q=16)
            s_re_lo, s_im_lo = src[:, 0, :, 0:w], src[:, 1, :, 0:w]
            s_re_hi, s_im_hi = src[:, 0, :, d:32], src[:, 1, :, d:32]
            d_re_hi, d_im_hi = dst[:, 0, :, d:32], dst[:, 1, :, d:32]
            S.I("dve", "tensor_copy", (t_src, t_dst), (t_dst,), out=dst[:, :, :, 0:d], in_=src[:, :, :, 0:d])
            S.I("dve", "tensor_tensor", (t_src, t_LP, t_dst), (t_dst,), out=d_re_hi, in0=s_re_lo, in1=are, op=ALU.mult)
            S.I("dve", "tensor_tensor", (t_src, t_LP, t_dst), (t_dst,), out=d_im_hi, in0=s_im_lo, in1=are, op=ALU.mult)
            S.I("dve", "tensor_tensor", (t_src, t_LP, t_BU[0]), (t_BU[0],), out=tAv, in0=s_im_lo, in1=aim, op=ALU.mult)
            S.I("dve", "tensor_tensor", (t_src, t_dst), (t_dst,), out=d_re_hi, in0=d_re_hi, in1=s_re_hi, op=ALU.add)
            S.I("dve", "tensor_tensor", (t_src, t_dst), (t_dst,), out=d_im_hi, in0=d_im_hi, in1=s_im_hi, op=ALU.add)
            S.I("dve", "tensor_tensor", (t_BU[0], t_dst), (t_dst,), out=d_re_hi, in0=d_re_hi, in1=tAv, op=ALU.subtract)
            S.I("dve", "tensor_tensor", (t_src, t_LP, t_BU[0]), (t_BU[0],), out=tAv, in0=s_re_lo, in1=aim, op=ALU.mult)
            S.I("dve", "tensor_tensor", (t_BU[0], t_dst), (t_dst,), out=d_im_hi, in0=d_im_hi, in1=tAv, op=ALU.add)
            src, t_src, dst, t_dst = dst, t_dst, src, t_src
        S.I("dve", "tensor_copy", (t_tmp, t_XS), (t_XS,), out=XS[:, :, :, 1:33], in_=XS2)

    pending = []

    def filler(c):
        if c == -1:
            while pending:
                pending.pop(0)()
        elif c % 2 == 1 and pending:
            pending.pop(0)()

    def ssm_items(bi):
        items = []
        for q0 in range(0, 16, 2):
            def pp_item(q0=q0):
                ssm_tree_two(q0, BT)
                for i in range(2):
                    S.I("pool", "tensor_copy", (t_BU[i], t_XS), (t_XS,), out=XS[:, :, q0 + i, 1:NSC + 1],
                        in_=BU[i][:, :, 7:BT:8])
            items.append(pp_item)

        def lb_item(bi=bi):
            level_b(NSC)
            if bi >= OWN0:
                S.dma(xss[:, :, :, (bi - OWN0) * NSC:(bi - OWN0 + 1) * NSC], XS[:, :, :, 0:NSC], reads=(t_XS,),
                      sem_t=t_XS, slow=True)
            if bi == NBLK - 1:
                S.dma(o_hp, XS[:, :, :, NSC], reads=(t_XS,), sem_t=t_XS, slow=True)
            S.I("dve", "tensor_copy", (t_XS,), (t_XS,), out=XS[:, :, :, 0], in_=XS[:, :, :, NSC])
        items.append(lb_item)
        return items

    for bi in range(NBLK):
        S.dma(xt[:], x_seq[bi * BT:(bi + 1) * BT, :].rearrange("(t p) d -> p t d", p=128), writes=(t_xt,))
        ffn(NTB, 0, False, filler)
        if bi >= HALO0:
            S.dma(x1s[(bi - HALO0) * BT:(bi - HALO0 + 1) * BT, :].rearrange("(t p) d -> p t d", p=128), xt[:],
                  reads=(t_xt,), writes=(t_x1s,), sem_t=t_xt)
        norm_mod_T(NTB, 1, False)
        for j in range(4):
            pp, tp = (p_a, t_pa) if j % 2 == 0 else (p_c, t_pc)
            S.MM([("matmul", (pp[:, 0:BT],), dict(lhsT=wuin[:, k, j * 128:(j + 1) * 128], rhs=hT[:, k, :],
                                                  start=(k == 0), stop=(k == 7))) for k in range(8)],
                 (t_wuin, t_hT), (tp,))
            S.I("dve", "tensor_scalar", (tp, t_const), (t_uT,), out=uT[:, j, :], in0=pp[:, 0:BT],
                scalar1=blkvalid[:, bi:bi + 1], scalar2=None, op0=ALU.mult)
        if bi >= OWN0:
            S.dma(uTs[:, :, (bi - OWN0) * BT:(bi - OWN0 + 1) * BT], uT[:], reads=(t_uT,), sem_t=t_uT)
        pending.extend(ssm_items(bi))
    filler(-1)

    S.dma(xt[:, 0, :], x_smp, writes=(t_xt,))
    ffn(1, 0, True)
    S.dma(x1s[32 * 128:33 * 128, :], xt[:, 0, :], reads=(t_xt,), writes=(t_x1s,), sem_t=t_xt)
    norm_mod_T(1, 1, True)
    for j in range(4):
        pp, tp = (p_a, t_pa) if j % 2 == 0 else (p_c, t_pc)
        S.MM([("matmul", (pp[:, 0:128],), dict(lhsT=wuin[:, k, j * 128:(j + 1) * 128], rhs=hT[:, k, 0:128],
                                               start=(k == 0), stop=(k == 7))) for k in range(8)],
             (t_wuin, t_hT), (tp,))
        S.I("dve", "tensor_copy", (tp,), (t_uT,), out=uT[:, j, 0:128], in_=pp[:, 0:128])
    S.dma(uTs[:, :, TOK_CORE:TOK_CORE + 128], uT[:, :, 0:128], reads=(t_uT,), sem_t=t_uT)
    for q in range(16):
        bufi = bufc[0] % 2
        bufc[0] += 1
        ssm_bu_and_scan(q, 128, bufi)
        S.I("pool", "tensor_copy", (t_BU[bufi], t_XS), (t_XS,), out=XS[:, :, q, 16:32], in_=BU[bufi][:, :, 7:128:8])
    S.dma(h0[:, 0], h0re_in, writes=(t_h0,))
    S.dma(h0[:, 1], h0im_in, writes=(t_h0,))
    a8re_b = LP[:, :, 8, 0].unsqueeze(2).to_broadcast((128, 16, SB))
    a8im_b = LP[:, :, 8, 1].unsqueeze(2).to_broadcast((128, 16, SB))
    S.I("dve", "tensor_tensor", (t_h0, t_LP), (t_hend,), out=hend[:, 0], in0=h0[:, 0], in1=a8re_b, op=ALU.mult)
    S.I("dve", "tensor_tensor", (t_h0, t_LP), (t_hend,), out=hend[:, 1], in0=h0[:, 1], in1=a8re_b, op=ALU.mult)
    S.I("dve", "tensor_tensor", (t_h0, t_LP, t_XS), (t_XS,), out=XS[:, 0, :, 0:16], in0=h0[:, 1], in1=a8im_b, op=ALU.mult)
    S.I("dve", "tensor_tensor", (t_h0, t_LP, t_XS), (t_XS,), out=XS[:, 1, :, 0:16], in0=h0[:, 0], in1=a8im_b, op=ALU.mult)
    S.I("dve", "tensor_tensor", (t_hend, t_XS), (t_hend,), out=hend[:, 0], in0=hend[:, 0], in1=XS[:, 0, :, 0:16],
        op=ALU.subtract)
    S.I("dve", "tensor_tensor", (t_hend, t_XS), (t_hend,), out=hend[:, 1], in0=hend[:, 1], in1=XS[:, 1, :, 0:16],
        op=ALU.add)
    S.I("dve", "tensor_tensor", (t_hend, t_XS), (t_hend,), out=hend[:, 0], in0=hend[:, 0], in1=XS[:, 0, :, 16:32],
        op=ALU.add)
    S.I("dve", "tensor_tensor", (t_hend, t_XS), (t_hend,), out=hend[:, 1], in0=hend[:, 1], in1=XS[:, 1, :, 16:32],
        op=ALU.add)
    S.dma(o_hs, hend, reads=(t_hend,), sem_t=t_hend)

    t_arena = T("arena")

    def AV(off, n):
        return arena[:, off:off + n]

    NTOK = TOK_CORE + 128
    VE = 80
    QTz = AV(0, 8 * NTOK).rearrange("p (h n) -> p h n", h=NH)
    catA = QTz[:, 0:4, :]
    KT = AV(17408, 4 * NSP * 128).rearrange("p (c n) -> p c n", c=4)
    VX = AV(34304, NSP * NH * VE).rearrange("p (t h e) -> p t h e", t=NSP, h=NH)
    WQKV = AV(55424, 8 * 1024).rearrange("p (k n) -> p k n", k=8)
    MS = AV(63616, 17 * 128).rearrange("p (t b i) -> p t b i", t=17, b=16)
    Es = AV(65792, 17 * 64).rearrange("p (t h i) -> p t h i", t=17, h=NH)
    catS = AV(8704, 4 * NTOK).rearrange("p (c n) -> p c n", c=4)
    lhsC = AV(17408, 4096).rearrange("p (q r n) -> p q r n", q=16, r=2)
    wglu = AV(21504, 4096).rearrange("p (k n) -> p k n", k=4)
    gTs = AV(25600, 4 * NTOK).rearrange("p (c n) -> p c n", c=4)
    Hbf = AV(34304, 2048).rearrange("p (q r n) -> p q r n", q=4, r=2)
    wout = AV(36352, 8192).rearrange("p (k n) -> p k n", k=8)
    KTs = AV(55424, 8192).rearrange("p (c n) -> p c n", c=4)
    VXs = AV(34304, 16 * NH * VE).rearrange("p (t h e) -> p t h e", t=16, h=NH)
    Ppad = [AV(44544 + i * 2176, 2176).rearrange("p (t m) -> p t m", t=17) for i in range(2)]
    actflat = actT[:].rearrange("p c n -> p (c n)")
    Mk = actflat[:, 0:17 * 128].rearrange("p (d q) -> p d q", d=17)
    Pb = [actflat[:, 2176 + i * 512:2176 + (i + 1) * 512] for i in range(2)]

    mask_in = din("maskp", [128, 17, 128])
    masks_in = din("masks", [128, 17, 16, 8])
    halo_valid_in = din("halo_valid", [128, 1])
    gqk_in = din("gqk", [1, 2 * HD])
    rope_in = din("rope", [128, NSP, 2, 8])
    cexre_in = din("ssm_cre", [128, 16, 128])
    cexim_in = din("ssm_cim", [128, 16, 128])
    w_glu_in = din("w_glu", [512, 1024])
    w_out_in = din("w_out", [D, D])
    w2g = din("w2_gate", [D, DFF])
    w2u = din("w2_up", [D, DFF])
    w2d = din("w2_down", [DFF, D])
    o_kp = dout("o_kp", [TOK_CORE, 512])
    o_vp = dout("o_vp", [TOK_CORE, 512])
    o_yp = dout("o_yp", [TOK_CORE, D])
    o_ys = dout("o_ys", [128, D])

    rope = sb("rope_sb", [128, NSP, 2, 8])
    gqk = sig[:].rearrange("p k n -> p (k n)")[:, 0:2 * HD]
    cTf = cT[:].rearrange("p k n -> p (k n)")
    hv = cTf[:, 0:1]
    kss = cTf[:, 8:16]
    krs = cTf[:, 16:24]
    rt = PR[:].rearrange("p s n -> p (s n)")[:, 0:256].rearrange("p (a h d) -> p a h d", a=4, h=8)
    t_rope, t_kss, t_krs, t_rt, t_MS, t_Mk = T("rope"), T("kss"), T("krs"), T("rt"), T("MS"), T("Mk")
    t_catA, t_QT, t_KT, t_VX, t_WQKV = T("catA"), T("QT"), T("KT"), T("VX"), T("WQKV")
    t_Pb = [T("Pb0"), T("Pb1")]
    t_Ppad = [T("Pp0"), T("Pp1")]
    t_KTs, t_Es, t_VXs = T("KTs"), T("Es"), T("VXs")

    def barrier_on(trackers, engines=ENG):
        for tt in trackers:
            for e in engines:
                S._wait(e, tt.w)
                for d in tt.r:
                    S._wait(e, d)
    t_catS, t_lhsC, t_wglu, t_gTs, t_Hbf, t_wout = T("catS"), T("lhsC"), T("wglu"), T("gTs"), T("Hbf"), T("wout")
    t_out = T("outst")

    for tt in (t_wg, t_wu, t_wd, t_act):
        for e in ("pe", "act", "dve", "pool", "sp"):
            S._wait(e, tt.w)
            for d in tt.r:
                S._wait(e, d)

    S.dma(rope[:], rope_in, writes=(t_rope,))
    S.dma(gqk, gqk_in.to_broadcast((128, 2 * HD)), writes=(t_rope,))
    S.dma(hv, halo_valid_in, writes=(t_rope,))
    for d0 in range(0, 17, 8):
        d1 = min(17, d0 + 8)
        S.dma(tmp[:, 0:(d1 - d0) * 128], mask_in[:, d0:d1, :].rearrange("p d q -> p (d q)"), writes=(t_tmp,))
        S.I("dve", "tensor_copy", (t_tmp,), (t_Mk,), out=Mk[:, d0:d1, :].rearrange("p d q -> p (d q)"),
            in_=tmp[:, 0:(d1 - d0) * 128])
    for t0 in range(0, 17, 8):
        t1 = min(17, t0 + 8)
        S.dma(tmp[:, 0:(t1 - t0) * 128], masks_in[:, t0:t1].rearrange("p t b i -> p (t b i)"), writes=(t_tmp,))
        S.I("dve", "tensor_copy", (t_tmp,), (t_MS,), out=MS[:, t0:t1].rearrange("p t b i -> p (t b i)"),
            in_=tmp[:, 0:(t1 - t0) * 128])

    def ada_gate_table(li, stage_ap, t_stage):
        for half in range(2):
            blk = (li * 3 + 2) * 2 + half
            for k in range(8):
                load_cast(stage_ap[:, k, :], w_ada[k * 128:(k + 1) * 128, blk * 512:(blk + 1) * 512], 512, t_stage)
            S.dma(bada_blk, bada_row[0:1, blk * 512:(blk + 1) * 512].to_broadcast((17, 512)), writes=(t_bblk,))
            S.MM([("matmul", (p_b[0:17, :],), dict(lhsT=sT[:, k, :], rhs=stage_ap[:, k, :], start=(k == 0), stop=(k == 7)))
                  for k in range(8)], (t_stage, t_const), (t_pb,))
            S.I("dve", "tensor_tensor", (t_pb, t_bblk), (t_grow,), out=grow, in0=p_b[0:17, :], in1=bada_blk, op=ALU.add)
            for which in range(2):
                S.MM([("matmul", (p_c[:, :],), dict(lhsT=esel_f[:, which, :], rhs=grow, start=True, stop=True))],
                     (t_grow, t_const), (t_pc,))
                S.I("act", "activation", (t_pc,), (t_gtab,), out=gtab[:, 0, which, half * 512:(half + 1) * 512],
                    in_=p_c[:, :], func=AF.Copy)

    stage_wq = WQKV[:, :, 0:512]
    ada_gate_table(1, stage_wq, t_WQKV)
    for k in range(8):
        load_cast(WQKV[:, k, :], w_in[k * 128:(k + 1) * 128, 512:1536], 1024, t_WQKV)
    S.I("dve", "memset", (t_VX,), (t_VX,), VX[:, :, :, 64:65], 1.0)
    S.I("pool", "memset", (t_QT,), (t_QT,), QTz, 0.0)

    def qk_norm_rope(buf, gsl, ti, tb):
        b3 = buf.rearrange("p (h d) -> p h d", h=NH)
        S.I("act", "activation", (tb,), (t_tmp,), out=tmp[:, 0:512], in_=buf, func=AF.Square)
        S.I("dve", "tensor_reduce", (t_tmp,), (t_kss,), out=kss, in_=tmp[:, 0:512].rearrange("p (h d) -> p h d", h=NH),
            axis=mybir.AxisListType.X, op=ALU.add)
        S.I("act", "activation", (t_kss, t_const), (t_krs,), out=krs, in_=kss, func=AF.Sqrt, scale=1.0 / HD,
            bias=eps_t[:, 0:1])
        S.I("dve", "reciprocal", (t_krs,), (t_krs,), out=krs, in_=krs)
        S.I("dve", "tensor_tensor", (tb, t_krs), (tb,), out=b3, in0=b3,
            in1=krs.unsqueeze(2).to_broadcast((128, NH, HD)), op=ALU.mult)
        S.I("dve", "tensor_tensor", (tb, t_rope), (tb,), out=b3, in0=b3,
            in1=gqk[:, gsl].unsqueeze(1).to_broadcast((128, NH, HD)), op=ALU.mult)
        cosb = rope[:, ti, 0, :].unsqueeze(1).to_broadcast((128, NH, 8))
        sinb = rope[:, ti, 1, :].unsqueeze(1).to_broadcast((128, NH, 8))
        xa, xb = b3[:, :, 0:8], b3[:, :, 8:16]
        S.I("dve", "tensor_tensor", (tb, t_rope), (t_rt,), out=rt[:, 0], in0=xa, in1=cosb, op=ALU.mult)
        S.I("dve", "tensor_tensor", (tb, t_rope), (t_rt,), out=rt[:, 1], in0=xb, in1=sinb, op=ALU.mult)
        S.I("dve", "tensor_tensor", (tb, t_rope), (t_rt,), out=rt[:, 2], in0=xb, in1=cosb, op=ALU.mult)
        S.I("dve", "tensor_tensor", (tb, t_rope), (t_rt,), out=rt[:, 3], in0=xa, in1=sinb, op=ALU.mult)
        S.I("dve", "tensor_tensor", (t_rt, tb), (tb,), out=xa, in0=rt[:, 0], in1=rt[:, 1], op=ALU.subtract)
        S.I("dve", "tensor_tensor", (t_rt, tb), (tb,), out=xb, in0=rt[:, 2], in1=rt[:, 3], op=ALU.add)

    def to_featmajor(buf, tb, scale):
        S.I("dve", "tensor_scalar", (tb,), (t_xn,), out=xn[:, 0:512], in0=buf, scalar1=scale, scalar2=None, op0=ALU.mult)
        S.MM([("transpose", (p_t[:, c * 128:(c + 1) * 128], xn[:, c * 128:(c + 1) * 128], ident_b[:]), {})
              for c in range(4)], (t_xn, t_const), (t_pt,))
        return p_t[:, 0:512].rearrange("p (c n) -> p c n", c=4)

    kbuf = BU[0][:].rearrange("p r n -> p (r n)")
    vbuf = BU[1][:].rearrange("p r n -> p (r n)")
    for ti in range(NSP if STOP >= 1 else 0):
        is_sample = ti == 32
        is_own = 16 <= ti < 32
        S.dma(xt[:, 0, :], x1s[ti * 128:(ti + 1) * 128, :], reads=(t_x1s,), writes=(t_xt,))
        norm_mod_T(1, 1, is_sample)
        S.MM([("matmul", (p_a[:, :],), dict(lhsT=hT[:, k, 0:128], rhs=WQKV[:, k, 512:1024], start=(k == 0), stop=(k == 7)))
              for k in range(8)], (t_hT, t_WQKV), (t_pa,))
        S.I("act", "activation", (t_pa,), (t_BU[1],), out=vbuf, in_=p_a[:, :], func=AF.Copy)
        if ti < 16:
            S.I("dve", "tensor_scalar", (t_BU[1], t_rope, t_VX), (t_VX,), out=VX[:, ti, :, 0:64],
                in0=vbuf.rearrange("p (h d) -> p h d", h=NH), scalar1=hv, scalar2=None, op0=ALU.mult)
            S.I("dve", "tensor_scalar", (t_rope, t_VX), (t_VX,), out=VX[:, ti, :, 64:65], in0=VX[:, ti, :, 64:65],
                scalar1=hv, scalar2=None, op0=ALU.mult)
        else:
            S.I("dve", "tensor_copy", (t_BU[1], t_VX), (t_VX,), out=VX[:, ti, :, 0:64],
                in_=vbuf.rearrange("p (h d) -> p h d", h=NH))
        if is_own:
            S.dma(o_vp[(ti - 16) * 128:(ti - 15) * 128, :], vbuf, reads=(t_BU[1],), sem_t=t_BU[1])
        if is_sample:
            for b in range(SB):
                S.dma(o_vs[b, WBUF - 8:WBUF, :], BU[1][b * 8:(b + 1) * 8].rearrange("p r n -> p (r n)"),
                      reads=(t_BU[1],), sem_t=t_BU[1])
        S.MM([("matmul", (p_c[:, :],), dict(lhsT=hT[:, k, 0:128], rhs=WQKV[:, k, 0:512], start=(k == 0), stop=(k == 7)))
              for k in range(8)], (t_hT, t_WQKV), (t_pc,))
        S.I("act", "activation", (t_pc,), (t_BU[0],), out=kbuf, in_=p_c[:, :], func=AF.Copy)
        qk_norm_rope(kbuf, slice(HD, 2 * HD), ti, t_BU[0])
        if is_own:
            S.dma(o_kp[(ti - 16) * 128:(ti - 15) * 128, :], kbuf, reads=(t_BU[0],), sem_t=t_BU[0])
        if is_sample:
            for b in range(SB):
                S.dma(o_ks[b, WBUF - 8:WBUF, :], BU[0][b * 8:(b + 1) * 8].rearrange("p r n -> p (r n)"),
                      reads=(t_BU[0],), sem_t=t_BU[0])
        src = to_featmajor(kbuf, t_BU[0], 1.0)
        S.I("act", "activation", (t_pt, t_KT), (t_KT,), out=KT[:, :, ti * 128:(ti + 1) * 128], in_=src, func=AF.Copy)
    if STOP >= 1:
        for k in range(8):
            load_cast(WQKV[:, k, 0:512], w_in[k * 128:(k + 1) * 128, 0:512], 512, t_WQKV)
    for ti in range(16, NSP if STOP >= 1 else 16):
        is_sample = ti == 32
        S.dma(xt[:, 0, :], x1s[ti * 128:(ti + 1) * 128, :], reads=(t_x1s,), writes=(t_xt,))
        norm_mod_T(1, 1, is_sample)
        S.MM([("matmul", (p_b[:, :],), dict(lhsT=hT[:, k, 0:128], rhs=WQKV[:, k, 0:512], start=(k == 0), stop=(k == 7)))
              for k in range(8)], (t_hT, t_WQKV), (t_pb,))
        S.I("act", "activation", (t_pb,), (t_BU[0],), out=kbuf, in_=p_b[:, :], func=AF.Copy)
        qk_norm_rope(kbuf, slice(0, HD), ti, t_BU[0])
        src = to_featmajor(kbuf, t_BU[0], HD ** -0.5)
        cs_ = slice((ti - 16) * 128, (ti - 15) * 128)
        S.I("act", "activation", (t_pt, t_QT), (t_QT,), out=QTz[0:64, 0:NH:2, cs_], in_=src[0:64], func=AF.Copy)
        S.I("act", "activation", (t_pt, t_QT), (t_QT,), out=QTz[64:128, 1:NH:2, cs_], in_=src[64:128], func=AF.Copy)

    def attn_finish(a, num_den, t_nd, obuf, t_ob):
        for g in range(2):
            acc, tacc = num_den(g), t_nd[g]
            S.I("dve", "reciprocal", (tacc,), (t_kss,), out=kss[:, 4 * g:4 * g + 4], in_=acc[:, :, 64])
            S.I("dve", "tensor_tensor", (tacc, t_kss), (t_ob,),
                out=obuf[:, g * 256:(g + 1) * 256].rearrange("p (h d) -> p h d", h=4), in0=acc[:, :, 0:64],
                in1=kss[:, 4 * g:4 * g + 4].unsqueeze(2).to_broadcast((128, 4, HD)), op=ALU.mult)
        src = to_featmajor(obuf, t_ob, 1.0)
        S.I("act", "activation", (t_pt, t_QT), (t_catA, t_QT), out=catA[:, :, a * 128:(a + 1) * 128], in_=src, func=AF.Copy)

    def acc_view(g):
        return (p_s1 if g == 0 else p_s2)[:, 0:4 * VE].rearrange("p (h e) -> p h e", h=4)

    sbanks = ((p_a, t_pa), (p_b, t_pb), (p_c, t_pc), (p_d, t_pd))
    it = [0]
    for a in range(16 if STOP >= 2 else 0):
        for h in range(NH):
            c = h // 2
            pacc, tacc = (p_s1, t_ps1) if h < 4 else (p_s2, t_ps2)
            hh = h % 4
            for d0 in range(0, 17, 4):
                dls = list(range(d0, min(17, d0 + 4)))
                nd = len(dls)
                pbk, tbk = sbanks[it[0] % 4]
                i2 = it[0] % 2
                it[0] += 1
                S.MM([("matmul", (pbk[:, i * 128:(i + 1) * 128],),
                       dict(lhsT=KT[:, c, (16 + a - dl) * 128:(17 + a - dl) * 128], rhs=QTz[:, h, a * 128:(a + 1) * 128],
                            start=True, stop=True)) for i, dl in enumerate(dls)], (t_KT, t_QT), (tbk,))
                S.I("act", "activation", (tbk,), (t_Pb[i2],), out=Pb[i2][:, 0:nd * 128], in_=pbk[:, 0:nd * 128], func=AF.Exp)
                S.I("dve", "tensor_tensor", (t_Pb[i2], t_Mk), (t_Pb[i2],), out=Pb[i2][:, 0:nd * 128], in0=Pb[i2][:, 0:nd * 128],
                    in1=Mk[:, d0:d0 + nd, :].rearrange("p d q -> p (d q)"), op=ALU.mult)
                S.MM([("matmul", (pacc[:, hh * VE:hh * VE + 65],),
                       dict(lhsT=Pb[i2][:, i * 128:(i + 1) * 128], rhs=VX[:, 16 + a - dl, h, 0:65],
                            start=(dl == 0), stop=(dl == 16))) for i, dl in enumerate(dls)], (t_Pb[i2], t_VX), (tacc,))
        attn_finish(a, acc_view, (t_ps1, t_ps2), tmp[:, 0:512], t_tmp)

    barrier_on((t_WQKV, t_VX, t_Pb[0], t_Pb[1]))
    Os = tmp[:, 0:NH * VE].rearrange("p (h e) -> p h e", h=NH)
    S.I("dve", "memset", (t_VXs,), (t_VXs,), VXs[:, :, :, 64:65], 1.0)
    stg = [xt[:, 0, 0:512], xt[:, 0, 512:1024], xt[:, 1, 0:512], xt[:, 1, 512:1024]]
    t_stg = [T("stg0"), T("stg1"), T("stg2"), T("stg3")]
    for tsx in t_stg:
        tsx.w = t_xt.w
        tsx.r = list(t_xt.r)
    sc = [0]
    pp_i = [0]
    for b in range(SB if STOP >= 3 else 0):
        for t in range(16):
            si = sc[0] % 4
            sc[0] += 1
            S.dma(stg[si], ck[b, t * 128:(t + 1) * 128, :], writes=(t_stg[si],))
            S.MM([("transpose", (p_a[:, c * 128:(c + 1) * 128], stg[si][:, c * 128:(c + 1) * 128], ident_f[:]), {})
                  for c in range(4)], (t_stg[si], t_const), (t_pa,))
            S.I("act", "activation", (t_pa, t_KTs), (t_KTs,), out=KTs[:, :, t * 128:(t + 1) * 128],
                in_=p_a[:, :].rearrange("p (c n) -> p c n", c=4), func=AF.Copy)
            si = sc[0] % 4
            sc[0] += 1
            S.dma(stg[si], cv[b, t * 128:(t + 1) * 128, :], writes=(t_stg[si],))
            S.I("pool", "tensor_copy", (t_stg[si], t_VXs), (t_VXs,), out=VXs[:, t, :, 0:64],
                in_=stg[si].rearrange("p (h d) -> p h d", h=NH))
        qs = slice(TOK_CORE + b * 8, TOK_CORE + b * 8 + 8)
        for (pbk, tbk, t_lo, t_hi) in ((p_b, t_pb, 0, 8), (p_c, t_pc, 8, 16), (p_d, t_pd, 16, 17)):
            items = []
            for t in range(t_lo, t_hi):
                for h in range(NH):
                    c = h // 2
                    lhs = KTs[:, c, t * 128:(t + 1) * 128] if t < 16 else KT[:, c, 32 * 128:33 * 128]
                    col = ((t - t_lo) * NH + h) * 8
                    items.append(("matmul", (pbk[:, col:col + 8],), dict(lhsT=lhs, rhs=QTz[:, h, qs], start=True, stop=True)))
            S.MM(items, (t_KTs, t_KT, t_QT), (tbk,))
            ncol = (t_hi - t_lo) * 64
            S.I("act", "activation", (tbk, t_Es), (t_Es,), out=Es[:, t_lo:t_hi].rearrange("p t h i -> p (t h i)"),
                in_=pbk[:, 0:ncol], func=AF.Exp)
        for h in range(NH):
            pi = pp_i[0] % 2
            pp_i[0] += 1
            pacc, tacc = (p_s1, t_ps1) if h < 4 else (p_s2, t_ps2)
            hh = h % 4
            S.I("dve", "memset", (t_Ppad[pi],), (t_Ppad[pi],), Ppad[pi], 0.0)
            S.I("dve", "tensor_tensor", (t_Es, t_MS, t_Ppad[pi]), (t_Ppad[pi],), out=Ppad[pi][:, :, b * 8:(b + 1) * 8],
                in0=Es[:, :, h, :], in1=MS[:, :, b, :], op=ALU.mult)
            S.MM([("matmul", (pacc[:, hh * VE:hh * VE + 65],),
                   dict(lhsT=Ppad[pi][:, t, :], rhs=(VXs[:, t, h, 0:65] if t < 16 else VX[:, 32, h, 0:65]),
                        start=(t == 0), stop=(t == 16))) for t in range(17)], (t_Ppad[pi], t_VXs, t_VX), (tacc,))
        for g in range(2):
            tacc = t_ps1 if g == 0 else t_ps2
            if b == 0:
                S.I("dve", "tensor_copy", (tacc, t_tmp), (t_tmp,), out=Os[:, 4 * g:4 * g + 4, 0:65], in_=acc_view(g)[:, :, 0:65])
            else:
                S.I("dve", "tensor_tensor", (tacc, t_tmp), (t_tmp,), out=Os[:, 4 * g:4 * g + 4, 0:65],
                    in0=acc_view(g)[:, :, 0:65], in1=Os[:, 4 * g:4 * g + 4, 0:65], op=ALU.add)
    if STOP >= 3:
        attn_finish(16, lambda g: Os[:, 4 * g:4 * g + 4, :], (t_tmp, t_tmp), kbuf, t_BU[0])
    t_xt.w = None
    t_xt.r = [d for tsx in t_stg for d in ([tsx.w] + tsx.r) if d is not None]

    barrier_on((t_KT, t_VX, t_QT, t_KTs, t_Es, t_VXs, t_WQKV, t_Ppad[0], t_Ppad[1]))
    S.dma(tmp[:, :], cexre_in[:, 0:8, :].rearrange("p q n -> p (q n)"), writes=(t_tmp,))
    S.I("dve", "tensor_copy", (t_tmp, t_lhsC), (t_lhsC,), out=lhsC[:, 0:8, 0, :], in_=tmp[:].rearrange("p (q n) -> p q n", q=8))
    S.dma(tmp[:, :], cexre_in[:, 8:16, :].rearrange("p q n -> p (q n)"), writes=(t_tmp,))
    S.I("dve", "tensor_copy", (t_tmp, t_lhsC), (t_lhsC,), out=lhsC[:, 8:16, 0, :], in_=tmp[:].rearrange("p (q n) -> p q n", q=8))
    for q0 in (0, 8):
        S.dma(tmp[:, :], cexim_in[:, q0:q0 + 8, :].rearrange("p q n -> p (q n)"), writes=(t_tmp,))
        S.I("dve", "tensor_scalar", (t_tmp, t_lhsC), (t_lhsC,), out=lhsC[:, q0:q0 + 8, 1, :],
            in0=tmp[:].rearrange("p (q n) -> p q n", q=8), scalar1=-1.0, scalar2=None, op0=ALU.mult)
    for k in range(4):
        load_cast(wglu[:, k, :], w_glu_in[k * 128:(k + 1) * 128, :], 1024, t_wglu)
    for k in range(8):
        load_cast(wout[:, k, :], w_out_in[k * 128:(k + 1) * 128, :], 1024, t_wout)
    for c in range(NFF):
        load_cast(wd[:, c, :], w2d[c * 128:(c + 1) * 128, :], D, t_wd)

    for ob in range(9 if STOP >= 4 else 0):
        is_sample = ob == 8
        n = 128 if is_sample else BT
        nsc = n // 8
        tok0 = TOK_CORE if is_sample else ob * BT
        S.dma(uT[:, :, 0:n], uTs[:, :, tok0:tok0 + n], writes=(t_uT,))
        if is_sample:
            S.dma(XS[:, 0, :, 0:16], h0re_in, writes=(t_XS,))
            S.dma(XS[:, 1, :, 0:16], h0im_in, writes=(t_XS,))
        else:
            S.dma(XS[:, :, :, 0:NSC], xss[:, :, :, ob * NSC:(ob + 1) * NSC], writes=(t_XS,), slow=True)
        for j in range(4):
            for qq in (0, 2):
                ssm_full_two(4 * j + qq, n, nsc)
            S.MM([("matmul", (p_a[:, 0:n],), dict(lhsT=lhsC[:, 4 * j + qq, ri, :], rhs=Hbf[:, qq, ri, 0:n],
                                                  start=(qq == 0 and ri == 0), stop=(qq == 3 and ri == 1)))
                  for qq in range(4) for ri in range(2)], (t_lhsC, t_Hbf), (t_pa,))
            yv = tmp[:, 0:n]
            y2 = tmp[:, 256:256 + n]
            S.I("dve", "scalar_tensor_tensor", (t_uT, t_const, t_pa), (t_tmp,), out=yv, in0=uT[:, j, 0:n],
                scalar=dT[:, j:j + 1], in1=p_a[:, 0:n], op0=ALU.mult, op1=ALU.add)
            S.I("dve", "tensor_tensor", (t_tmp,), (t_tmp,), out=y2, in0=yv, in1=yv, op=ALU.mult)
            S.I("dve", "tensor_scalar", (t_tmp,), (t_tmp,), out=y2, in0=y2, scalar1=0.044715, scalar2=1.0,
                op0=ALU.mult, op1=ALU.add)
            S.I("dve", "tensor_tensor", (t_tmp,), (t_tmp,), out=y2, in0=y2, in1=yv, op=ALU.mult)
            S.I("act", "activation", (t_tmp,), (t_tmp,), out=y2, in_=y2, func=AF.Sigmoid, scale=2.0 * math.sqrt(2.0 / PI))
            S.I("dve", "tensor_tensor", (t_tmp, t_gTs), (t_gTs,), out=gTs[:, j, tok0:tok0 + n], in0=yv, in1=y2, op=ALU.mult)
    for t0 in range(0, NTOK if STOP >= 4 else 0, 512):
        n = min(512, NTOK - t0)
        for c in range(4):
            S.MM([("matmul", (p_a[:, 0:n],), dict(lhsT=wglu[:, k, c * 128:(c + 1) * 128], rhs=gTs[:, k, t0:t0 + n],
                                                  start=(k == 0), stop=(k == 3))) for k in range(4)],
                 (t_wglu, t_gTs), (t_pa,))
            S.MM([("matmul", (p_b[:, 0:n],), dict(lhsT=wglu[:, k, 512 + c * 128:512 + (c + 1) * 128], rhs=gTs[:, k, t0:t0 + n],
                                                  start=(k == 0), stop=(k == 3))) for k in range(4)],
                 (t_wglu, t_gTs), (t_pb,))
            S.I("act", "activation", (t_pb,), (t_tmp,), out=tmp[:, 0:n], in_=p_b[:, 0:n], func=AF.Sigmoid)
            S.I("dve", "tensor_tensor", (t_tmp, t_pa, t_catS), (t_catS,), out=catS[:, c, t0:t0 + n], in0=tmp[:, 0:n],
                in1=p_a[:, 0:n], op=ALU.mult)

    for ti in range(17 if STOP >= 5 else 0):
        which = 1 if ti == 16 else 0
        S.dma(xt[:, 0, :], x1s[(16 + ti) * 128:(17 + ti) * 128, :], reads=(t_x1s,), writes=(t_xt,))
        for half in range(2):
            pp, tp = (p_a, t_pa) if half == 0 else (p_c, t_pc)
            S.MM([("matmul", (pp[:, :],),
                   dict(lhsT=(catA[:, k, ti * 128:(ti + 1) * 128] if k < 4 else catS[:, k - 4, ti * 128:(ti + 1) * 128]),
                        rhs=wout[:, k, half * 512:(half + 1) * 512], start=(k == 0), stop=(k == 7))) for k in range(8)],
                 (t_catA, t_catS, t_wout), (tp,))
            S.I("dve", "tensor_tensor", (tp, t_gtab), (t_tmp,), out=tmp[:, half * 512:(half + 1) * 512], in0=pp[:, :],
                in1=gtab[:, 0, which, half * 512:(half + 1) * 512], op=ALU.mult)
        S.I("dve", "tensor_tensor", (t_tmp, t_xt), (t_xt,), out=xt[:, 0, :], in0=tmp[:], in1=xt[:, 0, :], op=ALU.add)
        S.dma(x1s[(16 + ti) * 128:(17 + ti) * 128, :], xt[:, 0, :], reads=(t_xt,), writes=(t_x1s,), sem_t=t_xt)

    for tt in (t_catA, t_catS, t_wout, t_wglu, t_gTs, t_Hbf, t_lhsC, t_KT, t_VX, t_QT, t_WQKV, t_KTs, t_Es, t_VXs):
        for e in ("pool", "sp", "pe"):
            S._wait(e, tt.w)
            for d in tt.r:
                S._wait(e, d)
    ada_gate_table(2, AV(0, 4096).rearrange("p (k n) -> p k n", k=8), t_wg)
    for k in range(8):
        load_cast(wg[:, k, :], w2g[k * 128:(k + 1) * 128, :], DFF, t_wg)
        load_cast(wu[:, k, :], w2u[k * 128:(k + 1) * 128, :], DFF, t_wu)
    for ob in range(8 if STOP >= 6 else 0):
        S.dma(xt[:], x1s[(16 + 2 * ob) * 128:(18 + 2 * ob) * 128, :].rearrange("(t p) d -> p t d", p=128),
              reads=(t_x1s,), writes=(t_xt,))
        ffn(NTB, 2, False)
        S.dma(o_yp[ob * BT:(ob + 1) * BT, :].rearrange("(t p) d -> p t d", p=128), xt[:], reads=(t_xt,), sem_t=t_xt)
    if STOP >= 6:
        S.dma(xt[:, 0, :], x1s[32 * 128:33 * 128, :], reads=(t_x1s,), writes=(t_xt,))
        ffn(1, 2, True)
        S.dma(o_ys, xt[:, 0, :], reads=(t_xt,), sem_t=t_xt)

    for t in T.registry:
        if t.dkey is not None:
            S._wait("sp", (t.dkey, 16 * t.dcnt))

    for e in ENG:
        S.sems[e] = es.enter_context(nc.semaphore("s_" + e))
    for n in S.dsem_names:
        S.sems[n] = es.enter_context(nc.semaphore(n))
    with es:
        with nc.Block() as block:
            @block.tensor
            def _(E):
                for f in S.prog["pe"]:
                    f(E)

            @block.scalar
            def _(E):
                for f in S.prog["act"]:
                    f(E)

            @block.vector
            def _(E):
                for f in S.prog["dve"]:
                    f(E)

            @block.gpsimd
            def _(E):
                for f in S.prog["pool"]:
                    f(E)

            @block.sync
            def _(E):
                for f in S.prog["sp"]:
                    f(E)
    return nc


_NC = None


def _rope_tables(pos):
    half = 8
    inv = (500000.0 ** (-np.arange(half, dtype=np.float32) / half)).astype(np.float32)
    ang = pos.astype(np.float32)[:, None] * inv[None, :]
    return np.cos(ang).astype(np.float32), np.sin(ang).astype(np.float32)


def kernel(x_prompt, x_sample, c_prompt, c_sample, cache_k_win, cache_v_win,
           state_ssm_re, state_ssm_im, w_ada, b_ada, g_ffn1, w1_gate, w1_up, w1_down,
           g_mix, w_in, g_q, g_k, ssm_a_re, ssm_a_im, ssm_log_dt, ssm_b_re, ssm_b_im,
           ssm_c_re, ssm_c_im, ssm_d, w_glu, w_out, g_ffn2, w2_gate, w2_up, w2_down):
    global _NC
    if _NC is None:
        _NC = build_nc()
    nc = _NC
    f = np.float32
    xp = np.asarray(x_prompt, f)[0]
    xs = np.asarray(x_sample, f)
    ident = np.eye(128, dtype=f)
    gT = np.stack([np.asarray(g, f)[0].reshape(8, 128).T for g in (g_ffn1, g_mix, g_ffn2)], axis=1)
    gqk = np.concatenate([np.asarray(g_q, f)[0], np.asarray(g_k, f)[0]])[None, :]
    badaT = np.asarray(b_ada, f)[0].reshape(72, 128).T
    esel = np.zeros((17, 2, 128), f)
    esel[0, 0, :] = 1.0
    for b in range(SB):
        esel[1 + b, 1, b * 8:(b + 1) * 8] = 1.0

    def pairrow(a):
        return np.ascontiguousarray(np.asarray(a, f).reshape(16, 2, 64).transpose(1, 2, 0).reshape(128, 16))
    ar = pairrow(ssm_a_re[0])
    ai = pairrow(ssm_a_im[0])
    ldt = pairrow(np.repeat(np.asarray(ssm_log_dt, f)[0][:, None], 64, axis=1))

    def bexp(b):
        b = np.asarray(b, f).reshape(16, 2, 64, 16)
        o = np.zeros((2, 64, 16, 2, 16), f)
        for g2 in range(2):
            o[g2, :, :, g2, :] = b[:, g2].transpose(1, 0, 2)
        return np.ascontiguousarray(o.reshape(128, 16, 32))

    def cexp(c):
        c = np.asarray(c, f).reshape(16, 2, 16, 64)
        o = np.zeros((2, 64, 16, 128), f)
        for q in range(16):
            for g2 in range(2):
                c0 = 32 * (q % 4) + 16 * g2
                o[g2, :, q, c0:c0 + 16] = c[q, g2].T
        return np.ascontiguousarray(o.reshape(128, 16, 128))
    bre, bim = bexp(ssm_b_re[0]), bexp(ssm_b_im[0])
    cre, cim = cexp(ssm_c_re[0]), cexp(ssm_c_im[0])
    dTt = np.ascontiguousarray(np.asarray(ssm_d, f)[0].reshape(4, 128).T)

    def h0lay(s, i):
        s = np.asarray(s, f)[0, i * SB:(i + 1) * SB].reshape(SB, 16, 2, 64)
        return np.ascontiguousarray(s.transpose(2, 3, 1, 0).reshape(128, 16, SB))

    def mult(dist):
        dist = np.asarray(dist)
        m = ((dist >= 0) & (dist <= 128)).astype(f)
        m += ((dist >= 0) & (dist <= 512) & (dist % 4 == 0)).astype(f)
        m += ((dist >= 0) & (dist <= 2048) & (dist % 16 == 0)).astype(f)
        return m
    kk = np.arange(128)[:, None, None]
    dd = np.arange(17)[None, :, None]
    qq_ = np.arange(128)[None, None, :]
    maskp = mult(128 * dd + qq_ - kk).astype(f)
    masks = np.zeros((128, 17, SB, 8), f)
    ii = np.arange(8)[None, :]
    for t in range(16):
        masks[:, t, :, :] = mult(WBUF + ii - (128 * t + np.arange(128)[:, None]))[:, None, :]
    for b_ in range(SB):
        for i2 in range(8):
            for i1_ in range(i2 + 1):
                masks[b_ * 8 + i1_, 16, b_, i2] = mult(i2 - i1_)
    in_maps = []
    for i in range(NCORES):
        nreal = (i + 1) * TOK_CORE
        x_seq = np.zeros((NBLK * BT, D), f)
        x_seq[NBLK * BT - nreal:] = xp[:nreal]
        blkvalid = np.zeros((128, NBLK), f)
        blkvalid[:, NBLK - nreal // BT:] = 1.0
        c17 = np.concatenate([np.asarray(c_prompt, f), np.asarray(c_sample, f)[i * SB:(i + 1) * SB]], 0)
        cT = c17.T.reshape(8, 128, 17).transpose(1, 0, 2)
        pos = np.concatenate([(i - 1) * TOK_CORE + np.arange(2 * TOK_CORE), np.tile(PAST + np.arange(8), SB)])
        cs, sn = _rope_tables(pos)
        rope = np.stack([cs.reshape(NSP, 128, 8), sn.reshape(NSP, 128, 8)], axis=2).transpose(1, 0, 2, 3)
        in_maps.append({
            "x_seq": x_seq, "x_smp": np.ascontiguousarray(xs[i * SB:(i + 1) * SB].reshape(128, D)),
            "blkvalid": blkvalid, "cT": np.ascontiguousarray(cT),
            "bada_row": np.asarray(b_ada, f), "badaT": np.ascontiguousarray(badaT),
            "gT": np.ascontiguousarray(gT), "ident": ident, "esel": esel,
            "w_ada": np.asarray(w_ada, f)[0], "w1_gate": np.asarray(w1_gate, f)[0],
            "w1_up": np.asarray(w1_up, f)[0], "w1_down": np.asarray(w1_down, f)[0],
            "w_in": np.asarray(w_in, f)[0],
            "cache_k": np.asarray(cache_k_win, f)[0, i * SB:(i + 1) * SB].reshape(SB, WBUF, 512),
            "cache_v": np.asarray(cache_v_win, f)[0, i * SB:(i + 1) * SB].reshape(SB, WBUF, 512),
            "ssm_ar": ar, "ssm_ai": ai, "ssm_ldt": ldt, "ssm_bre": bre, "ssm_bim": bim,
            "ssm_dT": dTt, "ssm_cre": cre, "ssm_cim": cim, "maskp": maskp, "masks": masks,
            "halo_valid": np.full((128, 1), 1.0 if i > 0 else 0.0, f), "gqk": np.ascontiguousarray(gqk),
            "rope": np.ascontiguousarray(rope), "w_glu": np.asarray(w_glu, f)[0], "w_out": np.asarray(w_out, f)[0],
            "w2_gate": np.asarray(w2_gate, f)[0], "w2_up": np.asarray(w2_up, f)[0], "w2_down": np.asarray(w2_down, f)[0],
            "h0re": h0lay(state_ssm_re, i), "h0im": h0lay(state_ssm_im, i),
        })
    res = run_bass_kernel_spmd(nc, in_maps, core_ids=list(range(NCORES)))
    R = res.results
    y_p = np.concatenate([R[i]["o_yp"] for i in range(NCORES)], 0).reshape(1, SEQ, D)
    y_s = np.concatenate([R[i]["o_ys"] for i in range(NCORES)], 0).reshape(128, 8, D)
    kp = R[7]["o_kp"].reshape(1, 1, WBUF, NH, HD)
    vp = R[7]["o_vp"].reshape(1, 1, WBUF, NH, HD)
    hp = R[7]["o_hp"]
    hp = hp.reshape(2, 64, 2, 16).transpose(2, 3, 0, 1).reshape(2, 32, 64)
    hpr, hpi = hp[0][None, None], hp[1][None, None]
    ks = np.concatenate([R[i]["o_ks"] for i in range(NCORES)], 0).reshape(1, 128, WBUF, NH, HD)
    vs = np.concatenate([R[i]["o_vs"] for i in range(NCORES)], 0).reshape(1, 128, WBUF, NH, HD)
    hs = np.stack([R[i]["o_hs"] for i in range(NCORES)], 0)
    hs = hs.reshape(NCORES, 2, 64, 2, 16, SB).transpose(3, 0, 5, 4, 1, 2).reshape(2, 128, 32, 64)
    return (y_p, y_s, kp, vp, np.ascontiguousarray(hpr), np.ascontiguousarray(hpi), ks, vs,
            np.ascontiguousarray(hs[0][None]), np.ascontiguousarray(hs[1][None]))
```

```python
import math
from contextlib import ExitStack

import numpy as np
import concourse.bass as bass
import concourse.mybir as mybir
from concourse.bass_utils import run_bass_kernel_spmd

F32 = mybir.dt.float32
BF16 = mybir.dt.bfloat16
AF = mybir.ActivationFunctionType
ALU = mybir.AluOpType

NCORES = 8
D = 1024
DFF = 2816
NFF = DFF // 128
SEQ = 16384
TOK_CORE = SEQ // NCORES
NT_P = TOK_CORE // 128
NT = NT_P + 1
SB = 16
WBUF = 2048
NH = 8
HD = 64
EPS = 1e-6
PAST = 8192
ENG = ("pe", "act", "dve", "pool", "sp")


class T:
    registry = []

    def __init__(self, name):
        T.registry.append(self)
        self.name = name
        self.w = None
        self.r = []
        self.dkey = None
        self.dcnt = 0


class Sched:
    def __init__(self):
        self.prog = {e: [] for e in ENG}
        self.cnt = {e: 0 for e in ENG}
        self.seen = {e: {} for e in ENG}
        self.sems = {}
        self.dsem_names = []
        self.final = []

    def _wait(self, eng, dep):
        if dep is None:
            return
        key, val = dep
        if self.seen[eng].get(key, 0) >= val:
            return
        self.seen[eng][key] = val
        self.prog[eng].append(lambda E, key=key, val=val: E.wait_ge(self.sems[key], val))

    def _deps(self, eng, reads, writes):
        for t in reads:
            self._wait(eng, t.w)
        for t in writes:
            self._wait(eng, t.w)
            for d in t.r:
                self._wait(eng, d)

    def _commit(self, dep, reads, writes):
        for t in reads:
            t.r.append(dep)
        for t in writes:
            t.w = dep
            t.r = []

    def op(self, eng, fn, reads=(), writes=()):
        self._deps(eng, reads, writes)
        self.cnt[eng] += 1
        dep = (eng, self.cnt[eng])
        self.prog[eng].append(lambda E, fn=fn, eng=eng: fn(E).then_inc(self.sems[eng], 1))
        self._commit(dep, reads, writes)

    def mm(self, fns, reads=(), writes=()):
        self._deps("pe", reads, writes)
        for fn in fns[:-1]:
            self.prog["pe"].append(lambda E, fn=fn: fn(E))
        self.cnt["pe"] += 1
        dep = ("pe", self.cnt["pe"])
        last = fns[-1]
        self.prog["pe"].append(lambda E, fn=last: fn(E).then_inc(self.sems["pe"], 1))
        self._commit(dep, reads, writes)

    def I(self, eng, name, reads, writes, *args, **kw):
        self.op(eng, lambda E: getattr(E, name)(*args, **kw), reads, writes)

    def MM(self, items, reads, writes):
        self.mm([lambda E, n=n, a=a, k=k: getattr(E, n)(*a, **k) for (n, a, k) in items], reads, writes)

    def dma(self, out_ap, in_ap, reads=(), writes=(), sem_t=None, slow=False):
        eng = "sp"
        self._deps(eng, reads, writes)
        st = sem_t if sem_t is not None else (writes[0] if writes else reads[0])
        if st.dkey is None:
            st.dkey = "d_" + st.name
            self.dsem_names.append(st.dkey)
        st.dcnt += 1
        dep = (st.dkey, 16 * st.dcnt)
        key = st.dkey
        if slow:
            self.prog[eng].append(lambda E, o=out_ap, i=in_ap, key=key: E.dma_start(
                out=o, in_=i, allow_slow_non_contiguous=True).then_inc(self.sems[key], 16))
        else:
            self.prog[eng].append(lambda E, o=out_ap, i=in_ap, key=key: E.dma_start(
                out=o, in_=i).then_inc(self.sems[key], 16))
        self._commit(dep, reads, writes)
        return dep


BT = 256
NTB = BT // 128
NSC = BT // 8
NBLK = 64
OWN0 = 56
HALO0 = 48
NSP = 33
PI = math.pi


import os
STOP = int(os.environ.get('KSTOP', '99'))


def build_nc():
    nc = bass.Bass("TRN2", target_bir_lowering=False)
    S = Sched()
    es = ExitStack()

    def din(name, shape, dt=F32):
        return nc.dram_tensor(name, list(shape), dt, kind="ExternalInput").ap()

    def dout(name, shape, dt=F32):
        return nc.dram_tensor(name, list(shape), dt, kind="ExternalOutput").ap()

    def sb(name, shape, dt=F32):
        return es.enter_context(nc.sbuf_tensor(name, list(shape), dt))

    def ps(name, shape, dt=F32):
        return es.enter_context(nc.psum_tensor(name, list(shape), dt))

    x_seq = din("x_seq", [NBLK * BT, D])
    x_smp = din("x_smp", [128, D])
    blkvalid_in = din("blkvalid", [128, NBLK])
    cT_in = din("cT", [128, 8, 17])
    bada_row = din("bada_row", [1, 9 * D])
    badaT = din("badaT", [128, 72])
    gT_in = din("gT", [128, 3, 8])
    ident_in = din("ident", [128, 128])
    esel_in = din("esel", [17, 2, 128])
    w_ada = din("w_ada", [D, 9 * D])
    w1g = din("w1_gate", [D, DFF])
    w1u = din("w1_up", [D, DFF])
    w1d = din("w1_down", [DFF, D])
    w_in = din("w_in", [D, 2048])
    ck = din("cache_k", [SB, WBUF, 512])
    cv = din("cache_v", [SB, WBUF, 512])
    ssm_ar = din("ssm_ar", [128, 16])
    ssm_ai = din("ssm_ai", [128, 16])
    ssm_ldt = din("ssm_ldt", [128, 16])
    ssm_bre = din("ssm_bre", [128, 16, 32])
    ssm_bim = din("ssm_bim", [128, 16, 32])
    ssm_dT = din("ssm_dT", [128, 4])
    h0re_in = din("h0re", [128, 16, SB])
    h0im_in = din("h0im", [128, 16, SB])

    o_ks = dout("o_ks", [SB, WBUF, 512])
    o_vs = dout("o_vs", [SB, WBUF, 512])
    o_hp = dout("o_hp", [128, 2, 16])
    o_hs = dout("o_hs", [128, 2, 16, SB])
    x1s = nc.dram_tensor("x1s", [NSP * 128, D], F32).ap()
    uTs = nc.dram_tensor("uTs", [128, 4, TOK_CORE + 128], BF16).ap()
    xss = nc.dram_tensor("xss", [128, 2, 16, TOK_CORE // 8], F32).ap()

    ident_f = sb("ident_f", [128, 128])
    ident_b = sb("ident_b", [128, 128], BF16)
    esel_f = sb("esel_f", [17, 2, 128])
    cT = sb("cT_sb", [128, 8, 17])
    sT = sb("sT_sb", [128, 8, 17], BF16)
    sig = sb("sig_sb", [128, 8, 17])
    badaT_sb = sb("badaT_sb", [128, 72])
    gT = sb("gT_sb", [128, 3, 8])
    blkvalid = sb("blkvalid_sb", [128, NBLK])
    gtab = sb("gtab", [128, 1, 2, D])
    AB = sb("AB", [128, 3, 2, 8, 17])
    eps_t = sb("eps_t", [128, 1])
    ARN = 3 * 8 * DFF
    arena = sb("arena", [128, ARN], BF16)
    wg = arena[:, 0:8 * DFF].rearrange("p (k n) -> p k n", k=8)
    wu = arena[:, 8 * DFF:16 * DFF].rearrange("p (k n) -> p k n", k=8)
    wd = arena[:, 16 * DFF:24 * DFF].rearrange("p (c n) -> p c n", c=NFF)
    wuin = sb("wuin", [128, 8, 512], BF16)
    wada_b = wd[:, 0:4, :].rearrange("p c (a n) -> p (c a) n", a=2)
    xt = sb("xt", [128, NTB, D])
    tmp = sb("tmp", [128, D])
    bada_blk = tmp[0:17, 0:512]
    grow = tmp[0:17, 512:1024]
    xn = sb("xn", [128, D], BF16)
    ssq = sb("ssq", [128, 4])
    rstd = sb("rstd", [128, 8])
    hT = sb("hT", [128, 8, BT], BF16)
    actT = sb("actT", [128, NFF, BT], BF16)
    silu_t = sb("silu_t", [128, BT], BF16)
    uT = sb("uT", [128, 4, BT], BF16)
    PR = sb("PR", [128, 18, 16])
    PRI = sb("PRI", [128, 16], mybir.dt.int32)
    LP = sb("LP", [128, 16, 9, 3])
    bb = xt[:, 0, :].rearrange("p (r q n) -> p r q n", r=2, q=16)
    bex = xt[:, 1, :].rearrange("p (r q n) -> p r q n", r=2, q=16)
    wexp = actT[:, 0:16, :].rearrange("p a (b n) -> p (a b) n", n=128).rearrange("p (r q) n -> p r q n", r=2)
    lhsB = sb("lhsB", [128, 16, 2, 128], BF16)
    dT = sb("dT_sb", [128, 4])
    BU = [sb(f"BU{i}", [128, 2, BT]) for i in range(2)]
    wst = [BU[i][:].rearrange("p r n -> p (r n)") for i in range(2)]
    WCH = 2 * BT
    XS = sb("XS", [128, 2, 16, NSC + 1])
    t4 = sb("t4", [128, 4, 16])
    hend = tmp[:, 0:512].rearrange("p (r q b) -> p r q b", r=2, q=16)
    h0 = tmp[:, 512:1024].rearrange("p (r q b) -> p r q b", r=2, q=16)

    p_a = ps("p_a", [128, 512])
    p_b = ps("p_b", [128, 512])
    p_c = ps("p_c", [128, 512])
    p_d = ps("p_d", [128, 512])
    p_t = ps("p_t", [128, 1024], BF16)
    p_s1 = ps("p_s1", [128, 512])
    p_s2 = ps("p_s2", [128, 512])

    t_const = T("const")
    t_wg, t_wu, t_wd, t_wuin = T("wg"), T("wu"), T("wd"), T("wuin")
    t_wada = t_wd
    t_gtab, t_AB = T("gtab"), T("AB")
    t_pa, t_pb, t_pc, t_pd, t_pt, t_ps1, t_ps2 = T("pa"), T("pb"), T("pc"), T("pd"), T("pt"), T("ps1"), T("ps2")
    t_xt, t_xn, t_ssq, t_rstd, t_tmp = T("xt"), T("xn"), T("ssq"), T("rstd"), T("tmp")
    t_grow = t_bblk = t_tmp
    t_hT, t_act, t_silu, t_uT = T("hT"), T("act"), T("silu"), T("uT")
    t_cache = T("cache")
    t_x1s = T("x1s")
    t_PR, t_LP, t_lhsB = T("PR"), T("LP"), T("lhsB")
    t_bb = t_xt
    t_wexp = t_act
    t_BU = [T("BU0"), T("BU1")]
    t_wst = t_BU
    t_XS, t_t4 = T("XS"), T("t4")
    t_h0 = t_hend = t_tmp

    for b in range(SB):
        for (src, dst) in ((ck, o_ks), (cv, o_vs)):
            S.dma(dst[b, 0:WBUF - 8, :], src[b, 8:WBUF, :], sem_t=t_cache)

    for (dst, srcap) in ((ident_f, ident_in), (esel_f, esel_in), (cT, cT_in), (badaT_sb, badaT),
                         (gT, gT_in), (blkvalid, blkvalid_in), (dT, ssm_dT)):
        S.dma(dst[:], srcap, writes=(t_const,))
    S.dma(PR[:, 0, :], ssm_ar, writes=(t_PR,))
    S.dma(PR[:, 1, :], ssm_ai, writes=(t_PR,))
    S.dma(PR[:, 2, :], ssm_ldt, writes=(t_PR,))
    S.dma(bex[:, 0], ssm_bre, writes=(t_bb,))
    S.dma(bex[:, 1], ssm_bim, writes=(t_bb,))
    S.I("dve", "memset", (), (t_const,), eps_t[:], EPS)
    S.I("dve", "tensor_copy", (t_const,), (t_const,), out=ident_b[:], in_=ident_f[:])
    S.I("act", "activation", (t_const,), (t_const,), out=sig[:], in_=cT[:], func=AF.Sigmoid)
    S.I("dve", "tensor_tensor", (t_const,), (t_const,), out=sT[:], in0=sig[:], in1=cT[:], op=ALU.mult)

    def P(i):
        return PR[:, i, :]

    def pr(name, *a, **k):
        S.I("dve", name, (t_PR,), (t_PR,), *a, **k)

    AR, AI, LDT, DT_, MAG, ANG, R1, R2, SN, CS, LR, LI = range(12)
    S.I("act", "activation", (t_PR,), (t_PR,), out=P(DT_), in_=P(LDT), func=AF.Exp)
    pr("tensor_tensor", out=P(MAG), in0=P(AR), in1=P(DT_), op=ALU.mult)
    S.I("act", "activation", (t_PR,), (t_PR,), out=P(MAG), in_=P(MAG), func=AF.Exp)
    pr("tensor_tensor", out=P(ANG), in0=P(AI), in1=P(DT_), op=ALU.mult)
    T1, T2 = 12, 13
    for (dst_, off_) in ((R1, 10.0 * PI), (R2, 10.5 * PI)):
        pr("tensor_scalar", out=P(T1), in0=P(ANG), scalar1=off_, scalar2=1.0 / (2.0 * PI), op0=ALU.add, op1=ALU.mult)
        S.I("dve", "tensor_copy", (t_PR,), (t_PR,), out=PRI[:], in_=P(T1))
        S.I("dve", "tensor_copy", (t_PR,), (t_PR,), out=P(T2), in_=PRI[:])
        pr("tensor_scalar", out=P(dst_), in0=P(ANG), scalar1=off_, scalar2=None, op0=ALU.add)
        pr("scalar_tensor_tensor", out=P(dst_), in0=P(T2), scalar=-2.0 * PI, in1=P(dst_), op0=ALU.mult, op1=ALU.add)
        pr("tensor_scalar", out=P(T1), in0=P(dst_), scalar1=PI, scalar2=-2.0 * PI, op0=ALU.is_gt, op1=ALU.mult)
        pr("tensor_tensor", out=P(dst_), in0=P(dst_), in1=P(T1), op=ALU.add)
        pr("tensor_scalar", out=P(dst_), in0=P(dst_), scalar1=-PI, scalar2=PI, op0=ALU.max, op1=ALU.min)
    S.I("act", "activation", (t_PR,), (t_PR,), out=P(SN), in_=P(R1), func=AF.Sin)
    S.I("act", "activation", (t_PR,), (t_PR,), out=P(CS), in_=P(R2), func=AF.Sin)
    pr("tensor_tensor", out=P(LR), in0=P(MAG), in1=P(CS), op=ALU.mult)
    pr("tensor_tensor", out=P(LI), in0=P(MAG), in1=P(SN), op=ALU.mult)
    S.I("dve", "memset", (), (t_LP,), LP[:, :, 0, 0], 1.0)
    S.I("dve", "memset", (t_LP,), (t_LP,), LP[:, :, 0, 1:3], 0.0)
    S.I("dve", "tensor_copy", (t_PR, t_LP), (t_LP,), out=LP[:, :, 1, 0], in_=P(LR))
    S.I("dve", "tensor_copy", (t_PR, t_LP), (t_LP,), out=LP[:, :, 1, 1], in_=P(LI))
    T1, T2 = 12, 13
    for k in range(2, 9):
        S.I("dve", "tensor_tensor", (t_LP, t_PR), (t_PR,), out=P(T1), in0=LP[:, :, k - 1, 0], in1=P(LR), op=ALU.mult)
        S.I("dve", "tensor_tensor", (t_LP, t_PR), (t_PR,), out=P(T2), in0=LP[:, :, k - 1, 1], in1=P(LI), op=ALU.mult)
        S.I("dve", "tensor_tensor", (t_PR, t_LP), (t_LP,), out=LP[:, :, k, 0], in0=P(T1), in1=P(T2), op=ALU.subtract)
        S.I("dve", "tensor_tensor", (t_LP, t_PR), (t_PR,), out=P(T1), in0=LP[:, :, k - 1, 0], in1=P(LI), op=ALU.mult)
        S.I("dve", "tensor_tensor", (t_LP, t_PR), (t_PR,), out=P(T2), in0=LP[:, :, k - 1, 1], in1=P(LR), op=ALU.mult)
        S.I("dve", "tensor_tensor", (t_PR, t_LP), (t_LP,), out=LP[:, :, k, 1], in0=P(T1), in1=P(T2), op=ALU.add)
    S.I("dve", "tensor_scalar", (t_LP,), (t_LP,), out=LP[:, :, 1:9, 2], in0=LP[:, :, 1:9, 1], scalar1=-1.0,
        scalar2=None, op0=ALU.mult)
    NR, DEN, CR, CI = 14, 15, 16, 17
    pr("tensor_scalar", out=P(NR), in0=P(LR), scalar1=-1.0, scalar2=None, op0=ALU.add)
    pr("tensor_tensor", out=P(T1), in0=P(AR), in1=P(AR), op=ALU.mult)
    pr("tensor_tensor", out=P(T2), in0=P(AI), in1=P(AI), op=ALU.mult)
    pr("tensor_tensor", out=P(DEN), in0=P(T1), in1=P(T2), op=ALU.add)
    pr("reciprocal", out=P(DEN), in_=P(DEN))
    pr("tensor_tensor", out=P(T1), in0=P(NR), in1=P(AR), op=ALU.mult)
    pr("tensor_tensor", out=P(T2), in0=P(LI), in1=P(AI), op=ALU.mult)
    pr("tensor_tensor", out=P(CR), in0=P(T1), in1=P(T2), op=ALU.add)
    pr("tensor_tensor", out=P(CR), in0=P(CR), in1=P(DEN), op=ALU.mult)
    pr("tensor_tensor", out=P(T1), in0=P(LI), in1=P(AR), op=ALU.mult)
    pr("tensor_tensor", out=P(T2), in0=P(NR), in1=P(AI), op=ALU.mult)
    pr("tensor_tensor", out=P(CI), in0=P(T1), in1=P(T2), op=ALU.subtract)
    pr("tensor_tensor", out=P(CI), in0=P(CI), in1=P(DEN), op=ALU.mult)
    crb = P(CR).unsqueeze(2).to_broadcast((128, 16, 32))
    cib = P(CI).unsqueeze(2).to_broadcast((128, 16, 32))
    S.I("dve", "tensor_tensor", (t_PR, t_bb), (t_bb,), out=bb[:, 0], in0=bex[:, 0], in1=crb, op=ALU.mult)
    S.I("dve", "tensor_tensor", (t_PR, t_bb), (t_bb,), out=bb[:, 1], in0=bex[:, 1], in1=crb, op=ALU.mult)
    S.I("dve", "tensor_tensor", (t_PR, t_bb), (t_bb,), out=bex[:, 0], in0=bex[:, 0], in1=cib, op=ALU.mult)
    S.I("dve", "tensor_tensor", (t_PR, t_bb), (t_bb,), out=bex[:, 1], in0=bex[:, 1], in1=cib, op=ALU.mult)
    S.I("dve", "tensor_tensor", (t_bb,), (t_bb,), out=bb[:, 0], in0=bb[:, 0], in1=bex[:, 1], op=ALU.subtract)
    S.I("dve", "tensor_tensor", (t_bb,), (t_bb,), out=bb[:, 1], in0=bb[:, 1], in1=bex[:, 0], op=ALU.add)
    LPB = PR[:, 0:12, :].rearrange("p (m r) n -> p m r n", r=2)
    S.I("dve", "tensor_copy", (t_LP,), (t_LP,), out=LPB[:, 0, 0, :], in_=LP[:, :, 8, 0])
    S.I("dve", "tensor_copy", (t_LP,), (t_LP,), out=LPB[:, 0, 1, :], in_=LP[:, :, 8, 1])
    for m in range(1, 6):
        a_, b_ = LPB[:, m - 1, 0, :], LPB[:, m - 1, 1, :]
        S.I("dve", "tensor_tensor", (t_LP, t_PR), (t_PR,), out=P(T1), in0=a_, in1=a_, op=ALU.mult)
        S.I("dve", "tensor_tensor", (t_LP, t_PR), (t_PR,), out=P(T2), in0=b_, in1=b_, op=ALU.mult)
        S.I("dve", "tensor_tensor", (t_PR, t_LP), (t_LP,), out=LPB[:, m, 0, :], in0=P(T1), in1=P(T2), op=ALU.subtract)
        S.I("dve", "tensor_tensor", (t_LP, t_PR), (t_PR,), out=P(T1), in0=a_, in1=b_, op=ALU.mult)
        S.I("dve", "tensor_scalar", (t_PR, t_LP), (t_LP,), out=LPB[:, m, 1, :], in0=P(T1), scalar1=2.0, scalar2=None,
            op0=ALU.mult)
    S.I("dve", "memset", (), (t_wexp,), wexp[:], 0.0)
    for ri in range(2):
        for s in range(4):
            S.I("dve", "tensor_copy", (t_bb, t_wexp), (t_wexp,), out=wexp[:, ri, s::4, 32 * s:32 * s + 32],
                in_=bb[:, ri, s::4, :])
    for q in range(16):
        S.MM([("transpose", (p_t[:, ri * 128:(ri + 1) * 128], wexp[:, ri, q, :], ident_b[:]), {}) for ri in range(2)],
             (t_wexp, t_const), (t_pt,))
        S.I("act", "activation", (t_pt,), (t_lhsB,), out=lhsB[:, q, :, :],
            in_=p_t[:, 0:256].rearrange("p (r n) -> p r n", r=2), func=AF.Copy)
    S.I("dve", "memset", (), (t_XS,), XS[:], 0.0)

    wcount = [0]

    def load_cast(dst_ap, src_ap, ncols, t_dst):
        for c0 in range(0, ncols, WCH):
            c1 = min(ncols, c0 + WCH)
            i = wcount[0] % 2
            wcount[0] += 1
            S.dma(wst[i][:, 0:c1 - c0], src_ap[:, c0:c1], writes=(t_wst[i],))
            eng3 = (wcount[0] - 1) % 3
            if eng3 == 0:
                S.I("pool", "tensor_copy", (t_wst[i],), (t_dst,), out=dst_ap[:, c0:c1], in_=wst[i][:, 0:c1 - c0])
            elif eng3 == 1:
                S.I("act", "activation", (t_wst[i],), (t_dst,), out=dst_ap[:, c0:c1], in_=wst[i][:, 0:c1 - c0],
                    func=AF.Copy)
            else:
                S.I("dve", "tensor_copy", (t_wst[i],), (t_dst,), out=dst_ap[:, c0:c1], in_=wst[i][:, 0:c1 - c0])

    gate_blocks = {4: (0, 0), 5: (0, 1)}
    for blk in range(18):
        for k in range(8):
            load_cast(wada_b[:, k, :], w_ada[k * 128:(k + 1) * 128, blk * 512:(blk + 1) * 512], 512, t_wada)
        for cc in range(4):
            ch = blk * 4 + cc
            S.MM([("matmul", (p_a[:, cc * 32:cc * 32 + 17],),
                   dict(lhsT=wada_b[:, k, cc * 128:(cc + 1) * 128], rhs=sT[:, k, :], start=(k == 0), stop=(k == 7)))
                  for k in range(8)], (t_wada, t_const), (t_pa,))
            term, kk = ch // 8, ch % 8
            li_, kind = term // 3, term % 3
            if kind == 0:
                S.I("dve", "tensor_scalar", (t_pa, t_const), (t_AB,), out=AB[:, li_, 1, kk, :],
                    in0=p_a[:, cc * 32:cc * 32 + 17], scalar1=badaT_sb[:, ch:ch + 1], scalar2=None, op0=ALU.add)
            elif kind == 1:
                S.I("dve", "tensor_scalar", (t_pa, t_const), (t_AB,), out=AB[:, li_, 0, kk, :],
                    in0=p_a[:, cc * 32:cc * 32 + 17], scalar1=badaT_sb[:, ch:ch + 1], scalar2=1.0,
                    op0=ALU.add, op1=ALU.add)
                S.I("dve", "tensor_scalar", (t_AB, t_const), (t_AB,), out=AB[:, li_, 0, kk, :],
                    in0=AB[:, li_, 0, kk, :], scalar1=gT[:, li_, kk:kk + 1], scalar2=None, op0=ALU.mult)
        if blk in gate_blocks:
            li, half = gate_blocks[blk]
            S.dma(bada_blk[:], bada_row[0:1, blk * 512:(blk + 1) * 512].to_broadcast((17, 512)), writes=(t_bblk,))
            S.MM([("matmul", (p_b[0:17, :],), dict(lhsT=sT[:, k, :], rhs=wada_b[:, k, :], start=(k == 0), stop=(k == 7)))
                  for k in range(8)], (t_wada, t_const), (t_pb,))
            S.I("dve", "tensor_tensor", (t_pb, t_bblk), (t_grow,), out=grow[:], in0=p_b[0:17, :], in1=bada_blk[:],
                op=ALU.add)
            for which in range(2):
                S.MM([("matmul", (p_c[:, :],), dict(lhsT=esel_f[:, which, :], rhs=grow[:], start=True, stop=True))],
                     (t_grow, t_const), (t_pc,))
                S.I("act", "activation", (t_pc,), (t_gtab,), out=gtab[:, li, which, half * 512:(half + 1) * 512],
                    in_=p_c[:, :], func=AF.Copy)
    for k in range(8):
        load_cast(wg[:, k, :], w1g[k * 128:(k + 1) * 128, :], DFF, t_wg)
        load_cast(wu[:, k, :], w1u[k * 128:(k + 1) * 128, :], DFF, t_wu)
        load_cast(wuin[:, k, :], w_in[k * 128:(k + 1) * 128, 1536:2048], 512, t_wuin)
    for c in range(NFF):
        load_cast(wd[:, c, :], w1d[c * 128:(c + 1) * 128, :], D, t_wd)

    def norm_mod_T(nt, li, is_sample):
        for t in range(nt):
            S.I("act", "activation", (t_xt,), (t_xn, t_ssq), out=xn[:], in_=xt[:, t, :], func=AF.Square,
                accum_out=ssq[:, t:t + 1])
            S.I("act", "activation", (t_ssq, t_const), (t_rstd,), out=rstd[:, t:t + 1], in_=ssq[:, t:t + 1],
                func=AF.Sqrt, scale=1.0 / D, bias=eps_t[:, 0:1])
            S.I("dve", "reciprocal", (t_rstd,), (t_rstd,), out=rstd[:, 4 + t:5 + t], in_=rstd[:, t:t + 1])
            S.I("dve", "tensor_scalar", (t_xt, t_rstd), (t_xn,), out=xn[:], in0=xt[:, t, :],
                scalar1=rstd[:, 4 + t:5 + t], scalar2=None, op0=ALU.mult)
            S.MM([("transpose", (p_t[:, k * 128:(k + 1) * 128], xn[:, k * 128:(k + 1) * 128], ident_b[:]), {})
                  for k in range(8)], (t_xn, t_const), (t_pt,))
            for k in range(8):
                if not is_sample:
                    if k % 2 == 0:
                        S.I("dve", "tensor_scalar", (t_pt, t_AB), (t_hT,), out=hT[:, k, t * 128:(t + 1) * 128],
                            in0=p_t[:, k * 128:(k + 1) * 128], scalar1=AB[:, li, 0, k, 0:1],
                            scalar2=AB[:, li, 1, k, 0:1], op0=ALU.mult, op1=ALU.add)
                    else:
                        S.I("act", "activation", (t_pt, t_AB), (t_hT,), out=hT[:, k, t * 128:(t + 1) * 128],
                            in_=p_t[:, k * 128:(k + 1) * 128], func=AF.Identity, scale=AB[:, li, 0, k, 0:1],
                            bias=AB[:, li, 1, k, 0:1])
                else:
                    for b in range(SB):
                        S.I("dve", "tensor_scalar", (t_pt, t_AB), (t_hT,), out=hT[:, k, b * 8:(b + 1) * 8],
                            in0=p_t[:, k * 128 + b * 8:k * 128 + (b + 1) * 8], scalar1=AB[:, li, 0, k, 1 + b:2 + b],
                            scalar2=AB[:, li, 1, k, 1 + b:2 + b], op0=ALU.mult, op1=ALU.add)

    def ffn(nt, li, is_sample, filler=None):
        n = nt * 128
        norm_mod_T(nt, li, is_sample)
        for c in range(NFF):
            pg, tg = (p_a, t_pa) if c % 2 == 0 else (p_c, t_pc)
            pu, tu = (p_b, t_pb) if c % 2 == 0 else (p_d, t_pd)
            S.MM([("matmul", (pg[:, 0:n],), dict(lhsT=wg[:, k, c * 128:(c + 1) * 128], rhs=hT[:, k, 0:n],
                                                 start=(k == 0), stop=(k == 7))) for k in range(8)],
                 (t_wg, t_hT), (tg,))
            S.MM([("matmul", (pu[:, 0:n],), dict(lhsT=wu[:, k, c * 128:(c + 1) * 128], rhs=hT[:, k, 0:n],
                                                 start=(k == 0), stop=(k == 7))) for k in range(8)],
                 (t_wu, t_hT), (tu,))
            S.I("act", "activation", (tg,), (t_silu,), out=silu_t[:, 0:n], in_=pg[:, 0:n], func=AF.Silu)
            S.I("dve", "tensor_tensor", (t_silu, tu), (t_act,), out=actT[:, c, 0:n], in0=silu_t[:, 0:n],
                in1=pu[:, 0:n], op=ALU.mult)
            if filler is not None:
                filler(c)
        if filler is not None:
            filler(-1)
        which = 1 if is_sample else 0
        for t in range(nt):
            for half in range(2):
                pp, tp = (p_a, t_pa) if half == 0 else (p_c, t_pc)
                S.MM([("matmul", (pp[:, :],), dict(lhsT=actT[:, c, t * 128:(t + 1) * 128],
                                                   rhs=wd[:, c, half * 512:(half + 1) * 512],
                                                   start=(c == 0), stop=(c == NFF - 1))) for c in range(NFF)],
                     (t_act, t_wd), (tp,))
                S.I("dve", "tensor_tensor", (tp, t_gtab), (t_tmp,), out=tmp[:, half * 512:(half + 1) * 512],
                    in0=pp[:, :], in1=gtab[:, 0, which, half * 512:(half + 1) * 512], op=ALU.mult)
            S.I("dve", "scalar_tensor_tensor", (t_tmp, t_xt), (t_xt,), out=xt[:, t, :], in0=tmp[:], scalar=0.5,
                in1=xt[:, t, :], op0=ALU.mult, op1=ALU.add)

    def lp(q, k, j):
        return LP[:, q, k, j:j + 1]

    def ssm_bu_and_scan(q, n, bufi):
        j = q // 4
        S.MM([("matmul", (p_s1[:, 0:n],), dict(lhsT=lhsB[:, q, 0, :], rhs=uT[:, j, 0:n], start=True, stop=True))],
             (t_lhsB, t_uT), (t_ps1,))
        S.MM([("matmul", (p_s2[:, 0:n],), dict(lhsT=lhsB[:, q, 1, :], rhs=uT[:, j, 0:n], start=True, stop=True))],
             (t_lhsB, t_uT), (t_ps2,))
        bu = BU[bufi]
        tb = t_BU[bufi]
        S.I("act", "activation", (t_ps1,), (tb,), out=bu[:, 0, 0:n], in_=p_s1[:, 0:n], func=AF.Copy)
        S.I("act", "activation", (t_ps2,), (tb,), out=bu[:, 1, 0:n], in_=p_s2[:, 0:n], func=AF.Copy)
        for s in range(1, 8):
            cur_re, cur_im = bu[:, 0, s:n:8], bu[:, 1, s:n:8]
            prv_re, prv_im = bu[:, 0, s - 1:n:8], bu[:, 1, s - 1:n:8]
            S.I("dve", "scalar_tensor_tensor", (tb, t_LP), (tb,), out=cur_re, in0=prv_re, scalar=lp(q, 1, 0),
                in1=cur_re, op0=ALU.mult, op1=ALU.add)
            S.I("dve", "scalar_tensor_tensor", (tb, t_LP), (tb,), out=cur_re, in0=prv_im, scalar=lp(q, 1, 2),
                in1=cur_re, op0=ALU.mult, op1=ALU.add)
            S.I("dve", "scalar_tensor_tensor", (tb, t_LP), (tb,), out=cur_im, in0=prv_re, scalar=lp(q, 1, 1),
                in1=cur_im, op0=ALU.mult, op1=ALU.add)
            S.I("dve", "scalar_tensor_tensor", (tb, t_LP), (tb,), out=cur_im, in0=prv_im, scalar=lp(q, 1, 0),
                in1=cur_im, op0=ALU.mult, op1=ALU.add)

    def ssm_tree_two(q0, n):
        banks = ((p_s1, t_ps1), (p_s2, t_ps2))
        for i in range(2):
            q = q0 + i
            j = q // 4
            pp, tp = banks[i]
            for ri in range(2):
                S.MM([("matmul", (pp[:, ri * n:(ri + 1) * n],), dict(lhsT=lhsB[:, q, ri, :], rhs=uT[:, j, 0:n],
                                                                      start=True, stop=True))], (t_lhsB, t_uT), (tp,))
            S.I("act", "activation", (tp,), (t_BU[i],), out=BU[i][:, :, 0:n],
                in_=pp[:, 0:2 * n].rearrange("p (r n) -> p r n", r=2), func=AF.Copy)
        for (st, k) in ((2, 1), (4, 2), (8, 4)):
            views = []
            for i in range(2):
                bu = BU[i]
                views.append((bu[:, 0, st - 1:n:st], bu[:, 1, st - 1:n:st], bu[:, 0, st // 2 - 1:n:st], bu[:, 1, st // 2 - 1:n:st]))
            for part in range(2):
                for i in range(2):
                    q = q0 + i
                    cre, cim, pre, pim = views[i]
                    if part == 0:
                        S.I("dve", "scalar_tensor_tensor", (t_BU[i], t_LP), (t_BU[i],), out=cre, in0=pre, scalar=lp(q, k, 0),
                            in1=cre, op0=ALU.mult, op1=ALU.add)
                        S.I("dve", "scalar_tensor_tensor", (t_BU[i], t_LP), (t_BU[i],), out=cim, in0=pre, scalar=lp(q, k, 1),
                            in1=cim, op0=ALU.mult, op1=ALU.add)
                    else:
                        S.I("dve", "scalar_tensor_tensor", (t_BU[i], t_LP), (t_BU[i],), out=cre, in0=pim, scalar=lp(q, k, 2),
                            in1=cre, op0=ALU.mult, op1=ALU.add)
                        S.I("dve", "scalar_tensor_tensor", (t_BU[i], t_LP), (t_BU[i],), out=cim, in0=pim, scalar=lp(q, k, 0),
                            in1=cim, op0=ALU.mult, op1=ALU.add)

    def ssm_full_two(q0, n, nsc):
        banks = ((p_s1, t_ps1), (p_s2, t_ps2))
        for i in range(2):
            q = q0 + i
            j = q // 4
            pp, tp = banks[i]
            for ri in range(2):
                S.MM([("matmul", (pp[:, ri * n:(ri + 1) * n],), dict(lhsT=lhsB[:, q, ri, :], rhs=uT[:, j, 0:n],
                                                                      start=True, stop=True))], (t_lhsB, t_uT), (tp,))
            S.I("act", "activation", (tp,), (t_BU[i],), out=BU[i][:, :, 0:n],
                in_=pp[:, 0:2 * n].rearrange("p (r n) -> p r n", r=2), func=AF.Copy)
        for tau in range(8):
            for part in range(2):
                for i in range(2):
                    q = q0 + i
                    bu = BU[i]
                    cre, cim = bu[:, 0, tau:n:8], bu[:, 1, tau:n:8]
                    if tau == 0:
                        pre, pim = XS[:, 0, q, 0:nsc], XS[:, 1, q, 0:nsc]
                        rd = (t_BU[i], t_LP, t_XS)
                    else:
                        pre, pim = bu[:, 0, tau - 1:n:8], bu[:, 1, tau - 1:n:8]
                        rd = (t_BU[i], t_LP)
                    if part == 0:
                        S.I("dve", "scalar_tensor_tensor", rd, (t_BU[i],), out=cre, in0=pre, scalar=lp(q, 1, 0),
                            in1=cre, op0=ALU.mult, op1=ALU.add)
                        S.I("dve", "scalar_tensor_tensor", rd, (t_BU[i],), out=cim, in0=pre, scalar=lp(q, 1, 1),
                            in1=cim, op0=ALU.mult, op1=ALU.add)
                    else:
                        S.I("dve", "scalar_tensor_tensor", rd, (t_BU[i],), out=cre, in0=pim, scalar=lp(q, 1, 2),
                            in1=cre, op0=ALU.mult, op1=ALU.add)
                        S.I("dve", "scalar_tensor_tensor", rd, (t_BU[i],), out=cim, in0=pim, scalar=lp(q, 1, 0),
                            in1=cim, op0=ALU.mult, op1=ALU.add)
        for i in range(2):
            S.I("pool", "tensor_copy", (t_BU[i], t_Hbf), (t_Hbf,), out=Hbf[:, (q0 + i) % 4, :, 0:n], in_=BU[i][:, :, 0:n])

    a8re = LP[:, :, 8, 0]
    a8im = LP[:, :, 8, 1]

    bufc = [0]

    XS2 = tmp[:].rearrange("p (r q j) -> p r q j", r=2, q=16)
    tA = BU[0][:].rearrange("p r n -> p (r n)")

    def level_b(nsc, part):
        assert nsc == 32
        if part == 0:
            c_re, c_im = XS[:, 0, :, 0], XS[:, 1, :, 0]
            S.I("dve", "tensor_tensor", (t_XS, t_LP), (t_t4,), out=t4[:, 0], in0=c_re, in1=a8re, op=ALU.mult)
            S.I("dve", "tensor_tensor", (t_XS, t_LP), (t_t4,), out=t4[:, 1], in0=c_im, in1=a8im, op=ALU.mult)
            S.I("dve", "tensor_tensor", (t_XS, t_LP), (t_t4,), out=t4[:, 2], in0=c_im, in1=a8re, op=ALU.mult)
            S.I("dve", "tensor_tensor", (t_XS, t_LP), (t_t4,), out=t4[:, 3], in0=c_re, in1=a8im, op=ALU.mult)
            S.I("dve", "tensor_tensor", (t_t4,), (t_t4,), out=t4[:, 0], in0=t4[:, 0], in1=t4[:, 1], op=ALU.subtract)
            S.I("dve", "tensor_tensor", (t_t4,), (t_t4,), out=t4[:, 2], in0=t4[:, 2], in1=t4[:, 3], op=ALU.add)
            S.I("dve", "tensor_tensor", (t_t4, t_XS), (t_XS,), out=XS[:, 0, :, 1], in0=t4[:, 0], in1=XS[:, 0, :, 1], op=ALU.add)
            S.I("dve", "tensor_tensor", (t_t4, t_XS), (t_XS,), out=XS[:, 1, :, 1], in0=t4[:, 2], in1=XS[:, 1, :, 1], op=ALU.add)
        steps = {0: (0, 1), 1: (2, 3), 2: (4,)}[part]
        for m in steps:
            d = 1 << m
            if m % 2 == 0:
                src, t_src, dst, t_dst = XS[:, :, :, 1:33], t_XS, XS2, t_tmp
            else:
                src, t_src, dst, t_dst = XS2, t_tmp, XS[:, :, :, 1:33], t_XS
            w = 32 - d
            are = LPB[:, m, 0, :].unsqueeze(2).to_broadcast((128, 16, w))
            aim = LPB[:, m, 1, :].unsqueeze(2).to_broadcast((128, 16, w))
            tAv = tA[:, 0:16 * w].rearrange("p (q j) -> p q j", q=16)
            s_re_lo, s_im_lo = src[:, 0, :, 0:w], src[:, 1, :, 0:w]
            s_re_hi, s_im_hi = src[:, 0, :, d:32], src[:, 1, :, d:32]
            d_re_hi, d_im_hi = dst[:, 0, :, d:32], dst[:, 1, :, d:32]
            S.I("dve", "tensor_copy", (t_src, t_dst), (t_dst,), out=dst[:, :, :, 0:d], in_=src[:, :, :, 0:d])
            S.I("dve", "tensor_tensor", (t_src, t_LP, t_dst), (t_dst,), out=d_re_hi, in0=s_re_lo, in1=are, op=ALU.mult)
            S.I("dve", "tensor_tensor", (t_src, t_LP, t_dst), (t_dst,), out=d_im_hi, in0=s_im_lo, in1=are, op=ALU.mult)
            S.I("dve", "tensor_tensor", (t_src, t_LP, t_BU[0]), (t_BU[0],), out=tAv, in0=s_im_lo, in1=aim, op=ALU.mult)
            S.I("dve", "tensor_tensor", (t_src, t_dst), (t_dst,), out=d_re_hi, in0=d_re_hi, in1=s_re_hi, op=ALU.add)
            S.I("dve", "tensor_tensor", (t_src, t_dst), (t_dst,), out=d_im_hi, in0=d_im_hi, in1=s_im_hi, op=ALU.add)
            S.I("dve", "tensor_tensor", (t_BU[0], t_dst), (t_dst,), out=d_re_hi, in0=d_re_hi, in1=tAv, op=ALU.subtract)
            S.I("dve", "tensor_tensor", (t_src, t_LP, t_BU[0]), (t_BU[0],), out=tAv, in0=s_re_lo, in1=aim, op=ALU.mult)
            S.I("dve", "tensor_tensor", (t_BU[0], t_dst), (t_dst,), out=d_im_hi, in0=d_im_hi, in1=tAv, op=ALU.add)
        if part == 2:
            S.I("dve", "tensor_copy", (t_tmp, t_XS), (t_XS,), out=XS[:, :, :, 1:33], in_=XS2)

    pending = []

    def filler(c):
        if c == -1:
            while pending:
                pending.pop(0)()
        elif c % 2 == 1 and pending:
            pending.pop(0)()

    def ssm_items(bi):
        items = []
        for q0 in range(0, 16, 2):
            def pp_item(q0=q0):
                ssm_tree_two(q0, BT)
                for i in range(2):
                    S.I("pool", "tensor_copy", (t_BU[i], t_XS), (t_XS,), out=XS[:, :, q0 + i, 1:NSC + 1],
                        in_=BU[i][:, :, 7:BT:8])
            items.append(pp_item)

        items.append(lambda: level_b(NSC, 0))
        items.append(lambda: level_b(NSC, 1))

        def lb_item(bi=bi):
            level_b(NSC, 2)
            if bi >= OWN0:
                S.dma(xss[:, :, :, (bi - OWN0) * NSC:(bi - OWN0 + 1) * NSC], XS[:, :, :, 0:NSC], reads=(t_XS,),
                      sem_t=t_XS, slow=True)
            if bi == NBLK - 1:
                S.dma(o_hp, XS[:, :, :, NSC], reads=(t_XS,), sem_t=t_XS, slow=True)
            S.I("dve", "tensor_copy", (t_XS,), (t_XS,), out=XS[:, :, :, 0], in_=XS[:, :, :, NSC])
        items.append(lb_item)
        return items

    for bi in range(NBLK):
        S.dma(xt[:], x_seq[bi * BT:(bi + 1) * BT, :].rearrange("(t p) d -> p t d", p=128), writes=(t_xt,))
        ffn(NTB, 0, False, filler)
        if bi >= HALO0:
            S.dma(x1s[(bi - HALO0) * BT:(bi - HALO0 + 1) * BT, :].rearrange("(t p) d -> p t d", p=128), xt[:],
                  reads=(t_xt,), writes=(t_x1s,), sem_t=t_xt)
        norm_mod_T(NTB, 1, False)
        for j in range(4):
            pp, tp = (p_a, t_pa) if j % 2 == 0 else (p_c, t_pc)
            S.MM([("matmul", (pp[:, 0:BT],), dict(lhsT=wuin[:, k, j * 128:(j + 1) * 128], rhs=hT[:, k, :],
                                                  start=(k == 0), stop=(k == 7))) for k in range(8)],
                 (t_wuin, t_hT), (tp,))
            S.I("dve", "tensor_scalar", (tp, t_const), (t_uT,), out=uT[:, j, :], in0=pp[:, 0:BT],
                scalar1=blkvalid[:, bi:bi + 1], scalar2=None, op0=ALU.mult)
        if bi >= OWN0:
            S.dma(uTs[:, :, (bi - OWN0) * BT:(bi - OWN0 + 1) * BT], uT[:], reads=(t_uT,), sem_t=t_uT)
        pending.extend(ssm_items(bi))
    filler(-1)

    S.dma(xt[:, 0, :], x_smp, writes=(t_xt,))
    ffn(1, 0, True)
    S.dma(x1s[32 * 128:33 * 128, :], xt[:, 0, :], reads=(t_xt,), writes=(t_x1s,), sem_t=t_xt)
    norm_mod_T(1, 1, True)
    for j in range(4):
        pp, tp = (p_a, t_pa) if j % 2 == 0 else (p_c, t_pc)
        S.MM([("matmul", (pp[:, 0:128],), dict(lhsT=wuin[:, k, j * 128:(j + 1) * 128], rhs=hT[:, k, 0:128],
                                               start=(k == 0), stop=(k == 7))) for k in range(8)],
             (t_wuin, t_hT), (tp,))
        S.I("dve", "tensor_copy", (tp,), (t_uT,), out=uT[:, j, 0:128], in_=pp[:, 0:128])
    S.dma(uTs[:, :, TOK_CORE:TOK_CORE + 128], uT[:, :, 0:128], reads=(t_uT,), sem_t=t_uT)
    for q in range(16):
        bufi = bufc[0] % 2
        bufc[0] += 1
        ssm_bu_and_scan(q, 128, bufi)
        S.I("pool", "tensor_copy", (t_BU[bufi], t_XS), (t_XS,), out=XS[:, :, q, 16:32], in_=BU[bufi][:, :, 7:128:8])
    S.dma(h0[:, 0], h0re_in, writes=(t_h0,))
    S.dma(h0[:, 1], h0im_in, writes=(t_h0,))
    a8re_b = LP[:, :, 8, 0].unsqueeze(2).to_broadcast((128, 16, SB))
    a8im_b = LP[:, :, 8, 1].unsqueeze(2).to_broadcast((128, 16, SB))
    S.I("dve", "tensor_tensor", (t_h0, t_LP), (t_hend,), out=hend[:, 0], in0=h0[:, 0], in1=a8re_b, op=ALU.mult)
    S.I("dve", "tensor_tensor", (t_h0, t_LP), (t_hend,), out=hend[:, 1], in0=h0[:, 1], in1=a8re_b, op=ALU.mult)
    S.I("dve", "tensor_tensor", (t_h0, t_LP, t_XS), (t_XS,), out=XS[:, 0, :, 0:16], in0=h0[:, 1], in1=a8im_b, op=ALU.mult)
    S.I("dve", "tensor_tensor", (t_h0, t_LP, t_XS), (t_XS,), out=XS[:, 1, :, 0:16], in0=h0[:, 0], in1=a8im_b, op=ALU.mult)
    S.I("dve", "tensor_tensor", (t_hend, t_XS), (t_hend,), out=hend[:, 0], in0=hend[:, 0], in1=XS[:, 0, :, 0:16],
        op=ALU.subtract)
    S.I("dve", "tensor_tensor", (t_hend, t_XS), (t_hend,), out=hend[:, 1], in0=hend[:, 1], in1=XS[:, 1, :, 0:16],
        op=ALU.add)
    S.I("dve", "tensor_tensor", (t_hend, t_XS), (t_hend,), out=hend[:, 0], in0=hend[:, 0], in1=XS[:, 0, :, 16:32],
        op=ALU.add)
    S.I("dve", "tensor_tensor", (t_hend, t_XS), (t_hend,), out=hend[:, 1], in0=hend[:, 1], in1=XS[:, 1, :, 16:32],
        op=ALU.add)
    S.dma(o_hs, hend, reads=(t_hend,), sem_t=t_hend)

    t_arena = T("arena")

    def AV(off, n):
        return arena[:, off:off + n]

    NTOK = TOK_CORE + 128
    VE = 80
    QTz = AV(0, 8 * NTOK).rearrange("p (h n) -> p h n", h=NH)
    catA = QTz[:, 0:4, :]
    KT = AV(17408, 4 * NSP * 128).rearrange("p (c n) -> p c n", c=4)
    VX = AV(34304, NSP * NH * VE).rearrange("p (t h e) -> p t h e", t=NSP, h=NH)
    WQKV = AV(55424, 8 * 1024).rearrange("p (k n) -> p k n", k=8)
    MS = AV(63616, 17 * 128).rearrange("p (t b i) -> p t b i", t=17, b=16)
    Es = AV(65792, 17 * 64).rearrange("p (t h i) -> p t h i", t=17, h=NH)
    catS = AV(8704, 4 * NTOK).rearrange("p (c n) -> p c n", c=4)
    lhsC = AV(17408, 4096).rearrange("p (q r n) -> p q r n", q=16, r=2)
    wglu = AV(21504, 4096).rearrange("p (k n) -> p k n", k=4)
    gTs = AV(25600, 4 * NTOK).rearrange("p (c n) -> p c n", c=4)
    Hbf = AV(34304, 2048).rearrange("p (q r n) -> p q r n", q=4, r=2)
    wout = AV(36352, 8192).rearrange("p (k n) -> p k n", k=8)
    KTs = AV(55424, 8192).rearrange("p (c n) -> p c n", c=4)
    VXs = AV(34304, 16 * NH * VE).rearrange("p (t h e) -> p t h e", t=16, h=NH)
    Ppad = [AV(44544 + i * 2176, 2176).rearrange("p (t m) -> p t m", t=17) for i in range(2)]
    actflat = actT[:].rearrange("p c n -> p (c n)")
    Mk = actflat[:, 0:17 * 128].rearrange("p (d q) -> p d q", d=17)
    Pb = [actflat[:, 2176 + i * 512:2176 + (i + 1) * 512] for i in range(2)]

    mask_in = din("maskp", [128, 17, 128])
    masks_in = din("masks", [128, 17, 16, 8])
    halo_valid_in = din("halo_valid", [128, 1])
    gqk_in = din("gqk", [1, 2 * HD])
    rope_in = din("rope", [128, NSP, 2, 8])
    cexre_in = din("ssm_cre", [128, 16, 128])
    cexim_in = din("ssm_cim", [128, 16, 128])
    w_glu_in = din("w_glu", [512, 1024])
    w_out_in = din("w_out", [D, D])
    w2g = din("w2_gate", [D, DFF])
    w2u = din("w2_up", [D, DFF])
    w2d = din("w2_down", [DFF, D])
    o_kp = dout("o_kp", [TOK_CORE, 512])
    o_vp = dout("o_vp", [TOK_CORE, 512])
    o_yp = dout("o_yp", [TOK_CORE, D])
    o_ys = dout("o_ys", [128, D])

    rope = sb("rope_sb", [128, NSP, 2, 8])
    gqk = sig[:].rearrange("p k n -> p (k n)")[:, 0:2 * HD]
    cTf = cT[:].rearrange("p k n -> p (k n)")
    hv = cTf[:, 0:1]
    kss = cTf[:, 8:16]
    krs = cTf[:, 16:24]
    rt = PR[:].rearrange("p s n -> p (s n)")[:, 0:256].rearrange("p (a h d) -> p a h d", a=4, h=8)
    t_rope, t_kss, t_krs, t_rt, t_MS, t_Mk = T("rope"), T("kss"), T("krs"), T("rt"), T("MS"), T("Mk")
    t_catA, t_QT, t_KT, t_VX, t_WQKV = T("catA"), T("QT"), T("KT"), T("VX"), T("WQKV")
    t_Pb = [T("Pb0"), T("Pb1")]
    t_Ppad = [T("Pp0"), T("Pp1")]
    t_KTs, t_Es, t_VXs = T("KTs"), T("Es"), T("VXs")

    def barrier_on(trackers, engines=ENG):
        for tt in trackers:
            for e in engines:
                S._wait(e, tt.w)
                for d in tt.r:
                    S._wait(e, d)
    t_catS, t_lhsC, t_wglu, t_gTs, t_Hbf, t_wout = T("catS"), T("lhsC"), T("wglu"), T("gTs"), T("Hbf"), T("wout")
    t_out = T("outst")

    for tt in (t_wg, t_wu, t_wd, t_act):
        for e in ("pe", "act", "dve", "pool", "sp"):
            S._wait(e, tt.w)
            for d in tt.r:
                S._wait(e, d)

    S.dma(rope[:], rope_in, writes=(t_rope,))
    S.dma(gqk, gqk_in.to_broadcast((128, 2 * HD)), writes=(t_rope,))
    S.dma(hv, halo_valid_in, writes=(t_rope,))
    for d0 in range(0, 17, 8):
        d1 = min(17, d0 + 8)
        S.dma(tmp[:, 0:(d1 - d0) * 128], mask_in[:, d0:d1, :].rearrange("p d q -> p (d q)"), writes=(t_tmp,))
        S.I("dve", "tensor_copy", (t_tmp,), (t_Mk,), out=Mk[:, d0:d1, :].rearrange("p d q -> p (d q)"),
            in_=tmp[:, 0:(d1 - d0) * 128])
    for t0 in range(0, 17, 8):
        t1 = min(17, t0 + 8)
        S.dma(tmp[:, 0:(t1 - t0) * 128], masks_in[:, t0:t1].rearrange("p t b i -> p (t b i)"), writes=(t_tmp,))
        S.I("dve", "tensor_copy", (t_tmp,), (t_MS,), out=MS[:, t0:t1].rearrange("p t b i -> p (t b i)"),
            in_=tmp[:, 0:(t1 - t0) * 128])

    def ada_gate_table(li, stage_ap, t_stage):
        for half in range(2):
            blk = (li * 3 + 2) * 2 + half
            for k in range(8):
                load_cast(stage_ap[:, k, :], w_ada[k * 128:(k + 1) * 128, blk * 512:(blk + 1) * 512], 512, t_stage)
            S.dma(bada_blk, bada_row[0:1, blk * 512:(blk + 1) * 512].to_broadcast((17, 512)), writes=(t_bblk,))
            S.MM([("matmul", (p_b[0:17, :],), dict(lhsT=sT[:, k, :], rhs=stage_ap[:, k, :], start=(k == 0), stop=(k == 7)))
                  for k in range(8)], (t_stage, t_const), (t_pb,))
            S.I("dve", "tensor_tensor", (t_pb, t_bblk), (t_grow,), out=grow, in0=p_b[0:17, :], in1=bada_blk, op=ALU.add)
            for which in range(2):
                S.MM([("matmul", (p_c[:, :],), dict(lhsT=esel_f[:, which, :], rhs=grow, start=True, stop=True))],
                     (t_grow, t_const), (t_pc,))
                S.I("act", "activation", (t_pc,), (t_gtab,), out=gtab[:, 0, which, half * 512:(half + 1) * 512],
                    in_=p_c[:, :], func=AF.Copy)

    stage_wq = WQKV[:, :, 0:512]
    ada_gate_table(1, stage_wq, t_WQKV)
    for k in range(8):
        load_cast(WQKV[:, k, :], w_in[k * 128:(k + 1) * 128, 512:1536], 1024, t_WQKV)
    S.I("dve", "memset", (t_VX,), (t_VX,), VX[:, :, :, 64:65], 1.0)
    S.I("pool", "memset", (t_QT,), (t_QT,), QTz, 0.0)

    def qk_norm_rope(buf, gsl, ti, tb):
        b3 = buf.rearrange("p (h d) -> p h d", h=NH)
        S.I("act", "activation", (tb,), (t_tmp,), out=tmp[:, 0:512], in_=buf, func=AF.Square)
        S.I("dve", "tensor_reduce", (t_tmp,), (t_kss,), out=kss, in_=tmp[:, 0:512].rearrange("p (h d) -> p h d", h=NH),
            axis=mybir.AxisListType.X, op=ALU.add)
        S.I("act", "activation", (t_kss, t_const), (t_krs,), out=krs, in_=kss, func=AF.Sqrt, scale=1.0 / HD,
            bias=eps_t[:, 0:1])
        S.I("dve", "reciprocal", (t_krs,), (t_krs,), out=krs, in_=krs)
        S.I("dve", "tensor_tensor", (tb, t_krs), (tb,), out=b3, in0=b3,
            in1=krs.unsqueeze(2).to_broadcast((128, NH, HD)), op=ALU.mult)
        S.I("dve", "tensor_tensor", (tb, t_rope), (tb,), out=b3, in0=b3,
            in1=gqk[:, gsl].unsqueeze(1).to_broadcast((128, NH, HD)), op=ALU.mult)
        cosb = rope[:, ti, 0, :].unsqueeze(1).to_broadcast((128, NH, 8))
        sinb = rope[:, ti, 1, :].unsqueeze(1).to_broadcast((128, NH, 8))
        xa, xb = b3[:, :, 0:8], b3[:, :, 8:16]
        S.I("dve", "tensor_tensor", (tb, t_rope), (t_rt,), out=rt[:, 0], in0=xa, in1=cosb, op=ALU.mult)
        S.I("dve", "tensor_tensor", (tb, t_rope), (t_rt,), out=rt[:, 1], in0=xb, in1=sinb, op=ALU.mult)
        S.I("dve", "tensor_tensor", (tb, t_rope), (t_rt,), out=rt[:, 2], in0=xb, in1=cosb, op=ALU.mult)
        S.I("dve", "tensor_tensor", (tb, t_rope), (t_rt,), out=rt[:, 3], in0=xa, in1=sinb, op=ALU.mult)
        S.I("dve", "tensor_tensor", (t_rt, tb), (tb,), out=xa, in0=rt[:, 0], in1=rt[:, 1], op=ALU.subtract)
        S.I("dve", "tensor_tensor", (t_rt, tb), (tb,), out=xb, in0=rt[:, 2], in1=rt[:, 3], op=ALU.add)

    def to_featmajor(buf, tb, scale):
        S.I("dve", "tensor_scalar", (tb,), (t_xn,), out=xn[:, 0:512], in0=buf, scalar1=scale, scalar2=None, op0=ALU.mult)
        S.MM([("transpose", (p_t[:, c * 128:(c + 1) * 128], xn[:, c * 128:(c + 1) * 128], ident_b[:]), {})
              for c in range(4)], (t_xn, t_const), (t_pt,))
        return p_t[:, 0:512].rearrange("p (c n) -> p c n", c=4)

    kbuf = BU[0][:].rearrange("p r n -> p (r n)")
    vbuf = BU[1][:].rearrange("p r n -> p (r n)")
    for ti in range(NSP if STOP >= 1 else 0):
        is_sample = ti == 32
        is_own = 16 <= ti < 32
        S.dma(xt[:, 0, :], x1s[ti * 128:(ti + 1) * 128, :], reads=(t_x1s,), writes=(t_xt,))
        norm_mod_T(1, 1, is_sample)
        S.MM([("matmul", (p_a[:, :],), dict(lhsT=hT[:, k, 0:128], rhs=WQKV[:, k, 512:1024], start=(k == 0), stop=(k == 7)))
              for k in range(8)], (t_hT, t_WQKV), (t_pa,))
        S.I("act", "activation", (t_pa,), (t_BU[1],), out=vbuf, in_=p_a[:, :], func=AF.Copy)
        if ti < 16:
            S.I("dve", "tensor_scalar", (t_BU[1], t_rope, t_VX), (t_VX,), out=VX[:, ti, :, 0:64],
                in0=vbuf.rearrange("p (h d) -> p h d", h=NH), scalar1=hv, scalar2=None, op0=ALU.mult)
            S.I("dve", "tensor_scalar", (t_rope, t_VX), (t_VX,), out=VX[:, ti, :, 64:65], in0=VX[:, ti, :, 64:65],
                scalar1=hv, scalar2=None, op0=ALU.mult)
        else:
            S.I("dve", "tensor_copy", (t_BU[1], t_VX), (t_VX,), out=VX[:, ti, :, 0:64],
                in_=vbuf.rearrange("p (h d) -> p h d", h=NH))
        if is_own:
            S.dma(o_vp[(ti - 16) * 128:(ti - 15) * 128, :], vbuf, reads=(t_BU[1],), sem_t=t_BU[1])
        if is_sample:
            for b in range(SB):
                S.dma(o_vs[b, WBUF - 8:WBUF, :], BU[1][b * 8:(b + 1) * 8].rearrange("p r n -> p (r n)"),
                      reads=(t_BU[1],), sem_t=t_BU[1])
        S.MM([("matmul", (p_c[:, :],), dict(lhsT=hT[:, k, 0:128], rhs=WQKV[:, k, 0:512], start=(k == 0), stop=(k == 7)))
              for k in range(8)], (t_hT, t_WQKV), (t_pc,))
        S.I("act", "activation", (t_pc,), (t_BU[0],), out=kbuf, in_=p_c[:, :], func=AF.Copy)
        qk_norm_rope(kbuf, slice(HD, 2 * HD), ti, t_BU[0])
        if is_own:
            S.dma(o_kp[(ti - 16) * 128:(ti - 15) * 128, :], kbuf, reads=(t_BU[0],), sem_t=t_BU[0])
        if is_sample:
            for b in range(SB):
                S.dma(o_ks[b, WBUF - 8:WBUF, :], BU[0][b * 8:(b + 1) * 8].rearrange("p r n -> p (r n)"),
                      reads=(t_BU[0],), sem_t=t_BU[0])
        src = to_featmajor(kbuf, t_BU[0], 1.0)
        S.I("act", "activation", (t_pt, t_KT), (t_KT,), out=KT[:, :, ti * 128:(ti + 1) * 128], in_=src, func=AF.Copy)
    if STOP >= 1:
        for k in range(8):
            load_cast(WQKV[:, k, 0:512], w_in[k * 128:(k + 1) * 128, 0:512], 512, t_WQKV)
    for ti in range(16, NSP if STOP >= 1 else 16):
        is_sample = ti == 32
        S.dma(xt[:, 0, :], x1s[ti * 128:(ti + 1) * 128, :], reads=(t_x1s,), writes=(t_xt,))
        norm_mod_T(1, 1, is_sample)
        S.MM([("matmul", (p_b[:, :],), dict(lhsT=hT[:, k, 0:128], rhs=WQKV[:, k, 0:512], start=(k == 0), stop=(k == 7)))
              for k in range(8)], (t_hT, t_WQKV), (t_pb,))
        S.I("act", "activation", (t_pb,), (t_BU[0],), out=kbuf, in_=p_b[:, :], func=AF.Copy)
        qk_norm_rope(kbuf, slice(0, HD), ti, t_BU[0])
        src = to_featmajor(kbuf, t_BU[0], HD ** -0.5)
        cs_ = slice((ti - 16) * 128, (ti - 15) * 128)
        S.I("act", "activation", (t_pt, t_QT), (t_QT,), out=QTz[0:64, 0:NH:2, cs_], in_=src[0:64], func=AF.Copy)
        S.I("act", "activation", (t_pt, t_QT), (t_QT,), out=QTz[64:128, 1:NH:2, cs_], in_=src[64:128], func=AF.Copy)

    def attn_finish(a, num_den, t_nd, obuf, t_ob):
        for g in range(2):
            acc, tacc = num_den(g), t_nd[g]
            S.I("dve", "reciprocal", (tacc,), (t_kss,), out=kss[:, 4 * g:4 * g + 4], in_=acc[:, :, 64])
            S.I("dve", "tensor_tensor", (tacc, t_kss), (t_ob,),
                out=obuf[:, g * 256:(g + 1) * 256].rearrange("p (h d) -> p h d", h=4), in0=acc[:, :, 0:64],
                in1=kss[:, 4 * g:4 * g + 4].unsqueeze(2).to_broadcast((128, 4, HD)), op=ALU.mult)
        src = to_featmajor(obuf, t_ob, 1.0)
        S.I("act", "activation", (t_pt, t_QT), (t_catA, t_QT), out=catA[:, :, a * 128:(a + 1) * 128], in_=src, func=AF.Copy)

    def acc_view(g):
        return (p_s1 if g == 0 else p_s2)[:, 0:4 * VE].rearrange("p (h e) -> p h e", h=4)

    sbanks = ((p_a, t_pa), (p_b, t_pb), (p_c, t_pc), (p_d, t_pd))
    it = [0]
    for a in range(16 if STOP >= 2 else 0):
        for h in range(NH):
            c = h // 2
            pacc, tacc = (p_s1, t_ps1) if h < 4 else (p_s2, t_ps2)
            hh = h % 4
            for d0 in range(0, 17, 4):
                dls = list(range(d0, min(17, d0 + 4)))
                nd = len(dls)
                pbk, tbk = sbanks[it[0] % 4]
                i2 = it[0] % 2
                it[0] += 1
                S.MM([("matmul", (pbk[:, i * 128:(i + 1) * 128],),
                       dict(lhsT=KT[:, c, (16 + a - dl) * 128:(17 + a - dl) * 128], rhs=QTz[:, h, a * 128:(a + 1) * 128],
                            start=True, stop=True)) for i, dl in enumerate(dls)], (t_KT, t_QT), (tbk,))
                S.I("act", "activation", (tbk,), (t_Pb[i2],), out=Pb[i2][:, 0:nd * 128], in_=pbk[:, 0:nd * 128], func=AF.Exp)
                S.I("dve", "tensor_tensor", (t_Pb[i2], t_Mk), (t_Pb[i2],), out=Pb[i2][:, 0:nd * 128], in0=Pb[i2][:, 0:nd * 128],
                    in1=Mk[:, d0:d0 + nd, :].rearrange("p d q -> p (d q)"), op=ALU.mult)
                S.MM([("matmul", (pacc[:, hh * VE:hh * VE + 65],),
                       dict(lhsT=Pb[i2][:, i * 128:(i + 1) * 128], rhs=VX[:, 16 + a - dl, h, 0:65],
                            start=(dl == 0), stop=(dl == 16))) for i, dl in enumerate(dls)], (t_Pb[i2], t_VX), (tacc,))
        attn_finish(a, acc_view, (t_ps1, t_ps2), tmp[:, 0:512], t_tmp)

    barrier_on((t_WQKV, t_VX, t_Pb[0], t_Pb[1]))
    Os = tmp[:, 0:NH * VE].rearrange("p (h e) -> p h e", h=NH)
    S.I("dve", "memset", (t_VXs,), (t_VXs,), VXs[:, :, :, 64:65], 1.0)
    stg = [xt[:, 0, 0:512], xt[:, 0, 512:1024], xt[:, 1, 0:512], xt[:, 1, 512:1024]]
    t_stg = [T("stg0"), T("stg1"), T("stg2"), T("stg3")]
    for tsx in t_stg:
        tsx.w = t_xt.w
        tsx.r = list(t_xt.r)
    sc = [0]
    pp_i = [0]
    for b in range(SB if STOP >= 3 else 0):
        for t in range(16):
            si = sc[0] % 4
            sc[0] += 1
            S.dma(stg[si], ck[b, t * 128:(t + 1) * 128, :], writes=(t_stg[si],))
            S.MM([("transpose", (p_a[:, c * 128:(c + 1) * 128], stg[si][:, c * 128:(c + 1) * 128], ident_f[:]), {})
                  for c in range(4)], (t_stg[si], t_const), (t_pa,))
            S.I("act", "activation", (t_pa, t_KTs), (t_KTs,), out=KTs[:, :, t * 128:(t + 1) * 128],
                in_=p_a[:, :].rearrange("p (c n) -> p c n", c=4), func=AF.Copy)
            si = sc[0] % 4
            sc[0] += 1
            S.dma(stg[si], cv[b, t * 128:(t + 1) * 128, :], writes=(t_stg[si],))
            S.I("pool", "tensor_copy", (t_stg[si], t_VXs), (t_VXs,), out=VXs[:, t, :, 0:64],
                in_=stg[si].rearrange("p (h d) -> p h d", h=NH))
        qs = slice(TOK_CORE + b * 8, TOK_CORE + b * 8 + 8)
        for (pbk, tbk, t_lo, t_hi) in ((p_b, t_pb, 0, 8), (p_c, t_pc, 8, 16), (p_d, t_pd, 16, 17)):
            items = []
            for t in range(t_lo, t_hi):
                for h in range(NH):
                    c = h // 2
                    lhs = KTs[:, c, t * 128:(t + 1) * 128] if t < 16 else KT[:, c, 32 * 128:33 * 128]
                    col = ((t - t_lo) * NH + h) * 8
                    items.append(("matmul", (pbk[:, col:col + 8],), dict(lhsT=lhs, rhs=QTz[:, h, qs], start=True, stop=True)))
            S.MM(items, (t_KTs, t_KT, t_QT), (tbk,))
            ncol = (t_hi - t_lo) * 64
            S.I("act", "activation", (tbk, t_Es), (t_Es,), out=Es[:, t_lo:t_hi].rearrange("p t h i -> p (t h i)"),
                in_=pbk[:, 0:ncol], func=AF.Exp)
        for h in range(NH):
            pi = pp_i[0] % 2
            pp_i[0] += 1
            pacc, tacc = (p_s1, t_ps1) if h < 4 else (p_s2, t_ps2)
            hh = h % 4
            S.I("dve", "memset", (t_Ppad[pi],), (t_Ppad[pi],), Ppad[pi], 0.0)
            S.I("dve", "tensor_tensor", (t_Es, t_MS, t_Ppad[pi]), (t_Ppad[pi],), out=Ppad[pi][:, :, b * 8:(b + 1) * 8],
                in0=Es[:, :, h, :], in1=MS[:, :, b, :], op=ALU.mult)
            S.MM([("matmul", (pacc[:, hh * VE:hh * VE + 65],),
                   dict(lhsT=Ppad[pi][:, t, :], rhs=(VXs[:, t, h, 0:65] if t < 16 else VX[:, 32, h, 0:65]),
                        start=(t == 0), stop=(t == 16))) for t in range(17)], (t_Ppad[pi], t_VXs, t_VX), (tacc,))
        for g in range(2):
            tacc = t_ps1 if g == 0 else t_ps2
            if b == 0:
                S.I("dve", "tensor_copy", (tacc, t_tmp), (t_tmp,), out=Os[:, 4 * g:4 * g + 4, 0:65], in_=acc_view(g)[:, :, 0:65])
            else:
                S.I("dve", "tensor_tensor", (tacc, t_tmp), (t_tmp,), out=Os[:, 4 * g:4 * g + 4, 0:65],
                    in0=acc_view(g)[:, :, 0:65], in1=Os[:, 4 * g:4 * g + 4, 0:65], op=ALU.add)
    if STOP >= 3:
        attn_finish(16, lambda g: Os[:, 4 * g:4 * g + 4, :], (t_tmp, t_tmp), kbuf, t_BU[0])
    t_xt.w = None
    t_xt.r = [d for tsx in t_stg for d in ([tsx.w] + tsx.r) if d is not None]

    barrier_on((t_KT, t_VX, t_QT, t_KTs, t_Es, t_VXs, t_WQKV, t_Ppad[0], t_Ppad[1]))
    S.dma(tmp[:, :], cexre_in[:, 0:8, :].rearrange("p q n -> p (q n)"), writes=(t_tmp,))
    S.I("dve", "tensor_copy", (t_tmp, t_lhsC), (t_lhsC,), out=lhsC[:, 0:8, 0, :], in_=tmp[:].rearrange("p (q n) -> p q n", q=8))
    S.dma(tmp[:, :], cexre_in[:, 8:16, :].rearrange("p q n -> p (q n)"), writes=(t_tmp,))
    S.I("dve", "tensor_copy", (t_tmp, t_lhsC), (t_lhsC,), out=lhsC[:, 8:16, 0, :], in_=tmp[:].rearrange("p (q n) -> p q n", q=8))
    for q0 in (0, 8):
        S.dma(tmp[:, :], cexim_in[:, q0:q0 + 8, :].rearrange("p q n -> p (q n)"), writes=(t_tmp,))
        S.I("dve", "tensor_scalar", (t_tmp, t_lhsC), (t_lhsC,), out=lhsC[:, q0:q0 + 8, 1, :],
            in0=tmp[:].rearrange("p (q n) -> p q n", q=8), scalar1=-1.0, scalar2=None, op0=ALU.mult)
    for k in range(4):
        load_cast(wglu[:, k, :], w_glu_in[k * 128:(k + 1) * 128, :], 1024, t_wglu)
    for k in range(8):
        load_cast(wout[:, k, :], w_out_in[k * 128:(k + 1) * 128, :], 1024, t_wout)
    for c in range(NFF):
        load_cast(wd[:, c, :], w2d[c * 128:(c + 1) * 128, :], D, t_wd)

    for ob in range(9 if STOP >= 4 else 0):
        is_sample = ob == 8
        n = 128 if is_sample else BT
        nsc = n // 8
        tok0 = TOK_CORE if is_sample else ob * BT
        S.dma(uT[:, :, 0:n], uTs[:, :, tok0:tok0 + n], writes=(t_uT,))
        if is_sample:
            S.dma(XS[:, 0, :, 0:16], h0re_in, writes=(t_XS,))
            S.dma(XS[:, 1, :, 0:16], h0im_in, writes=(t_XS,))
        else:
            S.dma(XS[:, :, :, 0:NSC], xss[:, :, :, ob * NSC:(ob + 1) * NSC], writes=(t_XS,), slow=True)
        for j in range(4):
            for qq in (0, 2):
                ssm_full_two(4 * j + qq, n, nsc)
            S.MM([("matmul", (p_a[:, 0:n],), dict(lhsT=lhsC[:, 4 * j + qq, ri, :], rhs=Hbf[:, qq, ri, 0:n],
                                                  start=(qq == 0 and ri == 0), stop=(qq == 3 and ri == 1)))
                  for qq in range(4) for ri in range(2)], (t_lhsC, t_Hbf), (t_pa,))
            yv = tmp[:, 0:n]
            y2 = tmp[:, 256:256 + n]
            S.I("dve", "scalar_tensor_tensor", (t_uT, t_const, t_pa), (t_tmp,), out=yv, in0=uT[:, j, 0:n],
                scalar=dT[:, j:j + 1], in1=p_a[:, 0:n], op0=ALU.mult, op1=ALU.add)
            S.I("dve", "tensor_tensor", (t_tmp,), (t_tmp,), out=y2, in0=yv, in1=yv, op=ALU.mult)
            S.I("dve", "tensor_scalar", (t_tmp,), (t_tmp,), out=y2, in0=y2, scalar1=0.044715, scalar2=1.0,
                op0=ALU.mult, op1=ALU.add)
            S.I("dve", "tensor_tensor", (t_tmp,), (t_tmp,), out=y2, in0=y2, in1=yv, op=ALU.mult)
            S.I("act", "activation", (t_tmp,), (t_tmp,), out=y2, in_=y2, func=AF.Sigmoid, scale=2.0 * math.sqrt(2.0 / PI))
            S.I("dve", "tensor_tensor", (t_tmp, t_gTs), (t_gTs,), out=gTs[:, j, tok0:tok0 + n], in0=yv, in1=y2, op=ALU.mult)
    for t0 in range(0, NTOK if STOP >= 4 else 0, 512):
        n = min(512, NTOK - t0)
        for c in range(4):
            S.MM([("matmul", (p_a[:, 0:n],), dict(lhsT=wglu[:, k, c * 128:(c + 1) * 128], rhs=gTs[:, k, t0:t0 + n],
                                                  start=(k == 0), stop=(k == 3))) for k in range(4)],
                 (t_wglu, t_gTs), (t_pa,))
            S.MM([("matmul", (p_b[:, 0:n],), dict(lhsT=wglu[:, k, 512 + c * 128:512 + (c + 1) * 128], rhs=gTs[:, k, t0:t0 + n],
                                                  start=(k == 0), stop=(k == 3))) for k in range(4)],
                 (t_wglu, t_gTs), (t_pb,))
            S.I("act", "activation", (t_pb,), (t_tmp,), out=tmp[:, 0:n], in_=p_b[:, 0:n], func=AF.Sigmoid)
            S.I("dve", "tensor_tensor", (t_tmp, t_pa, t_catS), (t_catS,), out=catS[:, c, t0:t0 + n], in0=tmp[:, 0:n],
                in1=p_a[:, 0:n], op=ALU.mult)

    for ti in range(17 if STOP >= 5 else 0):
        which = 1 if ti == 16 else 0
        S.dma(xt[:, 0, :], x1s[(16 + ti) * 128:(17 + ti) * 128, :], reads=(t_x1s,), writes=(t_xt,))
        for half in range(2):
            pp, tp = (p_a, t_pa) if half == 0 else (p_c, t_pc)
            S.MM([("matmul", (pp[:, :],),
                   dict(lhsT=(catA[:, k, ti * 128:(ti + 1) * 128] if k < 4 else catS[:, k - 4, ti * 128:(ti + 1) * 128]),
                        rhs=wout[:, k, half * 512:(half + 1) * 512], start=(k == 0), stop=(k == 7))) for k in range(8)],
                 (t_catA, t_catS, t_wout), (tp,))
            S.I("dve", "tensor_tensor", (tp, t_gtab), (t_tmp,), out=tmp[:, half * 512:(half + 1) * 512], in0=pp[:, :],
                in1=gtab[:, 0, which, half * 512:(half + 1) * 512], op=ALU.mult)
        S.I("dve", "tensor_tensor", (t_tmp, t_xt), (t_xt,), out=xt[:, 0, :], in0=tmp[:], in1=xt[:, 0, :], op=ALU.add)
        S.dma(x1s[(16 + ti) * 128:(17 + ti) * 128, :], xt[:, 0, :], reads=(t_xt,), writes=(t_x1s,), sem_t=t_xt)

    for tt in (t_catA, t_catS, t_wout, t_wglu, t_gTs, t_Hbf, t_lhsC, t_KT, t_VX, t_QT, t_WQKV, t_KTs, t_Es, t_VXs):
        for e in ("pool", "sp", "pe"):
            S._wait(e, tt.w)
            for d in tt.r:
                S._wait(e, d)
    ada_gate_table(2, AV(0, 4096).rearrange("p (k n) -> p k n", k=8), t_wg)
    for k in range(8):
        load_cast(wg[:, k, :], w2g[k * 128:(k + 1) * 128, :], DFF, t_wg)
        load_cast(wu[:, k, :], w2u[k * 128:(k + 1) * 128, :], DFF, t_wu)
    for ob in range(8 if STOP >= 6 else 0):
        S.dma(xt[:], x1s[(16 + 2 * ob) * 128:(18 + 2 * ob) * 128, :].rearrange("(t p) d -> p t d", p=128),
              reads=(t_x1s,), writes=(t_xt,))
        ffn(NTB, 2, False)
        S.dma(o_yp[ob * BT:(ob + 1) * BT, :].rearrange("(t p) d -> p t d", p=128), xt[:], reads=(t_xt,), sem_t=t_xt)
    if STOP >= 6:
        S.dma(xt[:, 0, :], x1s[32 * 128:33 * 128, :], reads=(t_x1s,), writes=(t_xt,))
        ffn(1, 2, True)
        S.dma(o_ys, xt[:, 0, :], reads=(t_xt,), sem_t=t_xt)

    for t in T.registry:
        if t.dkey is not None:
            S._wait("sp", (t.dkey, 16 * t.dcnt))

    for e in ENG:
        S.sems[e] = es.enter_context(nc.semaphore("s_" + e))
    for n in S.dsem_names:
        S.sems[n] = es.enter_context(nc.semaphore(n))
    with es:
        with nc.Block() as block:
            @block.tensor
            def _(E):
                for f in S.prog["pe"]:
                    f(E)

            @block.scalar
            def _(E):
                for f in S.prog["act"]:
                    f(E)

            @block.vector
            def _(E):
                for f in S.prog["dve"]:
                    f(E)

            @block.gpsimd
            def _(E):
                for f in S.prog["pool"]:
                    f(E)

            @block.sync
            def _(E):
                for f in S.prog["sp"]:
                    f(E)
    return nc


_NC = None


def _rope_tables(pos):
    half = 8
    inv = (500000.0 ** (-np.arange(half, dtype=np.float32) / half)).astype(np.float32)
    ang = pos.astype(np.float32)[:, None] * inv[None, :]
    return np.cos(ang).astype(np.float32), np.sin(ang).astype(np.float32)


def kernel(x_prompt, x_sample, c_prompt, c_sample, cache_k_win, cache_v_win,
           state_ssm_re, state_ssm_im, w_ada, b_ada, g_ffn1, w1_gate, w1_up, w1_down,
           g_mix, w_in, g_q, g_k, ssm_a_re, ssm_a_im, ssm_log_dt, ssm_b_re, ssm_b_im,
           ssm_c_re, ssm_c_im, ssm_d, w_glu, w_out, g_ffn2, w2_gate, w2_up, w2_down):
    global _NC
    if _NC is None:
        _NC = build_nc()
    nc = _NC
    f = np.float32
    xp = np.asarray(x_prompt, f)[0]
    xs = np.asarray(x_sample, f)
    ident = np.eye(128, dtype=f)
    gT = np.stack([np.asarray(g, f)[0].reshape(8, 128).T for g in (g_ffn1, g_mix, g_ffn2)], axis=1)
    gqk = np.concatenate([np.asarray(g_q, f)[0], np.asarray(g_k, f)[0]])[None, :]
    badaT = np.asarray(b_ada, f)[0].reshape(72, 128).T
    esel = np.zeros((17, 2, 128), f)
    esel[0, 0, :] = 1.0
    for b in range(SB):
        esel[1 + b, 1, b * 8:(b + 1) * 8] = 1.0

    def pairrow(a):
        return np.ascontiguousarray(np.asarray(a, f).reshape(16, 2, 64).transpose(1, 2, 0).reshape(128, 16))
    ar = pairrow(ssm_a_re[0])
    ai = pairrow(ssm_a_im[0])
    ldt = pairrow(np.repeat(np.asarray(ssm_log_dt, f)[0][:, None], 64, axis=1))

    def bexp(b):
        b = np.asarray(b, f).reshape(16, 2, 64, 16)
        o = np.zeros((2, 64, 16, 2, 16), f)
        for g2 in range(2):
            o[g2, :, :, g2, :] = b[:, g2].transpose(1, 0, 2)
        return np.ascontiguousarray(o.reshape(128, 16, 32))

    def cexp(c):
        c = np.asarray(c, f).reshape(16, 2, 16, 64)
        o = np.zeros((2, 64, 16, 128), f)
        for q in range(16):
            for g2 in range(2):
                c0 = 32 * (q % 4) + 16 * g2
                o[g2, :, q, c0:c0 + 16] = c[q, g2].T
        return np.ascontiguousarray(o.reshape(128, 16, 128))
    bre, bim = bexp(ssm_b_re[0]), bexp(ssm_b_im[0])
    cre, cim = cexp(ssm_c_re[0]), cexp(ssm_c_im[0])
    dTt = np.ascontiguousarray(np.asarray(ssm_d, f)[0].reshape(4, 128).T)

    def h0lay(s, i):
        s = np.asarray(s, f)[0, i * SB:(i + 1) * SB].reshape(SB, 16, 2, 64)
        return np.ascontiguousarray(s.transpose(2, 3, 1, 0).reshape(128, 16, SB))

    def mult(dist):
        dist = np.asarray(dist)
        m = ((dist >= 0) & (dist <= 128)).astype(f)
        m += ((dist >= 0) & (dist <= 512) & (dist % 4 == 0)).astype(f)
        m += ((dist >= 0) & (dist <= 2048) & (dist % 16 == 0)).astype(f)
        return m
    kk = np.arange(128)[:, None, None]
    dd = np.arange(17)[None, :, None]
    qq_ = np.arange(128)[None, None, :]
    maskp = mult(128 * dd + qq_ - kk).astype(f)
    masks = np.zeros((128, 17, SB, 8), f)
    ii = np.arange(8)[None, :]
    for t in range(16):
        masks[:, t, :, :] = mult(WBUF + ii - (128 * t + np.arange(128)[:, None]))[:, None, :]
    for b_ in range(SB):
        for i2 in range(8):
            for i1_ in range(i2 + 1):
                masks[b_ * 8 + i1_, 16, b_, i2] = mult(i2 - i1_)
    in_maps = []
    for i in range(NCORES):
        nreal = (i + 1) * TOK_CORE
        x_seq = np.zeros((NBLK * BT, D), f)
        x_seq[NBLK * BT - nreal:] = xp[:nreal]
        blkvalid = np.zeros((128, NBLK), f)
        blkvalid[:, NBLK - nreal // BT:] = 1.0
        c17 = np.concatenate([np.asarray(c_prompt, f), np.asarray(c_sample, f)[i * SB:(i + 1) * SB]], 0)
        cT = c17.T.reshape(8, 128, 17).transpose(1, 0, 2)
        pos = np.concatenate([(i - 1) * TOK_CORE + np.arange(2 * TOK_CORE), np.tile(PAST + np.arange(8), SB)])
        cs, sn = _rope_tables(pos)
        rope = np.stack([cs.reshape(NSP, 128, 8), sn.reshape(NSP, 128, 8)], axis=2).transpose(1, 0, 2, 3)
        in_maps.append({
            "x_seq": x_seq, "x_smp": np.ascontiguousarray(xs[i * SB:(i + 1) * SB].reshape(128, D)),
            "blkvalid": blkvalid, "cT": np.ascontiguousarray(cT),
            "bada_row": np.asarray(b_ada, f), "badaT": np.ascontiguousarray(badaT),
            "gT": np.ascontiguousarray(gT), "ident": ident, "esel": esel,
            "w_ada": np.asarray(w_ada, f)[0], "w1_gate": np.asarray(w1_gate, f)[0],
            "w1_up": np.asarray(w1_up, f)[0], "w1_down": np.asarray(w1_down, f)[0],
            "w_in": np.asarray(w_in, f)[0],
            "cache_k": np.asarray(cache_k_win, f)[0, i * SB:(i + 1) * SB].reshape(SB, WBUF, 512),
            "cache_v": np.asarray(cache_v_win, f)[0, i * SB:(i + 1) * SB].reshape(SB, WBUF, 512),
            "ssm_ar": ar, "ssm_ai": ai, "ssm_ldt": ldt, "ssm_bre": bre, "ssm_bim": bim,
            "ssm_dT": dTt, "ssm_cre": cre, "ssm_cim": cim, "maskp": maskp, "masks": masks,
            "halo_valid": np.full((128, 1), 1.0 if i > 0 else 0.0, f), "gqk": np.ascontiguousarray(gqk),
            "rope": np.ascontiguousarray(rope), "w_glu": np.asarray(w_glu, f)[0], "w_out": np.asarray(w_out, f)[0],
            "w2_gate": np.asarray(w2_gate, f)[0], "w2_up": np.asarray(w2_up, f)[0], "w2_down": np.asarray(w2_down, f)[0],
            "h0re": h0lay(state_ssm_re, i), "h0im": h0lay(state_ssm_im, i),
        })
    res = run_bass_kernel_spmd(nc, in_maps, core_ids=list(range(NCORES)))
    R = res.results
    y_p = np.concatenate([R[i]["o_yp"] for i in range(NCORES)], 0).reshape(1, SEQ, D)
    y_s = np.concatenate([R[i]["o_ys"] for i in range(NCORES)], 0).reshape(128, 8, D)
    kp = R[7]["o_kp"].reshape(1, 1, WBUF, NH, HD)
    vp = R[7]["o_vp"].reshape(1, 1, WBUF, NH, HD)
    hp = R[7]["o_hp"]
    hp = hp.reshape(2, 64, 2, 16).transpose(2, 3, 0, 1).reshape(2, 32, 64)
    hpr, hpi = hp[0][None, None], hp[1][None, None]
    ks = np.concatenate([R[i]["o_ks"] for i in range(NCORES)], 0).reshape(1, 128, WBUF, NH, HD)
    vs = np.concatenate([R[i]["o_vs"] for i in range(NCORES)], 0).reshape(1, 128, WBUF, NH, HD)
    hs = np.stack([R[i]["o_hs"] for i in range(NCORES)], 0)
    hs = hs.reshape(NCORES, 2, 64, 2, 16, SB).transpose(3, 0, 5, 4, 1, 2).reshape(2, 128, 32, 64)
    return (y_p, y_s, kp, vp, np.ascontiguousarray(hpr), np.ascontiguousarray(hpi), ks, vs,
            np.ascontiguousarray(hs[0][None]), np.ascontiguousarray(hs[1][None]))
```
